# Optimizing a Trainium2 kernel written in Bass

```python
import jax, jax.numpy as jnp
from jax import lax
import numpy as np

D_MODEL = 2048
BATCH = 1
SEQ = 16384
DEPTH = 2

GRID_W = 64
CTX_LEN = 256
N_MIXERS = 2
N_SUB = 3
POOL_WINDOWS = (2, 4, 8, 16)
N_POOL_GROUPS = len(POOL_WINDOWS)
POOL_G = D_MODEL // N_POOL_GROUPS
HGRN_EXPAND = 128
HGRN_HEADS = D_MODEL // HGRN_EXPAND
HGRN_K = HGRN_EXPAND
HGRN_V = D_MODEL // HGRN_HEADS
CHUNK = 64
D_FF = 5632
N_POOL_LAYERS = (DEPTH + 1) // 2
N_HGRN_LAYERS = DEPTH // 2
ALPHA = (2 * DEPTH) ** 0.25
BETA = (8 * DEPTH) ** -0.25
LN_EPS = 1e-5
RMS_EPS = 1e-6

kernel_name = "hybrid_pool_hgrn2_macaron_deepnorm_prefix"


def layer_norm(x, g, b):
    xf = x.astype(jnp.float32)
    mu = jnp.mean(xf, -1, keepdims=True)
    var = jnp.mean(jnp.square(xf - mu), -1, keepdims=True)
    return ((xf - mu) * lax.rsqrt(var + LN_EPS) * g + b).astype(x.dtype)


def mod_in(x, m, s):
    return x * (1 + m[..., s, 1, :][..., None, :]) + m[..., s, 0, :][..., None, :]


def residual(x, y, m, s, g, b):
    return layer_norm(ALPHA * x + m[..., s, 2, :][..., None, :] * y, g, b)


def swiglu(h, w_in, w_out):
    a, u = jnp.split(h @ w_in, 2, axis=-1)
    return (jax.nn.silu(a) * u) @ w_out


def window_mean(x, w, axis):
    n = x.shape[axis]
    cs = jnp.cumsum(x.astype(jnp.float32), axis=axis)
    pad = [(0, 0)] * x.ndim
    pad[axis] = (1, 0)
    cs = jnp.pad(cs, pad)
    idx = jnp.arange(n)
    lo = jnp.clip(idx - w // 2, 0, n)
    hi = jnp.clip(idx - w // 2 + w, 0, n)
    s = jnp.take(cs, hi, axis=axis) - jnp.take(cs, lo, axis=axis)
    shape = [1] * x.ndim
    shape[axis] = n
    cnt = (hi - lo).astype(jnp.float32).reshape(shape)
    return (s / cnt).astype(x.dtype)


def pool_mixer(h, w, scale, grid):
    b, n, d = h.shape
    if grid:
        rows = n // GRID_W
        t = h.reshape(b, rows, GRID_W, d)
    else:
        t = h
    outs = []
    for gi, win in enumerate(POOL_WINDOWS):
        xg = t[..., gi * POOL_G:(gi + 1) * POOL_G]
        if grid:
            m = window_mean(window_mean(xg, win, 1), win, 2)
        else:
            m = window_mean(xg, win, 1)
        outs.append(m - xg)
    p = jnp.stack(outs, axis=-2).reshape(b, n, N_POOL_GROUPS, POOL_G)
    y = jnp.einsum('bngc,gcd->bngd', p, w).reshape(b, n, d)
    return y * scale


def _split_heads(a):
    b, n, _ = a.shape
    return a.reshape(b, n, HGRN_HEADS, -1).transpose(0, 2, 1, 3)


def _flip(a):
    return jnp.flip(a, axis=2)


def _hgrn_gates(a, lb):
    a = _split_heads(a).astype(jnp.float32)
    lb = lb.reshape(HGRN_HEADS, 1, HGRN_K)
    log_f = jnp.logaddexp(jnp.log(lb), jnp.log1p(-lb) + jax.nn.log_sigmoid(a))
    k = (1.0 - lb) * jax.nn.sigmoid(-a)
    return k, log_f


def _hgrn_project(h, w_in, lb_f, lb_b):
    q, i, af, ab, og = jnp.split(h @ w_in, 5, axis=-1)
    q = _split_heads(jax.nn.silu(q)) * (HGRN_K ** -0.5)
    v = _split_heads(i)
    kf, gf = _hgrn_gates(af, lb_f)
    kb, gb = _hgrn_gates(ab, lb_b)
    return q, v, kf, gf, kb, gb, og


def gla_chunk_scan(q, k, v, g, s0):
    b, nh, t, _ = q.shape
    nc = t // CHUNK

    def to_chunks(a):
        return a.reshape(b, nh, nc, CHUNK, a.shape[-1]).transpose(2, 0, 1, 3, 4)

    mask = jnp.tril(jnp.ones((CHUNK, CHUNK), bool))[:, :, None]

    def step(S, inp):
        qc, kc, vc, gc = inp
        G = jnp.cumsum(gc, axis=-2)
        rel = jnp.where(mask, G[:, :, :, None, :] - G[:, :, None, :, :], -jnp.inf)
        A = jnp.einsum('bhtk,bhsk,bhtsk->bhts', qc, kc, jnp.exp(rel))
        o = jnp.einsum('bhts,bhsv->bhtv', A, vc) + jnp.einsum('bhtk,bhkv->bhtv', qc * jnp.exp(G), S)
        G_last = G[:, :, -1:, :]
        S = jnp.exp(G_last[:, :, 0, :])[..., None] * S + jnp.einsum('bhsk,bhsv->bhkv', kc * jnp.exp(G_last - G), vc)
        return S, o

    S, o = lax.scan(step, s0, (to_chunks(q), to_chunks(k), to_chunks(v), to_chunks(g)))
    o = o.transpose(1, 2, 0, 3, 4).reshape(b, nh, t, -1)
    return o, S


def gla_final_state(k, v, g):
    G = jnp.cumsum(g, axis=-2)
    return jnp.einsum('bhsk,bhsv->bhkv', k * jnp.exp(G[:, :, -1:, :] - G), v)


def _hgrn_readout(o, og, norm_g, w_out):
    b, _, n, _ = o.shape
    o = o * lax.rsqrt(jnp.mean(jnp.square(o), -1, keepdims=True) + RMS_EPS) * norm_g
    o = o.transpose(0, 2, 1, 3).reshape(b, n, D_MODEL)
    return (o * jax.nn.silu(og.astype(jnp.float32))).astype(og.dtype) @ w_out


def hgrn_mixer(h, hc, w_in, lb_f, lb_b, norm_g, w_out, ctx_out):
    b = h.shape[0]
    zero = jnp.zeros((b, HGRN_HEADS, HGRN_K, HGRN_V), jnp.float32)
    if ctx_out:
        qc, vc, kfc, gfc, kbc, gbc, ogc = _hgrn_project(hc, w_in, lb_f, lb_b)
        oc_f, s_f = gla_chunk_scan(qc, kfc, vc, gfc, zero)
        oc_b, s_b = gla_chunk_scan(_flip(qc), _flip(kbc), _flip(vc), _flip(gbc), zero)
        yc = _hgrn_readout(oc_f + _flip(oc_b), ogc, norm_g, w_out)
    else:
        ic, afc, abc = jnp.split(hc @ w_in[:, D_MODEL:4 * D_MODEL], 3, axis=-1)
        vc = _split_heads(ic)
        kfc, gfc = _hgrn_gates(afc, lb_f)
        kbc, gbc = _hgrn_gates(abc, lb_b)
        s_f = gla_final_state(kfc, vc, gfc)
        s_b = gla_final_state(_flip(kbc), _flip(vc), _flip(gbc))
        yc = None
    q, v, kf, gf, kb, gb, og = _hgrn_project(h, w_in, lb_f, lb_b)
    o_f, _ = gla_chunk_scan(q, kf, v, gf, s_f)
    o_b, _ = gla_chunk_scan(_flip(q), _flip(kb), _flip(v), _flip(gb), s_b)
    y = _hgrn_readout(o_f + _flip(o_b), og, norm_g, w_out)
    return y, yc


def setup_inputs(seed: int = 0) -> dict:
    key = jax.random.key(seed)
    ks = jax.random.split(key, 16)
    D = D_MODEL

    def nrm(k, shape, s):
        return jax.random.normal(k, shape, jnp.float32) * s

    col_scale = jnp.concatenate([jnp.ones((D,)), BETA * jnp.ones((D,)), jnp.ones((3 * D,))]).astype(jnp.float32)
    return {
        "x": nrm(ks[0], (BATCH, SEQ, D), 1.0),
        "c": nrm(ks[1], (BATCH, D), 1.0),
        "ctx": nrm(ks[2], (BATCH, CTX_LEN, D), 1.0),
        "c_ctx": nrm(ks[3], (D,), 1.0),
        "mod_w": nrm(ks[4], (DEPTH, D, N_SUB * 3 * D), 0.5 * D ** -0.5),
        "mod_b": nrm(ks[5], (DEPTH, N_SUB * 3 * D), 0.02),
        "ln_g": 1.0 + nrm(ks[6], (DEPTH, N_SUB, D), 0.02),
        "ln_b": nrm(ks[7], (DEPTH, N_SUB, D), 0.02),
        "ffn_w_in": nrm(ks[8], (DEPTH, 2, D, 2 * D_FF), BETA * D ** -0.5),
        "ffn_w_out": nrm(ks[9], (DEPTH, 2, D_FF, D), BETA * D_FF ** -0.5),
        "pool_w": nrm(ks[10], (N_POOL_LAYERS, N_POOL_GROUPS, POOL_G, POOL_G), BETA * POOL_G ** -0.5),
        "pool_scale": 1.0 + nrm(ks[11], (N_POOL_LAYERS, D), 0.02),
        "hgrn_w_in": nrm(ks[12], (N_HGRN_LAYERS, D, 5 * D), D ** -0.5) * col_scale,
        "hgrn_lb": 1.0 + nrm(ks[13], (2, DEPTH, D), 0.1),
        "hgrn_norm_g": 1.0 + nrm(ks[14], (N_HGRN_LAYERS, HGRN_V), 0.02),
        "hgrn_w_out": nrm(ks[15], (N_HGRN_LAYERS, D, D), BETA * D ** -0.5),
    }


def reference(x, c, ctx, c_ctx, mod_w, mod_b, ln_g, ln_b, ffn_w_in, ffn_w_out, pool_w, pool_scale,
              hgrn_w_in, hgrn_lb, hgrn_norm_g, hgrn_w_out):
    b = x.shape[0]
    p = jax.nn.softmax(hgrn_lb.astype(jnp.float32), axis=1)
    lower_bounds = jnp.cumsum(p, axis=1) - p[:, :1]
    sc = jax.nn.silu(c)
    sctx = jax.nn.silu(c_ctx)
    for i in range(DEPTH):
        last = i == DEPTH - 1
        kind = i % N_MIXERS
        j = i // N_MIXERS
        ctx_used = (not last) or kind == 1
        mx = (sc @ mod_w[i] + mod_b[i]).reshape(b, N_SUB, 3, D_MODEL)
        mc = (sctx @ mod_w[i] + mod_b[i]).reshape(N_SUB, 3, D_MODEL)
        x = residual(x, 0.5 * swiglu(mod_in(x, mx, 0), ffn_w_in[i, 0], ffn_w_out[i, 0]), mx, 0, ln_g[i, 0], ln_b[i, 0])
        if ctx_used:
            ctx = residual(ctx, 0.5 * swiglu(mod_in(ctx, mc, 0), ffn_w_in[i, 0], ffn_w_out[i, 0]), mc, 0, ln_g[i, 0], ln_b[i, 0])
        if kind == 0:
            y = pool_mixer(mod_in(x, mx, 1), pool_w[j], pool_scale[j], True)
            yc = None if last else pool_mixer(mod_in(ctx, mc, 1), pool_w[j], pool_scale[j], False)
        else:
            y, yc = hgrn_mixer(mod_in(x, mx, 1), mod_in(ctx, mc, 1), hgrn_w_in[j], lower_bounds[0, i],
                               lower_bounds[1, i], hgrn_norm_g[j], hgrn_w_out[j], not last)
        x = residual(x, y, mx, 1, ln_g[i, 1], ln_b[i, 1])
        if not last:
            ctx = residual(ctx, yc, mc, 1, ln_g[i, 1], ln_b[i, 1])
        x = residual(x, 0.5 * swiglu(mod_in(x, mx, 2), ffn_w_in[i, 1], ffn_w_out[i, 1]), mx, 2, ln_g[i, 2], ln_b[i, 2])
        if not last:
            ctx = residual(ctx, 0.5 * swiglu(mod_in(ctx, mc, 2), ffn_w_in[i, 1], ffn_w_out[i, 1]), mc, 2, ln_g[i, 2], ln_b[i, 2])
    return x
```

```python
import numpy as np
import concourse.bass as bass
import concourse.mybir as mybir

F32 = mybir.dt.float32
BF16 = mybir.dt.bfloat16
AF = mybir.ActivationFunctionType
ALU = mybir.AluOpType
AX = mybir.AxisListType

ENGS = ("pe", "act", "dve", "pool", "sp")


class Res:
    __slots__ = ("name", "w", "r")

    def __init__(self, name):
        self.name = name
        self.w = None
        self.r = {}


class Op:
    __slots__ = ("eng", "fn", "deps", "is_dma", "dsem", "dval", "sig", "idx", "inc")

    def __init__(self, eng, fn, is_dma=False):
        self.eng = eng
        self.fn = fn
        self.deps = []
        self.is_dma = is_dma
        self.dsem = None
        self.dval = 0
        self.sig = 0
        self.idx = 0


class DmaSem:
    def __init__(self, handle):
        self.h = handle
        self.count = 0


class Prog:
    def __init__(self, nc, stack):
        self.nc = nc
        self.stack = stack
        self.streams = {e: [] for e in ENGS}
        self.nres = 0
        self.esem = {}
        for e in ENGS:
            self.esem[e] = stack.enter_context(nc.semaphore("es_" + e))
        self._dma_ops_since_barrier = []
        self._barrier_deps = None
        self._sem_free = []
        self._sem_phase = []
        self._nsem = 0

    def sbuf(self, name, shape, dt):
        return self.stack.enter_context(self.nc.sbuf_tensor(name, list(shape), dt))

    def psum(self, name, shape, dt=F32):
        return self.stack.enter_context(self.nc.psum_tensor(name, list(shape), dt))

    def dsem(self, name):
        if self._sem_free:
            d = self._sem_free.pop()
        else:
            self._nsem += 1
            d = DmaSem(self.stack.enter_context(self.nc.semaphore("ds%d" % self._nsem)))
        self._sem_phase.append(d)
        return d

    def csem(self, name):
        return DmaSem(self.stack.enter_context(self.nc.semaphore(name)))

    def end_phase(self):
        self.barrier()
        self._sem_free.extend(self._sem_phase)
        self._sem_phase = []

    def res(self, name="r"):
        self.nres += 1
        return Res(name)

    def _track(self, o, reads, writes):
        deps = []
        for r in reads:
            if r.w is not None:
                deps.append(r.w)
        for w in writes:
            if w.w is not None:
                deps.append(w.w)
            deps.extend(w.r.values())
        o.deps = deps
        for r in reads:
            r.r[(id(o.dsem) if o.is_dma else o.eng)] = o
        for w in writes:
            w.w = o
            w.r = {}
        st = self.streams[o.eng]
        o.idx = len(st)
        st.append(o)
        return o

    def op(self, eng, fn, reads=(), writes=()):
        return self._track(Op(eng, fn), reads, writes)

    def dma(self, eng, fn, sem, reads=(), writes=(), inc=16):
        o = Op(eng, fn, is_dma=True)
        o.inc = inc
        sem.count += inc
        o.dsem = sem
        o.dval = sem.count
        return self._track(o, reads, writes)

    def emit(self, final_waits=()):
        nc = self.nc
        for e in ENGS:
            for o in self.streams[e]:
                for d in o.deps:
                    if not d.is_dma:
                        if d.eng == e and e in ("pe", "sp"):
                            continue
                        d.sig = -1
        self.nsig = {}
        for e in ENGS:
            n = 0
            for o in self.streams[e]:
                if o.sig == -1:
                    n += 1
                    o.sig = n
            self.nsig[e] = (n, len(self.streams[e]))
        streams = self.streams
        esem = self.esem

        def run(engname, engobj):
            known = {}
            for o in streams[engname]:
                waits = {}
                for d in o.deps:
                    if d.is_dma:
                        key = id(d.dsem)
                        h, v = d.dsem.h, d.dval
                    else:
                        if d.eng == engname and engname in ("pe", "sp"):
                            continue
                        key = d.eng
                        h, v = esem[d.eng], d.sig
                    if known.get(key, 0) >= v:
                        continue
                    if key not in waits or waits[key][1] < v:
                        waits[key] = (h, v)
                for key, (h, v) in waits.items():
                    engobj.wait_ge(h, v)
                    known[key] = v
                ins = o.fn(engobj)
                if o.is_dma:
                    ins.then_inc(o.dsem.h, o.inc)
                elif o.sig:
                    ins.then_inc(esem[engname], 1)
            for (h, v) in final_waits.get(engname, []) if isinstance(final_waits, dict) else []:
                engobj.wait_ge(h, v)

        with nc.Block() as block:
            @block.tensor
            def _(t):
                run("pe", t)

            @block.scalar
            def _(s):
                run("act", s)

            @block.vector
            def _(v):
                run("dve", v)

            @block.gpsimd
            def _(g):
                run("pool", g)

            @block.sync
            def _(s):
                run("sp", s)


class Arena:
    def __init__(self, P, nbytes):
        self.words = nbytes // 4
        self.t = P.sbuf("arena", [128, self.words], F32)
        self.off = 0

    def reset(self, to=0):
        self.off = to

    def mark(self):
        return self.off

    def alloc(self, shape, dt):
        n = 1
        for s in shape:
            n *= s
        if dt == BF16:
            w = (n + 1) // 2
        else:
            w = n
        w = (w + 7) // 8 * 8
        assert self.off + w <= self.words, ("arena overflow", self.off, w, self.words)
        v = self.t[:, self.off:self.off + w]
        self.off += w
        if dt == BF16:
            v = v.bitcast(BF16)[:, 0:n]
        else:
            v = v[:, 0:n]
        if len(shape) == 2:
            return v.rearrange("p (a b) -> p a b", b=shape[1])
        if len(shape) == 3:
            return v.rearrange("p (a b c) -> p a b c", b=shape[1], c=shape[2])
        return v


def _barrier(self):
    last = []
    for e in ENGS:
        if self.streams[e]:
            last.append(self.streams[e][-1])
    last.extend(self._dma_ops_since_barrier)
    self._dma_ops_since_barrier = []
    self._barrier_deps = {e: list(last) for e in ENGS}


def _track2(self, o, reads, writes):
    r = Prog._track_orig(self, o, reads, writes)
    bd = getattr(self, "_barrier_deps", None)
    if bd and bd.get(o.eng):
        o.deps = list(o.deps) + [d for d in bd[o.eng] if d is not o]
        bd[o.eng] = None
    if o.is_dma:
        self._dma_ops_since_barrier.append(o)
    return r


Prog._track_orig = Prog._track
Prog._track = _track2
Prog.barrier = _barrier


import numpy as np
from contextlib import ExitStack

D = 2048
NC16 = 16
DFF = 5632
NFC = 44
ALPHA = 4 ** 0.25
LN_EPS = 1e-5
RMS_EPS = 1e-6
EPS_P = LN_EPS / (ALPHA * ALPHA)


class Ctx:
    pass


def setup_common(P, TMAX=512):
    C = Ctx()
    C.P = P
    C.ones = P.sbuf("ones", [128, 128], F32); C.r_ones = P.res()
    C.ident = P.sbuf("ident", [128, 128], F32); C.r_ident = P.res()
    C.identb = P.sbuf("identb", [128, 128], BF16); C.r_identb = P.res()
    P.op("dve", lambda e: e.memset(C.ones[:], 1.0), writes=[C.r_ones])
    P.op("dve", lambda e: e.memset(C.ident[:], 0.0), writes=[C.r_ident])
    P.op("pool", lambda e: e.affine_select(out=C.ident[:], in_=C.ident[:], compare_op=ALU.not_equal, fill=1.0,
                                           base=0, pattern=[[-1, 128]], channel_multiplier=1),
         reads=[C.r_ident], writes=[C.r_ident])
    P.op("dve", lambda e: e.tensor_copy(out=C.identb[:], in_=C.ident[:]), reads=[C.r_ident], writes=[C.r_identb])
    C.pb = [P.psum("pb%d" % i, [128, 512]) for i in range(8)]
    C.r_pb = [P.res("pb%d" % i) for i in range(8)]
    C.sq = [P.sbuf("sq%d" % i, [128, TMAX], F32) for i in range(2)]
    C.r_sq = [P.res() for i in range(2)]
    C.st = [P.sbuf("st%d" % i, [128, TMAX], F32) for i in range(4)]
    C.r_st = [P.res() for i in range(4)]
    return C


def emit_ln(C, xs, r_xs, T, gcol, bcol, r_vec, eps=EPS_P, bank0=0):
    P = C.P
    pm, pq = C.pb[bank0], C.pb[bank0 + 1]
    r_pm, r_pq = C.r_pb[bank0], C.r_pb[bank0 + 1]
    for c in range(16):
        sq, r_sq = C.sq[c % 2], C.r_sq[c % 2]
        P.op("act", lambda e, c=c, sq=sq: e.activation(out=sq[:, :T], in_=xs[:, c, :T], func=AF.Square),
             reads=[r_xs[c]], writes=[r_sq])
        P.op("pe", lambda e, c=c: e.matmul(pm[:, :T], lhsT=C.ones[:], rhs=xs[:, c, :T], start=(c == 0), stop=(c == 15)),
             reads=[C.r_ones, r_xs[c]], writes=[r_pm])
        P.op("pe", lambda e, c=c, sq=sq: e.matmul(pq[:, :T], lhsT=C.ones[:], rhs=sq[:, :T], start=(c == 0), stop=(c == 15)),
             reads=[C.r_ones, r_sq], writes=[r_pq])
    m, msq, var, rstd = C.st
    r_m, r_msq, r_var, r_rstd = C.r_st
    P.op("act", lambda e: e.activation(out=m[:, :T], in_=pm[:, :T], func=AF.Copy, scale=1.0 / D), reads=[r_pm], writes=[r_m])
    P.op("dve", lambda e: e.tensor_tensor(out=msq[:, :T], in0=m[:, :T], in1=m[:, :T], op=ALU.mult), reads=[r_m], writes=[r_msq])
    P.op("dve", lambda e: e.scalar_tensor_tensor(out=var[:, :T], in0=pq[:, :T], scalar=1.0 / D, in1=msq[:, :T],
                                                 op0=ALU.mult, op1=ALU.subtract), reads=[r_pq, r_msq], writes=[r_var])
    P.op("dve", lambda e: e.tensor_scalar(out=var[:, :T], in0=var[:, :T], scalar1=eps, scalar2=None,
                                          op0=ALU.add), reads=[r_var], writes=[r_var])
    P.op("act", lambda e: e.activation(out=var[:, :T], in_=var[:, :T], func=AF.Sqrt), reads=[r_var], writes=[r_var])
    P.op("dve", lambda e: e.reciprocal(out=rstd[:, :T], in_=var[:, :T]), reads=[r_var], writes=[r_rstd])
    for c in range(16):
        P.op("dve", lambda e, c=c: e.tensor_tensor(out=xs[:, c, :T], in0=xs[:, c, :T], in1=m[:, :T], op=ALU.subtract),
             reads=[r_xs[c], r_m], writes=[r_xs[c]])
        P.op("dve", lambda e, c=c: e.tensor_tensor(out=xs[:, c, :T], in0=xs[:, c, :T], in1=rstd[:, :T], op=ALU.mult),
             reads=[r_xs[c], r_rstd], writes=[r_xs[c]])
        P.op("act", lambda e, c=c: e.activation(out=xs[:, c, :T], in_=xs[:, c, :T], func=AF.Identity,
                                                bias=bcol(c), scale=gcol(c)),
             reads=[r_xs[c], r_vec], writes=[r_xs[c]])


def setup_ffn(C, A=None):
    P = C.P
    if A is None:
        alloc = lambda name, shape, dt: P.sbuf(name, [128] + list(shape), dt)
    else:
        alloc = lambda name, shape, dt: A.alloc(list(shape), dt)
    C.xm = alloc("xm", [16, 512], BF16); C.r_xm = [P.res() for _ in range(16)]
    C.g = alloc("g", [NFC, 512], BF16); C.r_g = [P.res() for _ in range(NFC)]
    C.NWIN = 3
    C.FCB = 2
    C.win = [alloc("win%d" % i, [16, 2, C.FCB * 128], BF16) for i in range(C.NWIN)]
    C.r_win = [P.res() for _ in range(C.NWIN)]
    C.s_win = [P.dsem("s_win%d" % i) for i in range(C.NWIN)]
    C.NWOUT = 2
    C.WOB = 11
    C.wout = [alloc("wout%d" % i, [C.WOB, 512], BF16) for i in range(C.NWOUT)]
    C.r_wout = [P.res() for _ in range(C.NWOUT)]
    C.s_wout = [P.dsem("s_wout%d" % i) for i in range(C.NWOUT)]
    C.silu = [alloc("silu%d" % i, [512], F32) for i in range(2)]
    C.r_silu = [P.res() for _ in range(2)]
    C.win_i = 0
    C.wout_i = 0
    C.silu_i = 0


def emit_ffn(C, xs, r_xs, T, w_in, w_out, opscol, shcol, gpcol, r_vec):
    P = C.P
    xm, g = C.xm, C.g
    for c in range(16):
        P.op("act", lambda e, c=c: e.activation(out=xm[:, c, :T], in_=xs[:, c, :T], func=AF.Identity,
                                                bias=shcol(c), scale=opscol(c)),
             reads=[r_xs[c], r_vec], writes=[C.r_xm[c]])
    w_in_v = w_in.rearrange("(c p) f -> p c f", p=128)
    FCB = C.FCB
    for fb in range(NFC // FCB):
        si = C.win_i % C.NWIN; C.win_i += 1
        ws, r_ws, s_ws = C.win[si], C.r_win[si], C.s_win[si]
        for au in range(2):
            c0 = au * DFF + fb * FCB * 128
            P.dma("pool", lambda e, ws=ws, au=au, c0=c0: e.dma_start(out=ws[:, :, au, :], in_=w_in_v[:, :, c0:c0 + FCB * 128]),
                  s_ws, writes=[r_ws])
        for j in range(FCB):
            fc = fb * FCB + j
            pa, r_pa = C.pb[(fc % 2) * 2], C.r_pb[(fc % 2) * 2]
            pu, r_pu = C.pb[(fc % 2) * 2 + 1], C.r_pb[(fc % 2) * 2 + 1]
            for c in range(16):
                P.op("pe", lambda e, c=c, ws=ws, j=j, pa=pa: e.matmul(pa[:, :T], lhsT=ws[:, c, 0, j * 128:(j + 1) * 128],
                                                                       rhs=xm[:, c, :T], start=(c == 0), stop=(c == 15)),
                     reads=[r_ws, C.r_xm[c]], writes=[r_pa])
            for c in range(16):
                P.op("pe", lambda e, c=c, ws=ws, j=j, pu=pu: e.matmul(pu[:, :T], lhsT=ws[:, c, 1, j * 128:(j + 1) * 128],
                                                                       rhs=xm[:, c, :T], start=(c == 0), stop=(c == 15)),
                     reads=[r_ws, C.r_xm[c]], writes=[r_pu])
            sl = C.silu_i % 2; C.silu_i += 1
            sb, r_sb = C.silu[sl], C.r_silu[sl]
            P.op("act", lambda e, sb=sb, pa=pa: e.activation(out=sb[:, :T], in_=pa[:, :T], func=AF.Silu),
                 reads=[r_pa], writes=[r_sb])
            P.op("dve", lambda e, sb=sb, pu=pu, fc=fc: e.tensor_tensor(out=g[:, fc, :T], in0=sb[:, :T], in1=pu[:, :T], op=ALU.mult),
                 reads=[r_sb, r_pu], writes=[C.r_g[fc]])
    w_out_v = w_out.rearrange("(f p) d -> p f d", p=128)
    WOB = C.WOB
    for db in range(4):
        for fb in range(NFC // WOB):
            si = C.wout_i % C.NWOUT; C.wout_i += 1
            ws, r_ws, s_ws = C.wout[si], C.r_wout[si], C.s_wout[si]
            P.dma("pool", lambda e, ws=ws, fb=fb, db=db: e.dma_start(out=ws[:], in_=w_out_v[:, fb * WOB:(fb + 1) * WOB, db * 512:(db + 1) * 512]),
                  s_ws, writes=[r_ws])
            for j in range(WOB):
                fc = fb * WOB + j
                for dci in range(4):
                    py, r_py = C.pb[4 + dci], C.r_pb[4 + dci]
                    P.op("pe", lambda e, ws=ws, j=j, dci=dci, fc=fc, py=py: e.matmul(py[:, :T], lhsT=ws[:, j, dci * 128:(dci + 1) * 128],
                                                                                     rhs=g[:, fc, :T], start=(fc == 0), stop=(fc == NFC - 1)),
                         reads=[r_ws, C.r_g[fc]], writes=[r_py])
        for dci in range(4):
            c = db * 4 + dci
            py, r_py = C.pb[4 + dci], C.r_pb[4 + dci]
            P.op("dve", lambda e, c=c, py=py: e.scalar_tensor_tensor(out=xs[:, c, :T], in0=py[:, :T], scalar=gpcol(c), in1=xs[:, c, :T],
                                                                     op0=ALU.mult, op1=ALU.add),
                 reads=[r_py, r_xs[c], r_vec], writes=[r_xs[c]])


V_SC, V_SH, V_GT = 0, 1, 2
V_LNG, V_LNB, V_PLG, V_PLB = 6, 7, 8, 9
V_OPS, V_GP = 10, 12
NVROW = 14


def build_ffn_program(tiles, NT, pre_ln=False, gate_mul=0.5 / ALPHA):
    nc = bass.Bass("TRN2", target_bir_lowering=False)
    xin = nc.dram_tensor("xin", [128, 16, NT], F32, kind="ExternalInput").ap()
    vec = nc.dram_tensor("vec", [128, NVROW, 16], F32, kind="ExternalInput").ap()
    w_in = nc.dram_tensor("w_in", [D, 2 * DFF], F32, kind="ExternalInput").ap()
    w_out = nc.dram_tensor("w_out", [DFF, D], F32, kind="ExternalInput").ap()
    xout = nc.dram_tensor("xout", [128, 16, NT], F32, kind="ExternalOutput").ap()
    with ExitStack() as st:
        P = Prog(nc, st)
        C = setup_common(P)
        setup_ffn(C)
        V = P.sbuf("V", [128, NVROW, 16], F32); r_V = P.res()
        s_v = P.dsem("s_v"); s_x = P.dsem("s_x"); s_o = P.dsem("s_o")
        P.dma("sp", lambda e: e.dma_start(out=V[:], in_=vec), s_v, writes=[r_V])
        for ms in range(2):
            P.op("dve", lambda e, ms=ms: e.tensor_scalar(out=V[:, V_OPS + ms, :], in0=V[:, V_SC + 3 * ms, :], scalar1=1.0, scalar2=None,
                                                         op0=ALU.add), reads=[r_V], writes=[r_V])
            P.op("dve", lambda e, ms=ms: e.tensor_scalar(out=V[:, V_GP + ms, :], in0=V[:, V_GT + 3 * ms, :], scalar1=gate_mul, scalar2=None,
                                                         op0=ALU.mult), reads=[r_V], writes=[r_V])
        xs = P.sbuf("xs", [128, 16, 512], F32); r_xs = [P.res() for _ in range(16)]
        col = lambda k: (lambda c: V[:, k, c:c + 1])
        for (t0, T, ms) in tiles:
            P.dma("sp", lambda e, t0=t0, T=T: e.dma_start(out=xs[:, :, :T], in_=xin[:, :, t0:t0 + T]), s_x, writes=r_xs)
            if pre_ln:
                emit_ln(C, xs, r_xs, T, col(V_PLG), col(V_PLB), r_V)
            emit_ffn(C, xs, r_xs, T, w_in, w_out, col(V_OPS + ms), col(V_SH + 3 * ms), col(V_GP + ms), r_V)
            emit_ln(C, xs, r_xs, T, col(V_LNG), col(V_LNB), r_V)
            P.dma("sp", lambda e, t0=t0, T=T: e.dma_start(out=xout[:, :, t0:t0 + T], in_=xs[:, :, :T]), s_o, reads=r_xs)
        P.emit(final_waits={"sp": [(s_o.h, s_o.count)]})
    return nc


GRID_W = 64
POOL_WINDOWS = (2, 4, 8, 16)


def emit_pool_slab(C, A, xin, zout, R, W, r0, nown, vertical, valid_d, inv_d, pool_w, opscol, shcol, combcol, r_vec, tag):
    P = C.P
    PW = W + 16
    N = R * PW
    NO = nown * W
    mark = A.mark()
    xg = A.alloc([4, R * W], F32); r_xg = [P.res() for _ in range(4)]
    hpad = A.alloc([R, PW], F32); r_hpad = P.res()
    bufs = [A.alloc([R, PW], F32) for _ in range(2)]; r_bufs = [P.res() for _ in range(2)]
    inv = A.alloc([NO], F32); r_inv = P.res()
    tmp = A.alloc([NO], F32); r_tmp = P.res()
    pbuf = A.alloc([4, NO], BF16); r_pbuf = [P.res() for _ in range(4)]
    pw = A.alloc([4, 512], BF16); r_pw = P.res()
    if valid_d is not None:
        valid = A.alloc([R * W], F32); r_valid = P.res()
    s_in = P.dsem("s_pin" + tag); s_out = P.dsem("s_pout" + tag); s_w = P.dsem("s_pw" + tag)
    s_inv = P.dsem("s_pinv" + tag)
    if valid_d is not None:
        s_val = P.dsem("s_pval" + tag)
        P.dma("sp", lambda e: e.dma_start(out=valid[:], in_=valid_d), s_val, writes=[r_valid])
    P.op("dve", lambda e: e.memset(hpad[:], 0.0), writes=[r_hpad])
    for b in range(2):
        P.op("dve", lambda e, b=b: e.memset(bufs[b][:], 0.0), writes=[r_bufs[b]])
    flat = lambda t: t.rearrange("p r w -> p (r w)")
    TB = 512 if NO >= 512 else NO
    ntb = NO // TB
    for gi, win in enumerate(POOL_WINDOWS):
        hw = win // 2
        P.dma("sp", lambda e, gi=gi: e.dma_start(out=xg[:], in_=xin[:, gi * 4:(gi + 1) * 4, :]), s_in, writes=r_xg)
        P.dma("sp", lambda e, gi=gi: e.dma_start(out=inv[:], in_=inv_d[gi]), s_inv, writes=[r_inv])
        P.dma("pool", lambda e, gi=gi: e.dma_start(out=pw[:], in_=pool_w[gi].rearrange("(i p) o -> p i o", p=128)), s_w, writes=[r_pw])
        for ic in range(4):
            c = gi * 4 + ic
            hd = hpad[:, :, 8:8 + W]
            P.op("act", lambda e, ic=ic, c=c, hd=hd: e.activation(out=hd, in_=xg[:, ic, :].rearrange("p (r w) -> p r w", w=W), func=AF.Identity,
                                                                  bias=shcol(c), scale=opscol(c)),
                 reads=[r_xg[ic], r_vec], writes=[r_hpad])
            if valid_d is not None:
                P.op("dve", lambda e, hd=hd: e.tensor_tensor(out=hd, in0=hd, in1=valid[:].rearrange("p (r w) -> p r w", w=W), op=ALU.mult),
                     reads=[r_hpad, r_valid], writes=[r_hpad])
            src, r_src = hpad, r_hpad
            bi = 0
            shifts = []
            k = 1
            while k < win:
                shifts.append(k); k *= 2
            if vertical:
                k = 1
                while k < win:
                    shifts.append(k * PW); k *= 2
            for k in shifts:
                dst, r_dst = bufs[bi], r_bufs[bi]; bi ^= 1
                P.op("dve", lambda e, src=src, dst=dst, k=k: e.tensor_tensor(out=flat(dst)[:, k:N], in0=flat(src)[:, k:N], in1=flat(src)[:, 0:N - k], op=ALU.add),
                     reads=[r_src], writes=[r_dst])
                src, r_src = dst, r_dst
            ro = r0 + hw - 1 if vertical else r0
            vview = src[:, ro:ro + nown, 8 + hw - 1:8 + hw - 1 + W]
            P.op("dve", lambda e, vview=vview: e.tensor_tensor(out=tmp[:].rearrange("p (r w) -> p r w", w=W), in0=vview,
                                                               in1=inv[:].rearrange("p (r w) -> p r w", w=W), op=ALU.mult),
                 reads=[r_src, r_inv], writes=[r_tmp])
            P.op("dve", lambda e, ic=ic: e.tensor_tensor(out=pbuf[:, ic, :].rearrange("p (r w) -> p r w", w=W), in0=tmp[:].rearrange("p (r w) -> p r w", w=W),
                                                         in1=hpad[:, r0:r0 + nown, 8:8 + W], op=ALU.subtract),
                 reads=[r_tmp, r_hpad], writes=[r_pbuf[ic]])
        for oc in range(4):
            c = gi * 4 + oc
            for tb in range(ntb):
                bk = 4 + (oc * ntb + tb) % 4
                pk, r_pk = C.pb[bk], C.r_pb[bk]
                for ic in range(4):
                    P.op("pe", lambda e, ic=ic, oc=oc, tb=tb, pk=pk: e.matmul(pk[:, :TB], lhsT=pw[:, ic, oc * 128:(oc + 1) * 128],
                                                                              rhs=pbuf[:, ic, tb * TB:(tb + 1) * TB], start=(ic == 0), stop=(ic == 3)),
                         reads=[r_pw, r_pbuf[ic]], writes=[r_pk])
                xo = xg[:, oc, r0 * W + tb * TB: r0 * W + (tb + 1) * TB]
                P.op("dve", lambda e, xo=xo, pk=pk, c=c: e.scalar_tensor_tensor(out=xo, in0=pk[:, :TB], scalar=combcol(c), in1=xo,
                                                                                op0=ALU.mult, op1=ALU.add),
                     reads=[r_pk, r_xg[oc], r_vec], writes=[r_xg[oc]])
        P.dma("sp", lambda e, gi=gi: e.dma_start(out=zout[:, gi * 4:(gi + 1) * 4, :], in_=xg[:, :, r0 * W:(r0 + nown) * W]), s_out, reads=r_xg)
    A.reset(mark)
    return s_out


def pool_consts(core):
    R, W = 47, 64
    rows = 32 * core - 8 + np.arange(R)
    valid = ((rows >= 0) & (rows < 256)).astype(np.float32)
    valid = np.broadcast_to(valid[None, :, None], (128, R, W)).reshape(128, R * W).copy()
    inv = np.zeros((4, 128, 32 * W), np.float32)
    invc = np.zeros((4, 128, 256), np.float32)
    for gi, win in enumerate(POOL_WINDOWS):
        hw = win // 2
        r = 32 * core + np.arange(32)
        cr = np.minimum(r - hw + win, 256) - np.maximum(r - hw, 0)
        c = np.arange(W)
        cc = np.minimum(c - hw + win, W) - np.maximum(c - hw, 0)
        t = (1.0 / (cr[:, None] * cc[None, :]).astype(np.float64)).astype(np.float32).reshape(-1)
        inv[gi] = t[None, :]
        tt = np.arange(256)
        ct = np.minimum(tt - hw + win, 256) - np.maximum(tt - hw, 0)
        invc[gi] = (1.0 / ct.astype(np.float64)).astype(np.float32)[None, :]
    return valid, inv, invc


def build_pool_program():
    nc = bass.Bass("TRN2", target_bir_lowering=False)
    xin = nc.dram_tensor("xin", [128, 16, 3008], F32, kind="ExternalInput").ap()
    xctx = nc.dram_tensor("xctx", [128, 16, 256], F32, kind="ExternalInput").ap()
    vec = nc.dram_tensor("vec", [128, 7, 16], F32, kind="ExternalInput").ap()
    valid = nc.dram_tensor("valid", [128, 3008], F32, kind="ExternalInput").ap()
    inv = nc.dram_tensor("inv", [4, 128, 2048], F32, kind="ExternalInput").ap()
    invc = nc.dram_tensor("invc", [4, 128, 256], F32, kind="ExternalInput").ap()
    pool_w = nc.dram_tensor("pool_w", [4, 512, 512], F32, kind="ExternalInput").ap()
    z = nc.dram_tensor("z", [128, 16, 2048], F32, kind="ExternalOutput").ap()
    zc = nc.dram_tensor("zc", [128, 16, 256], F32, kind="ExternalOutput").ap()
    with ExitStack() as st:
        P = Prog(nc, st)
        C = setup_common(P)
        A = Arena(P, 150 * 1024)
        V = P.sbuf("V", [128, 11, 16], F32); r_V = P.res()
        s_v = P.dsem("s_v")
        P.dma("sp", lambda e: e.dma_start(out=V[:, 0:7, :], in_=vec), s_v, writes=[r_V])
        for ms in range(2):
            P.op("dve", lambda e, ms=ms: e.tensor_scalar(out=V[:, 7 + 2 * ms, :], in0=V[:, 3 * ms, :], scalar1=1.0, scalar2=None, op0=ALU.add), reads=[r_V], writes=[r_V])
            P.op("dve", lambda e, ms=ms: e.scalar_tensor_tensor(out=V[:, 8 + 2 * ms, :], in0=V[:, 2 + 3 * ms, :], scalar=1.0 / ALPHA, in1=V[:, 6, :], op0=ALU.mult, op1=ALU.mult),
                 reads=[r_V], writes=[r_V])
        col = lambda k: (lambda c: V[:, k, c:c + 1])
        s1 = emit_pool_slab(C, A, xin, z, 47, 64, 8, 32, True, valid, inv, pool_w, col(7), col(1), col(8), r_V, "a")
        P.barrier()
        s2 = emit_pool_slab(C, A, xctx, zc, 1, 256, 0, 1, False, None, invc, pool_w, col(9), col(4), col(10), r_V, "c")
        P.emit(final_waits={"sp": [(s1.h, s1.count), (s2.h, s2.count)]})
    return nc


HK = 128
NH = 16


def hgrn_masks():
    p = np.arange(128)
    half, sl = p // 64, p % 64
    j = np.arange(8)[None, :, None]
    t = np.arange(64)[None, None, :]
    ok = (j % 2 == half[:, None, None])
    mf = (ok & (sl[:, None, None] <= t)).astype(np.float32).reshape(128, 512)
    mb = (ok & (sl[:, None, None] >= t)).astype(np.float32).reshape(128, 512)
    return mf, mb


def setup_hgrn1(C, A):
    P = C.P
    H = Ctx()
    H.xm = A.alloc([16, 512], BF16); H.r_xm = [P.res() for _ in range(16)]
    H.vt = A.alloc([4, 2048], BF16); H.r_vt = [P.res() for _ in range(4)]
    H.NW = 6
    H.ws = [A.alloc([16, 256], BF16) for _ in range(H.NW)]; H.r_ws = [P.res() for _ in range(H.NW)]
    H.s_ws = [P.dsem("s_hw%d" % i) for i in range(H.NW)]
    H.wi = 0
    names = ["sg", "lg", "kk", "F", "Fs", "e1", "e2", "t1"]
    H.t = {}; H.r_t = {}
    H.t[("qs", 0)] = A.alloc([512], F32); H.r_t[("qs", 0)] = P.res()
    for d in range(2):
        for n in names:
            H.t[(n, d)] = A.alloc([512], F32); H.r_t[(n, d)] = P.res()
        H.t[("kgf", d)] = H.t[("sg", d)]; H.r_t[("kgf", d)] = H.r_t[("sg", d)]
    H.sgo = A.alloc([512], F32); H.r_sgo = P.res()
    H.osum = A.alloc([512], F32); H.r_osum = P.res()
    for n in ["qg", "kg", "kd", "qD", "At"]:
        for d in range(2):
            H.t[(n, d)] = A.alloc([512], BF16); H.r_t[(n, d)] = P.res()
    H.kdT = [A.alloc([4, 128], BF16) for _ in range(2)]; H.r_kdT = [P.res() for _ in range(2)]
    H.S = [A.alloc([16, 128], F32) for _ in range(2)]; H.r_S = [[P.res() for _ in range(16)] for _ in range(2)]
    H.Sb = [A.alloc([128], BF16) for _ in range(2)]; H.r_Sb = [P.res() for _ in range(2)]
    H.Dsum = A.alloc([2, 16], F32); H.r_Dsum = P.res()
    H.mask = [A.alloc([512], F32) for _ in range(2)]; H.r_mask = P.res()
    H.qDo = [A.alloc([512], BF16) for _ in range(2)]; H.r_qDo = [P.res() for _ in range(2)]
    H.xst = [A.alloc([4, 512], F32) for _ in range(2)]; H.r_xst = [P.res() for _ in range(2)]
    H.s_xst = [P.dsem("s_xst%d" % i) for i in range(2)]; H.xst_i = 0
    H.Sc = [A.alloc([16, 128], F32) for _ in range(2)]; H.r_Sc = [P.res() for _ in range(2)]
    H.Dc = A.alloc([2, 16], F32); H.r_Dc = P.res()
    H.s_sg = P.dsem("s_h1sg"); H.s_ol = P.dsem("s_h1ol"); H.s_q = [P.dsem("s_h1q%d" % d) for d in range(2)]
    H.s_S = [P.dsem("s_h1S%d" % d) for d in range(2)]; H.s_D = P.dsem("s_h1D"); H.s_core = P.dsem("s_h1c")
    return H


def emit_fill_xm_dram(C, H, xin, t0, T, opscol, shcol, r_vec):
    P = C.P
    for q in range(4):
        i = H.xst_i % 2; H.xst_i += 1
        xst, r_xst, s_xst = H.xst[i], H.r_xst[i], H.s_xst[i]
        P.dma("sp", lambda e, q=q, xst=xst: e.dma_start(out=xst[:, :, :T], in_=xin[:, q * 4:(q + 1) * 4, t0:t0 + T]), s_xst, writes=[r_xst])
        for cc in range(4):
            c = q * 4 + cc
            P.op("act", lambda e, c=c, cc=cc, xst=xst: e.activation(out=H.xm[:, c, :T], in_=xst[:, cc, :T], func=AF.Identity, bias=shcol(c), scale=opscol(c)),
                 reads=[r_xst, r_vec], writes=[H.r_xm[c]])


def emit_hgrn_local(C, H, T, w_in, lbcol, omlbcol, r_vec, outs, seg, need_out, first_seg, last_seg):
    P = C.P
    xm = H.xm
    nt = T // 128
    nch = T // 64
    w_v = w_in.rearrange("(c p) f -> p c f", p=128)

    def load_w(col0):
        si = H.wi % H.NW; H.wi += 1
        ws, r_ws, s_ws = H.ws[si], H.r_ws[si], H.s_ws[si]
        P.dma("pool", lambda e: e.dma_start(out=ws[:], in_=w_v[:, :, col0:col0 + 256]), s_ws, writes=[r_ws])
        return ws, r_ws

    for cb in range(8):
        ws, r_ws = load_w(D + cb * 256)
        for tt in range(nt):
            pk, r_pk = C.pb[tt], C.r_pb[tt]
            for c in range(16):
                P.op("pe", lambda e, c=c, tt=tt, ws=ws, pk=pk: e.matmul(pk[:, :256], lhsT=xm[:, c, tt * 128:(tt + 1) * 128], rhs=ws[:, c, :],
                                                                        start=(c == 0), stop=(c == 15)),
                     reads=[H.r_xm[c], r_ws], writes=[r_pk])
            P.op("act", lambda e, tt=tt, cb=cb, pk=pk: e.activation(out=H.vt[:, tt, cb * 256:(cb + 1) * 256], in_=pk[:, :256], func=AF.Copy),
                 reads=[r_pk], writes=[H.r_vt[tt]])
    types = ["q", "af", "ab", "og"] if need_out else ["af", "ab"]
    tcol = {"q": 0, "af": 2 * D, "ab": 3 * D, "og": 4 * D}
    pbank = 0
    for hg in range(8):
        slots = {ty: load_w(tcol[ty] + hg * 256) for ty in types}
        for hh in range(2):
            h = hg * 2 + hh
            pr = {}
            for ty in types:
                bk = pbank % 3; pbank += 1
                pk, r_pk = C.pb[bk], C.r_pb[bk]
                ws, r_ws = slots[ty]
                for c in range(16):
                    P.op("pe", lambda e, c=c, ws=ws, hh=hh, pk=pk: e.matmul(pk[:, :T], lhsT=ws[:, c, hh * 128:(hh + 1) * 128], rhs=xm[:, c, :T],
                                                                            start=(c == 0), stop=(c == 15)),
                         reads=[H.r_xm[c], r_ws], writes=[r_pk])
                pr[ty] = (pk, r_pk)
                if ty == "q":
                    qs, r_qs = H.t[("qs", 0)], H.r_t[("qs", 0)]
                    P.op("act", lambda e, pk=pk, qs=qs: e.activation(out=qs[:, :T], in_=pk[:, :T], func=AF.Silu), reads=[r_pk], writes=[r_qs])
                    P.op("dve", lambda e, qs=qs: e.tensor_scalar(out=qs[:, :T], in0=qs[:, :T], scalar1=HK ** -0.5, scalar2=None, op0=ALU.mult),
                         reads=[r_qs], writes=[r_qs])
                elif ty == "og":
                    P.op("act", lambda e, pk=pk: e.activation(out=H.sgo[:, :T], in_=pk[:, :T], func=AF.Silu), reads=[r_pk], writes=[H.r_sgo])
                    P.dma("sp", lambda e, h=h: e.dma_start(out=outs["sg"](h, T), in_=H.sgo[:, :T]), H.s_sg, reads=[H.r_sgo])
                else:
                    d = 0 if ty == "af" else 1
                    sg, r_sg = H.t[("sg", d)], H.r_t[("sg", d)]
                    P.op("act", lambda e, pk=pk, sg=sg: e.activation(out=sg[:, :T], in_=pk[:, :T], func=AF.Sigmoid), reads=[r_pk], writes=[r_sg])
            qs, r_qs = H.t[("qs", 0)], H.r_t[("qs", 0)]
            for d in range(2):
                tg = lambda n: (H.t[(n, d)], H.r_t[(n, d)])
                sg, r_sg = tg("sg"); lg, r_lg = tg("lg"); kk, r_kk = tg("kk"); F, r_F = tg("F"); Fs, r_Fs = tg("Fs")
                e1, r_e1 = tg("e1"); e2, r_e2 = tg("e2"); kgf, r_kgf = tg("kgf"); t1, r_t1 = tg("t1")
                qg, r_qg = tg("qg"); kg, r_kg = tg("kg"); kd, r_kd = tg("kd"); qD, r_qD = tg("qD"); At, r_At = tg("At")
                P.op("dve", lambda e, sg=sg, d=d, h=h: e.tensor_scalar(out=sg[:, :T], in0=sg[:, :T], scalar1=omlbcol(d, h), scalar2=lbcol(d, h),
                                                                        op0=ALU.mult, op1=ALU.add), reads=[r_sg, r_vec], writes=[r_sg])
                P.op("act", lambda e, sg=sg, lg=lg: e.activation(out=lg[:, :T], in_=sg[:, :T], func=AF.Ln), reads=[r_sg], writes=[r_lg])
                P.op("dve", lambda e, sg=sg, kk=kk: e.tensor_scalar(out=kk[:, :T], in0=sg[:, :T], scalar1=-1.0, scalar2=1.0, op0=ALU.mult, op1=ALU.add),
                     reads=[r_sg], writes=[r_kk])
                for j in range(nch):
                    P.op("dve", lambda e, j=j, F=F, lg=lg: e.tensor_tensor_scan(out=F[:, j * 64:(j + 1) * 64], data0=C.ones[:, 0:64], data1=lg[:, j * 64:(j + 1) * 64],
                                                                               initial=0.0, op0=ALU.mult, op1=ALU.add),
                         reads=[r_lg, C.r_ones], writes=[r_F])
                if need_out:
                    for q4 in range(T // 128):
                        init = 0.0 if q4 == 0 else Fs[:, q4 * 128 - 1:q4 * 128]
                        P.op("dve", lambda e, q4=q4, Fs=Fs, lg=lg, init=init: e.tensor_tensor_scan(out=Fs[:, q4 * 128:(q4 + 1) * 128], data0=C.ones[:, 0:128],
                                                                                                 data1=lg[:, q4 * 128:(q4 + 1) * 128], initial=init, op0=ALU.mult, op1=ALU.add),
                             reads=[r_lg, C.r_ones, r_Fs], writes=[r_Fs])
                F3 = F[:, :T].rearrange("p (j t) -> p j t", t=64)
                if d == 0:
                    P.op("act", lambda e, e1=e1, F=F: e.activation(out=e1[:, :T], in_=F[:, :T], func=AF.Exp), reads=[r_F], writes=[r_e1])
                    P.op("act", lambda e, e2=e2, F=F: e.activation(out=e2[:, :T], in_=F[:, :T], func=AF.Exp, scale=-1.0), reads=[r_F], writes=[r_e2])
                    dec = lambda j, e1=e1: e1[:, j * 64 + 63:j * 64 + 64]
                    decb = e1[:, :T].rearrange("p (j t) -> p j t", t=64)[:, :, 63:64].to_broadcast([128, nch, 64])
                else:
                    P.op("dve", lambda e, t1=t1, F=F, lg=lg: e.tensor_tensor(out=t1[:, :T], in0=F[:, :T], in1=lg[:, :T], op=ALU.subtract), reads=[r_F, r_lg], writes=[r_t1])
                    P.op("dve", lambda e, t1=t1, F3=F3: e.tensor_tensor(out=t1[:, :T].rearrange("p (j t) -> p j t", t=64), in0=t1[:, :T].rearrange("p (j t) -> p j t", t=64),
                                                                        in1=F3[:, :, 63:64].to_broadcast([128, nch, 64]), op=ALU.subtract), reads=[r_t1, r_F], writes=[r_t1])
                    P.op("act", lambda e, e1=e1, t1=t1: e.activation(out=e1[:, :T], in_=t1[:, :T], func=AF.Exp, scale=-1.0), reads=[r_t1], writes=[r_e1])
                    P.op("act", lambda e, e2=e2, t1=t1: e.activation(out=e2[:, :T], in_=t1[:, :T], func=AF.Exp), reads=[r_t1], writes=[r_e2])
                    dec = lambda j, e1=e1: e1[:, j * 64:j * 64 + 1]
                    decb = e1[:, :T].rearrange("p (j t) -> p j t", t=64)[:, :, 0:1].to_broadcast([128, nch, 64])
                P.op("dve", lambda e, kgf=kgf, kk=kk, e2=e2: e.tensor_tensor(out=kgf[:, :T], in0=kk[:, :T], in1=e2[:, :T], op=ALU.mult), reads=[r_kk, r_e2], writes=[r_kgf])
                P.op("dve", lambda e, kd=kd, kgf=kgf, decb=decb: e.tensor_tensor(out=kd[:, :T].rearrange("p (j t) -> p j t", t=64), in0=kgf[:, :T].rearrange("p (j t) -> p j t", t=64),
                                                                                in1=decb, op=ALU.mult), reads=[r_kgf, r_e1], writes=[r_kd])
                if need_out:
                    P.op("act", lambda e, kg=kg, kgf=kgf: e.activation(out=kg[:, :T], in_=kgf[:, :T], func=AF.Copy), reads=[r_kgf], writes=[r_kg])
                    P.op("dve", lambda e, qg=qg, qs=qs, e1=e1: e.tensor_tensor(out=qg[:, :T], in0=qs[:, :T], in1=e1[:, :T], op=ALU.mult), reads=[r_qs, r_e1], writes=[r_qg])
                    if d == 0:
                        P.op("act", lambda e, t1=t1, Fs=Fs: e.activation(out=t1[:, :T], in_=Fs[:, :T], func=AF.Exp), reads=[r_Fs], writes=[r_t1])
                        dsrc = t1[:, T - 1:T]
                    else:
                        P.op("dve", lambda e, t1=t1, Fs=Fs, lg=lg: e.tensor_tensor(out=t1[:, :T], in0=Fs[:, :T], in1=lg[:, :T], op=ALU.subtract), reads=[r_Fs, r_lg], writes=[r_t1])
                        P.op("dve", lambda e, t1=t1, Fs=Fs: e.tensor_scalar(out=t1[:, :T], in0=t1[:, :T], scalar1=Fs[:, T - 1:T], scalar2=None, op0=ALU.subtract),
                             reads=[r_t1, r_Fs], writes=[r_t1])
                        P.op("act", lambda e, t1=t1: e.activation(out=t1[:, :T], in_=t1[:, :T], func=AF.Exp, scale=-1.0), reads=[r_t1], writes=[r_t1])
                        dsrc = t1[:, 0:1]
                    P.op("dve", lambda e, d=d, h=h, t1=t1, qs=qs: e.tensor_tensor(out=H.qDo[d][:, :T], in0=qs[:, :T], in1=t1[:, :T], op=ALU.mult),
                         reads=[r_qs, r_t1], writes=[H.r_qDo[d]])
                    P.dma("sp", lambda e, d=d, h=h: e.dma_start(out=outs["qD"](d, h, T), in_=H.qDo[d][:, :T]), H.s_q[d], reads=[H.r_qDo[d]])
                else:
                    for q4 in range(T // 128):
                        init = 0.0 if q4 == 0 else Fs[:, q4 * 128 - 1:q4 * 128]
                        P.op("dve", lambda e, q4=q4, Fs=Fs, lg=lg, init=init: e.tensor_tensor_scan(out=Fs[:, q4 * 128:(q4 + 1) * 128], data0=C.ones[:, 0:128],
                                                                                                 data1=lg[:, q4 * 128:(q4 + 1) * 128], initial=init, op0=ALU.mult, op1=ALU.add),
                             reads=[r_lg, C.r_ones, r_Fs], writes=[r_Fs])
                    P.op("act", lambda e, t1=t1, Fs=Fs: e.activation(out=t1[:, T - 1:T], in_=Fs[:, T - 1:T], func=AF.Exp), reads=[r_Fs], writes=[r_t1])
                    dsrc = t1[:, T - 1:T]
                P.op("dve", lambda e, d=d, h=h, dsrc=dsrc: e.tensor_copy(out=H.Dsum[:, d, h:h + 1], in_=dsrc), reads=[r_t1], writes=[H.r_Dsum])
                pT = C.pb[3][:].bitcast(BF16)
                r_pT = C.r_pb[3]
                for tt in range(nt):
                    P.op("pe", lambda e, tt=tt, kd=kd, pT=pT, d=d: e.transpose(out=pT[:, d * 512 + tt * 128: d * 512 + (tt + 1) * 128], in_=kd[:, tt * 128:(tt + 1) * 128], identity=C.identb[:]),
                         reads=[r_kd, C.r_identb], writes=[r_pT])
                P.op("act", lambda e, d=d, pT=pT: e.activation(out=H.kdT[d][:, 0:nt, :].rearrange("p a b -> p (a b)"), in_=pT[:, d * 512:d * 512 + nt * 128], func=AF.Copy),
                     reads=[r_pT], writes=[H.r_kdT[d]])
                pA, r_pA = C.pb[4], C.r_pb[4]
                po, r_po = C.pb[5 + d], C.r_pb[5 + d]
                pS, r_pS = C.pb[7], C.r_pb[7]
                if need_out:
                    for j in range(nch):
                        hf = (j % 2) * 64
                        P.op("pe", lambda e, j=j, hf=hf, kg=kg, qg=qg: e.matmul(pA[hf:hf + 64, j * 64:(j + 1) * 64], lhsT=kg[:, j * 64:(j + 1) * 64], rhs=qg[:, j * 64:(j + 1) * 64],
                                                                                start=True, stop=True), reads=[r_kg, r_qg], writes=[r_pA])
                    P.op("dve", lambda e, At=At, d=d: e.tensor_tensor(out=At[:, :T], in0=pA[:, :T], in1=H.mask[d][:, :T], op=ALU.mult),
                         reads=[r_pA, H.r_mask], writes=[r_At])
                order = list(range(nch)) if d == 0 else list(range(nch - 1, -1, -1))
                S = H.S[d]; r_S = H.r_S[d][h]; Sb = H.Sb[d]; r_Sb = H.r_Sb[d]
                for n, j in enumerate(order):
                    hf = (j % 2) * 64; tt = j // 2
                    if need_out:
                        P.op("pe", lambda e, j=j, hf=hf, tt=tt, At=At, h=h, n=n, po=po: e.matmul(po[:, j * 64:(j + 1) * 64], lhsT=H.vt[hf:hf + 64, tt, h * 128:(h + 1) * 128],
                                                                                         rhs=At[hf:hf + 64, j * 64:(j + 1) * 64], start=True, stop=(n == 0)),
                             reads=[H.r_vt[tt], r_At], writes=[r_po])
                        if n > 0:
                            P.op("pe", lambda e, j=j, Sb=Sb, qg=qg, po=po: e.matmul(po[:, j * 64:(j + 1) * 64], lhsT=Sb[:, :], rhs=qg[:, j * 64:(j + 1) * 64], start=False, stop=True),
                                 reads=[r_Sb, r_qg], writes=[r_po])
                    P.op("pe", lambda e, hf=hf, tt=tt, d=d, h=h: e.matmul(pS[:, 0:128], lhsT=H.kdT[d][hf:hf + 64, tt, :], rhs=H.vt[hf:hf + 64, tt, h * 128:(h + 1) * 128],
                                                                         start=True, stop=True), reads=[H.r_kdT[d], H.r_vt[tt]], writes=[r_pS])
                    if n == 0:
                        P.op("dve", lambda e, S=S, h=h: e.tensor_copy(out=S[:, h, :], in_=pS[:, 0:128]), reads=[r_pS], writes=[r_S])
                    else:
                        P.op("dve", lambda e, S=S, h=h, j=j, dec=dec: e.scalar_tensor_tensor(out=S[:, h, :], in0=S[:, h, :], scalar=dec(j), in1=pS[:, 0:128],
                                                                                            op0=ALU.mult, op1=ALU.add), reads=[r_pS, r_S, r_e1], writes=[r_S])
                    if need_out and n < nch - 1:
                        P.op("act", lambda e, S=S, h=h, Sb=Sb: e.activation(out=Sb[:, :], in_=S[:, h, :], func=AF.Copy), reads=[r_S], writes=[r_Sb])
            if need_out:
                P.op("act", lambda e: e.activation(out=H.osum[:, :T], in_=C.pb[5][:, :T], func=AF.Copy), reads=[C.r_pb[5]], writes=[H.r_osum])
                P.op("dve", lambda e: e.tensor_tensor(out=H.osum[:, :T], in0=H.osum[:, :T], in1=C.pb[6][:, :T], op=ALU.add), reads=[H.r_osum, C.r_pb[6]], writes=[H.r_osum])
                P.dma("sp", lambda e, h=h: e.dma_start(out=outs["ol"](h, T), in_=H.osum[:, :T]), H.s_ol, reads=[H.r_osum])
    allS = lambda d: [r for r in H.r_S[d]]
    for d in range(2):
        P.dma("sp", lambda e, d=d: e.dma_start(out=outs["segS"](d, seg), in_=H.S[d][:]), H.s_S[d], reads=allS(d))
    P.dma("sp", lambda e: e.dma_start(out=outs["segD"](seg), in_=H.Dsum[:]), H.s_D, reads=[H.r_Dsum])
    if need_out:
        for h in range(16):
            if first_seg:
                P.op("dve", lambda e, h=h: e.tensor_copy(out=H.Sc[0][:, h, :], in_=H.S[0][:, h, :]), reads=[H.r_S[0][h]], writes=[H.r_Sc[0]])
                P.op("dve", lambda e, h=h: e.tensor_copy(out=H.Sc[1][:, h, :], in_=H.S[1][:, h, :]), reads=[H.r_S[1][h]], writes=[H.r_Sc[1]])
            else:
                P.op("dve", lambda e, h=h: e.scalar_tensor_tensor(out=H.Sc[0][:, h, :], in0=H.Sc[0][:, h, :], scalar=H.Dsum[:, 0, h:h + 1], in1=H.S[0][:, h, :],
                                                                  op0=ALU.mult, op1=ALU.add), reads=[H.r_S[0][h], H.r_Dsum], writes=[H.r_Sc[0]])
                P.op("dve", lambda e, h=h: e.scalar_tensor_tensor(out=H.Sc[1][:, h, :], in0=H.S[1][:, h, :], scalar=H.Dc[:, 1, h:h + 1], in1=H.Sc[1][:, h, :],
                                                                  op0=ALU.mult, op1=ALU.add), reads=[H.r_S[1][h], H.r_Dc], writes=[H.r_Sc[1]])
        if first_seg:
            P.op("dve", lambda e: e.tensor_copy(out=H.Dc[:], in_=H.Dsum[:]), reads=[H.r_Dsum], writes=[H.r_Dc])
        else:
            P.op("dve", lambda e: e.tensor_tensor(out=H.Dc[:], in0=H.Dc[:], in1=H.Dsum[:], op=ALU.mult), reads=[H.r_Dsum, H.r_Dc], writes=[H.r_Dc])
        if last_seg:
            for d in range(2):
                P.dma("sp", lambda e, d=d: e.dma_start(out=outs["coreS"](d), in_=H.Sc[d][:]), H.s_core, reads=[H.r_Sc[d]])
            P.dma("sp", lambda e: e.dma_start(out=outs["coreD"](), in_=H.Dc[:]), H.s_core, reads=[H.r_Dc])


def hgrn1_final_sems(H):
    return [(s.h, s.count) for s in [H.s_sg, H.s_ol, H.s_q[0], H.s_q[1], H.s_S[0], H.s_S[1], H.s_D, H.s_core] if s.count]


def build_hgrn1_program(segs, NT, nseg):
    nc = bass.Bass("TRN2", target_bir_lowering=False)
    xin = nc.dram_tensor("xin", [128, 16, NT], F32, kind="ExternalInput").ap()
    vec = nc.dram_tensor("vec", [128, 8, 16], F32, kind="ExternalInput").ap()
    w_in = nc.dram_tensor("w_in", [D, 5 * D], F32, kind="ExternalInput").ap()
    maskd = nc.dram_tensor("masks", [2, 128, 512], F32, kind="ExternalInput").ap()
    NO = 2048
    ol = nc.dram_tensor("ol", [128, 16, NO], F32, kind="ExternalOutput").ap()
    sg = nc.dram_tensor("sg", [128, 16, NO], F32, kind="ExternalOutput").ap()
    qD = nc.dram_tensor("qD", [2, 128, 16, NO], BF16, kind="ExternalOutput").ap()
    segS = nc.dram_tensor("segS", [2, nseg, 128, 16, 128], F32, kind="ExternalOutput").ap()
    segD = nc.dram_tensor("segD", [nseg, 128, 2, 16], F32, kind="ExternalOutput").ap()
    coreS = nc.dram_tensor("coreS", [2, 128, 16, 128], F32, kind="ExternalOutput").ap()
    coreD = nc.dram_tensor("coreD", [128, 2, 16], F32, kind="ExternalOutput").ap()
    with ExitStack() as st:
        P = Prog(nc, st)
        C = setup_common(P)
        A = Arena(P, 190 * 1024)
        H = setup_hgrn1(C, A)
        V = P.sbuf("V", [128, 8, 16], F32); r_V = P.res()
        s_v = P.dsem("s_v"); s_x = P.dsem("s_x"); s_m = P.dsem("s_m")
        P.dma("sp", lambda e: e.dma_start(out=V[:], in_=vec), s_v, writes=[r_V])
        for k in (0, 2):
            P.op("dve", lambda e, k=k: e.tensor_scalar(out=V[:, k, :], in0=V[:, k, :], scalar1=1.0, scalar2=None, op0=ALU.add), reads=[r_V], writes=[r_V])
        P.op("dve", lambda e: e.tensor_tensor(out=V[:, 4, :], in0=V[:, 5, :], in1=V[:, 4, :], op=ALU.subtract), reads=[r_V], writes=[r_V])
        P.op("dve", lambda e: e.tensor_tensor(out=V[:, 5, :], in0=V[:, 7, :], in1=V[:, 6, :], op=ALU.subtract), reads=[r_V], writes=[r_V])
        P.op("act", lambda e: e.activation(out=V[:, 4:6, :], in_=V[:, 4:6, :], func=AF.Sigmoid), reads=[r_V], writes=[r_V])
        P.op("dve", lambda e: e.tensor_scalar(out=V[:, 6:8, :], in0=V[:, 4:6, :], scalar1=-1.0, scalar2=1.0, op0=ALU.mult, op1=ALU.add), reads=[r_V], writes=[r_V])
        for d in range(2):
            P.dma("sp", lambda e, d=d: e.dma_start(out=H.mask[d][:], in_=maskd[d]), s_m, writes=[H.r_mask])
        col = lambda k: (lambda c: V[:, k, c:c + 1])
        lbcol = lambda d, h: V[:, 4 + d, h:h + 1]
        omlbcol = lambda d, h: V[:, 6 + d, h:h + 1]
        lat = [s for s in segs if s[3]]
        for (t0, T, ms, need_out, si) in segs:
            emit_fill_xm_dram(C, H, xin, t0, T, col(0 + 2 * ms), col(1 + 2 * ms), r_V)
            outs = {
                "sg": lambda h, T, t0=t0: sg[:, h, t0:t0 + T],
                "ol": lambda h, T, t0=t0: ol[:, h, t0:t0 + T],
                "qD": lambda d, h, T, t0=t0: qD[d, :, h, t0:t0 + T],
                "segS": lambda d, seg: segS[d, seg],
                "segD": lambda seg: segD[seg],
                "coreS": lambda d: coreS[d],
                "coreD": lambda: coreD,
            }
            emit_hgrn_local(C, H, T, w_in, lbcol, omlbcol, r_V, outs, si, need_out,
                            first_seg=(need_out and si == lat[0][4]), last_seg=(need_out and si == lat[-1][4]))
        P.emit(final_waits={"sp": hgrn1_final_sems(H)})
    return nc


def setup_hgrn2(C, A):
    P = C.P
    G = Ctx()
    G.SinB = [[A.alloc([16, 128], BF16) for _ in range(4)] for _ in range(2)]
    G.r_SinB = [[P.res() for _ in range(4)] for _ in range(2)]
    G.Sw = A.alloc([16, 128], F32); G.r_Sw = P.res()
    G.ld = [A.alloc([16, 128], F32) for _ in range(2)]; G.r_ld = [P.res() for _ in range(2)]
    G.s_ld = [P.dsem("s_g2ld%d" % i) for i in range(2)]; G.ld_i = 0
    G.ldD = [A.alloc([16], F32) for _ in range(2)]; G.r_ldD = [P.res() for _ in range(2)]
    G.s_ldD = [P.dsem("s_g2ldD%d" % i) for i in range(2)]
    G.segD = A.alloc([5, 2, 16], F32); G.r_segD = P.res(); G.s_segD = P.dsem("s_g2sd")
    G.xs = A.alloc([16, 512], F32); G.r_xs = [P.res() for _ in range(16)]
    G.onb = A.alloc([16, 512], BF16); G.r_onb = [P.res() for _ in range(16)]
    G.NB = 2
    G.olh = [A.alloc([512], F32) for _ in range(G.NB)]; G.r_olh = [P.res() for _ in range(G.NB)]; G.s_olh = [P.dsem("s_g2ol%d" % i) for i in range(G.NB)]
    G.sgh = [A.alloc([512], F32) for _ in range(G.NB)]; G.r_sgh = [P.res() for _ in range(G.NB)]; G.s_sgh = [P.dsem("s_g2sg%d" % i) for i in range(G.NB)]
    G.qDh = [[A.alloc([512], BF16) for _ in range(G.NB)] for _ in range(2)]
    G.r_qDh = [[P.res() for _ in range(G.NB)] for _ in range(2)]
    G.s_qDh = [[P.dsem("s_g2q%d%d" % (d, i)) for i in range(G.NB)] for d in range(2)]
    G.hb = 0
    G.t = [A.alloc([512], F32) for _ in range(2)]; G.r_t = [P.res() for _ in range(2)]
    G.ws = [A.alloc([16, 512], BF16) for _ in range(2)]; G.r_ws = [P.res() for _ in range(2)]; G.s_ws = [P.dsem("s_g2w%d" % i) for i in range(2)]
    G.wi = 0
    G.s_x = P.dsem("s_g2x"); G.s_out = P.dsem("s_g2out")
    return G


def emit_hgrn_combine(C, G, segS, segD, chainS, chainD, nlat=4, ictx=4):
    P = C.P
    P.dma("sp", lambda e: e.dma_start(out=G.segD[:], in_=segD.rearrange("s p d h -> p s d h")), G.s_segD, writes=[G.r_segD])

    def load(ap):
        i = G.ld_i % 2; G.ld_i += 1
        P.dma("sp", lambda e, i=i, ap=ap: e.dma_start(out=G.ld[i][:], in_=ap), G.s_ld[i], writes=[G.r_ld[i]])
        return G.ld[i], G.r_ld[i]

    for d in range(2):
        t, r_t = load(segS(d, ictx))
        P.op("dve", lambda e, t=t: e.tensor_copy(out=G.Sw[:], in_=t[:]), reads=[r_t], writes=[G.r_Sw])
        for i in range(7):
            t, r_t = load(chainS(d, i))
            P.dma("sp", lambda e, d=d, i=i: e.dma_start(out=G.ldD[i % 2][:], in_=chainD(d, i)), G.s_ldD[i % 2], writes=[G.r_ldD[i % 2]])
            for h in range(16):
                P.op("dve", lambda e, h=h, t=t, i=i: e.scalar_tensor_tensor(out=G.Sw[:, h, :], in0=G.Sw[:, h, :], scalar=G.ldD[i % 2][:, h:h + 1], in1=t[:, h, :],
                                                                            op0=ALU.mult, op1=ALU.add), reads=[r_t, G.r_ldD[i % 2], G.r_Sw], writes=[G.r_Sw])
        order = list(range(nlat)) if d == 0 else list(range(nlat - 1, -1, -1))
        for n, j in enumerate(order):
            P.op("act", lambda e, d=d, j=j: e.activation(out=G.SinB[d][j][:], in_=G.Sw[:], func=AF.Copy), reads=[G.r_Sw], writes=[G.r_SinB[d][j]])
            if n < nlat - 1:
                t, r_t = load(segS(d, j))
                for h in range(16):
                    P.op("dve", lambda e, h=h, t=t, j=j, d=d: e.scalar_tensor_tensor(out=G.Sw[:, h, :], in0=G.Sw[:, h, :], scalar=G.segD[:, j, d, h:h + 1], in1=t[:, h, :],
                                                                                    op0=ALU.mult, op1=ALU.add), reads=[r_t, G.r_segD, G.r_Sw], writes=[G.r_Sw])


def emit_hgrn_final(C, G, j, T, t0, xin, ol, sg, qD, w_out, xout, gpcol, ngcol, lngcol, lnbcol, r_vec):
    P = C.P
    xs, r_xs = G.xs, G.r_xs
    P.dma("sp", lambda e: e.dma_start(out=xs[:, :, :T], in_=xin[:, :, t0:t0 + T]), G.s_x, writes=r_xs)
    for h in range(16):
        b = G.hb % G.NB; G.hb += 1
        P.dma("sp", lambda e, h=h, b=b: e.dma_start(out=G.olh[b][:, :T], in_=ol[:, h, t0:t0 + T]), G.s_olh[b], writes=[G.r_olh[b]])
        P.dma("sp", lambda e, h=h, b=b: e.dma_start(out=G.sgh[b][:, :T], in_=sg[:, h, t0:t0 + T]), G.s_sgh[b], writes=[G.r_sgh[b]])
        for d in range(2):
            P.dma("sp", lambda e, h=h, b=b, d=d: e.dma_start(out=G.qDh[d][b][:, :T], in_=qD[d, :, h, t0:t0 + T]), G.s_qDh[d][b], writes=[G.r_qDh[d][b]])
        po, r_po = C.pb[2 + (h % 2)], C.r_pb[2 + (h % 2)]
        for d in range(2):
            P.op("pe", lambda e, d=d, h=h, b=b, po=po: e.matmul(po[:, :T], lhsT=G.SinB[d][j][:, h, :], rhs=G.qDh[d][b][:, :T], start=(d == 0), stop=(d == 1)),
                 reads=[G.r_SinB[d][j], G.r_qDh[d][b]], writes=[r_po])
        o = G.olh[b]; r_o = G.r_olh[b]
        P.op("dve", lambda e, o=o, po=po: e.tensor_tensor(out=o[:, :T], in0=o[:, :T], in1=po[:, :T], op=ALU.add), reads=[r_po, r_o], writes=[r_o])
        sq, r_sq = C.sq[h % 2], C.r_sq[h % 2]
        P.op("act", lambda e, o=o, sq=sq: e.activation(out=sq[:, :T], in_=o[:, :T], func=AF.Square), reads=[r_o], writes=[r_sq])
        pr, r_pr = C.pb[h % 2], C.r_pb[h % 2]
        P.op("pe", lambda e, sq=sq, pr=pr: e.matmul(pr[:, :T], lhsT=C.ones[:], rhs=sq[:, :T], start=True, stop=True), reads=[C.r_ones, r_sq], writes=[r_pr])
        t, r_t = G.t[h % 2], G.r_t[h % 2]
        P.op("dve", lambda e, t=t, pr=pr: e.tensor_scalar(out=t[:, :T], in0=pr[:, :T], scalar1=1.0 / HK, scalar2=RMS_EPS, op0=ALU.mult, op1=ALU.add),
             reads=[r_pr], writes=[r_t])
        P.op("act", lambda e, t=t: e.activation(out=t[:, :T], in_=t[:, :T], func=AF.Sqrt), reads=[r_t], writes=[r_t])
        P.op("dve", lambda e, t=t: e.reciprocal(out=t[:, :T], in_=t[:, :T]), reads=[r_t], writes=[r_t])
        P.op("dve", lambda e, t=t, o=o: e.tensor_tensor(out=o[:, :T], in0=o[:, :T], in1=t[:, :T], op=ALU.mult), reads=[r_t, r_o], writes=[r_o])
        P.op("dve", lambda e, o=o, h=h, b=b: e.scalar_tensor_tensor(out=G.onb[:, h, :T], in0=o[:, :T], scalar=ngcol(), in1=G.sgh[b][:, :T], op0=ALU.mult, op1=ALU.mult),
             reads=[r_o, G.r_sgh[b], r_vec], writes=[G.r_onb[h]])
    w_v = w_out.rearrange("(h p) d -> p h d", p=128)
    for db in range(4):
        si = G.wi % 2; G.wi += 1
        ws, r_ws, s_ws = G.ws[si], G.r_ws[si], G.s_ws[si]
        P.dma("pool", lambda e, ws=ws, db=db: e.dma_start(out=ws[:], in_=w_v[:, :, db * 512:(db + 1) * 512]), s_ws, writes=[r_ws])
        for dci in range(4):
            py, r_py = C.pb[4 + dci], C.r_pb[4 + dci]
            for h in range(16):
                P.op("pe", lambda e, ws=ws, h=h, dci=dci, py=py: e.matmul(py[:, :T], lhsT=ws[:, h, dci * 128:(dci + 1) * 128], rhs=G.onb[:, h, :T],
                                                                         start=(h == 0), stop=(h == 15)), reads=[r_ws, G.r_onb[h]], writes=[r_py])
        for dci in range(4):
            c = db * 4 + dci
            py, r_py = C.pb[4 + dci], C.r_pb[4 + dci]
            P.op("dve", lambda e, c=c, py=py: e.scalar_tensor_tensor(out=xs[:, c, :T], in0=py[:, :T], scalar=gpcol(c), in1=xs[:, c, :T], op0=ALU.mult, op1=ALU.add),
                 reads=[r_py, r_xs[c], r_vec], writes=[r_xs[c]])
    emit_ln(C, xs, r_xs, T, lngcol, lnbcol, r_vec)
    P.dma("sp", lambda e: e.dma_start(out=xout[:, :, t0:t0 + T], in_=xs[:, :, :T]), G.s_out, reads=r_xs)


def build_hgrn2_program():
    nc = bass.Bass("TRN2", target_bir_lowering=False)
    NO = 2048
    xin = nc.dram_tensor("xin", [128, 16, NO], F32, kind="ExternalInput").ap()
    vec = nc.dram_tensor("vec", [128, 5, 16], F32, kind="ExternalInput").ap()
    ol = nc.dram_tensor("ol", [128, 16, NO], F32, kind="ExternalInput").ap()
    sg = nc.dram_tensor("sg", [128, 16, NO], F32, kind="ExternalInput").ap()
    qD = nc.dram_tensor("qD", [2, 128, 16, NO], BF16, kind="ExternalInput").ap()
    segS = nc.dram_tensor("segS", [2, 5, 128, 16, 128], F32, kind="ExternalInput").ap()
    segD = nc.dram_tensor("segD", [5, 128, 2, 16], F32, kind="ExternalInput").ap()
    chainS = nc.dram_tensor("chainS", [2, 7, 128, 16, 128], F32, kind="ExternalInput").ap()
    chainD = nc.dram_tensor("chainD", [2, 7, 128, 16], F32, kind="ExternalInput").ap()
    w_out = nc.dram_tensor("w_out", [D, D], F32, kind="ExternalInput").ap()
    xout = nc.dram_tensor("xout", [128, 16, NO], F32, kind="ExternalOutput").ap()
    with ExitStack() as st:
        P = Prog(nc, st)
        C = setup_common(P)
        A = Arena(P, 185 * 1024)
        G = setup_hgrn2(C, A)
        V = P.sbuf("V", [128, 5, 16], F32); r_V = P.res()
        s_v = P.dsem("s_v")
        P.dma("sp", lambda e: e.dma_start(out=V[:], in_=vec), s_v, writes=[r_V])
        P.op("dve", lambda e: e.tensor_scalar(out=V[:, 4, :], in0=V[:, 0, :], scalar1=1.0 / ALPHA, scalar2=None, op0=ALU.mult), reads=[r_V], writes=[r_V])
        emit_hgrn_combine(C, G, lambda d, i: segS[d, i], segD, lambda d, i: chainS[d, i], lambda d, i: chainD[d, i])
        col = lambda k: (lambda c: V[:, k, c:c + 1])
        for j in range(4):
            emit_hgrn_final(C, G, j, 512, j * 512, xin, ol, sg, qD, w_out, xout, col(4), lambda: V[:, 3, 0:1], col(1), col(2), r_V)
        P.emit(final_waits={"sp": [(G.s_out.h, G.s_out.count)]})
    return nc


NQ = 18


def build_mod_program():
    nc = bass.Bass("TRN2", target_bir_lowering=False)
    cc = nc.dram_tensor("cc", [2, 16, 128], F32, kind="ExternalInput").ap()
    w = nc.dram_tensor("w", [2, D, NQ * 128], F32, kind="ExternalInput").ap()
    b = nc.dram_tensor("b", [2 * NQ, 128], F32, kind="ExternalInput").ap()
    mv = nc.dram_tensor("mv", [128, 2, NQ, 2], F32, kind="ExternalOutput").ap()
    with ExitStack() as st:
        P = Prog(nc, st)
        C = setup_common(P)
        cs = P.sbuf("cs", [16, 2, 128], F32); r_cs = P.res()
        bs = P.sbuf("bs", [2 * NQ, 128], F32); r_bs = P.res()
        scT = P.sbuf("scT", [128, 16, 2], BF16); r_scT = P.res()
        mbT = P.sbuf("mbT", [128, 2 * NQ], F32); r_mbT = P.res()
        mrow = [P.sbuf("mrow%d" % i, [2, 384], F32) for i in range(2)]; r_mrow = [P.res() for _ in range(2)]
        MVs = P.sbuf("MVs", [128, 2, NQ, 2], F32); r_MVs = P.res()
        ws = [P.sbuf("mws%d" % i, [128, 16, 384], BF16) for i in range(2)]; r_ws = [P.res() for _ in range(2)]
        s_ws = [P.dsem("s_mws%d" % i) for i in range(2)]
        s_c = P.dsem("s_c"); s_b = P.dsem("s_b"); s_o = P.dsem("s_o")
        for r in range(2):
            P.dma("sp", lambda e, r=r: e.dma_start(out=cs[:, r, :], in_=cc[r]), s_c, writes=[r_cs])
        P.dma("sp", lambda e: e.dma_start(out=bs[:], in_=b), s_b, writes=[r_bs])
        P.op("act", lambda e: e.activation(out=cs[:], in_=cs[:], func=AF.Silu), reads=[r_cs], writes=[r_cs])
        pt, r_pt = C.pb[3], C.r_pb[3]
        for r in range(2):
            P.op("pe", lambda e, r=r: e.transpose(out=pt[:, r * 16:(r + 1) * 16], in_=cs[:, r, :], identity=C.ident[0:16, 0:16]),
                 reads=[r_cs, C.r_ident], writes=[r_pt])
        for r in range(2):
            P.op("dve", lambda e, r=r: e.tensor_copy(out=scT[:, :, r], in_=pt[:, r * 16:(r + 1) * 16]), reads=[r_pt], writes=[r_scT])
        pb_, r_pb_ = C.pb[4], C.r_pb[4]
        P.op("pe", lambda e: e.transpose(out=pb_[:, 0:2 * NQ], in_=bs[:], identity=C.ident[0:2 * NQ, 0:2 * NQ]), reads=[r_bs, C.r_ident], writes=[r_pb_])
        P.op("dve", lambda e: e.tensor_copy(out=mbT[:], in_=pb_[:, 0:2 * NQ]), reads=[r_pb_], writes=[r_mbT])
        pT, r_pT = C.pb[2], C.r_pb[2]
        k = 0
        for i in range(2):
            for nb in range(6):
                si = k % 2
                P.dma("pool", lambda e, i=i, nb=nb, si=si: e.dma_start(out=ws[si][:], in_=w[i].rearrange("(c p) f -> p c f", p=128)[:, :, nb * 384:(nb + 1) * 384]),
                      s_ws[si], writes=[r_ws[si]])
                pm, r_pm = C.pb[k % 2], C.r_pb[k % 2]
                for c in range(16):
                    P.op("pe", lambda e, c=c, si=si, pm=pm: e.matmul(pm[0:2, 0:384], lhsT=scT[:, c, :], rhs=ws[si][:, c, :], start=(c == 0), stop=(c == 15)),
                         reads=[r_scT, r_ws[si]], writes=[r_pm])
                P.op("act", lambda e, si=si, pm=pm: e.activation(out=mrow[si][:], in_=pm[0:2, 0:384], func=AF.Copy), reads=[r_pm], writes=[r_mrow[si]])
                for t in range(3):
                    q = i * NQ + nb * 3 + t
                    P.op("pe", lambda e, si=si, t=t, q=q: e.transpose(out=pT[:, q * 2:(q + 1) * 2], in_=mrow[si][:, t * 128:(t + 1) * 128], identity=C.ident[0:2, 0:2]),
                         reads=[r_mrow[si], C.r_ident], writes=[r_pT])
                k += 1
        for r in range(2):
            P.op("dve", lambda e, r=r: e.tensor_tensor(out=MVs[:].rearrange("p i q r -> p (i q) r")[:, :, r], in0=pT[:, 0:4 * NQ].rearrange("p (q r) -> p q r", r=2)[:, :, r],
                                                       in1=mbT[:], op=ALU.add), reads=[r_pT, r_mbT], writes=[r_MVs])
        P.dma("sp", lambda e: e.dma_start(out=mv, in_=MVs[:]), s_o, reads=[r_MVs])
        P.emit(final_waits={"sp": [(s_o.h, s_o.count)]})
    return nc


from concourse.bass_utils import run_bass_kernel_spmd

NCORES = 8
_PROGS = {}


def _prog(name, fn):
    if name not in _PROGS:
        _PROGS[name] = fn()
    return _PROGS[name]


def to_fm(a):
    return np.ascontiguousarray(np.asarray(a, np.float32).T.reshape(16, 128, -1).transpose(1, 0, 2))


def from_fm(a):
    return np.asarray(a).transpose(1, 0, 2).reshape(D, -1).T


def vrow(v):
    return np.asarray(v, np.float32).reshape(16, 128).T


def _run(nc, in_maps):
    res = run_bass_kernel_spmd(nc, in_maps, core_ids=list(range(NCORES)))
    return res.results


NVT = 24
VT_LNG, VT_LNB, VT_PS, VT_LBRAW, VT_NG, VT_LB, VT_OMLB = 0, 6, 12, 13, 17, 18, 20
CB_W = 2 * 2048 + 32
MVW = 2 * NQ * 2


def emit_mod_shard(C, P, A, cc, w, b, mvb, r_mvb):
    cs = A.alloc([2, 128], F32)[0:16]; r_cs = P.res()
    bs = A.alloc([128], F32)[0:2 * NQ]; r_bs = P.res()
    scT = A.alloc([16, 2], BF16); r_scT = P.res()
    mbT = A.alloc([2 * NQ], F32); r_mbT = P.res()
    mrow = [A.alloc([384], F32)[0:2] for i in range(2)]; r_mrow = [P.res() for _ in range(2)]
    MVs = A.alloc([2, NQ, 2], F32); r_MVs = P.res()
    ws = [A.alloc([16, 384], BF16) for i in range(2)]; r_ws = [P.res() for _ in range(2)]
    s_ws = [P.dsem("s_mws%d" % i) for i in range(2)]
    s_c = P.dsem("s_c"); s_b = P.dsem("s_b"); s_o = P.dsem("s_o")
    for r in range(2):
        P.dma("sp", lambda e, r=r: e.dma_start(out=cs[:, r, :], in_=cc[r]), s_c, writes=[r_cs])
    P.dma("sp", lambda e: e.dma_start(out=bs[:], in_=b), s_b, writes=[r_bs])
    P.op("act", lambda e: e.activation(out=cs[:], in_=cs[:], func=AF.Silu), reads=[r_cs], writes=[r_cs])
    pt, r_pt = C.pb[3], C.r_pb[3]
    for r in range(2):
        P.op("pe", lambda e, r=r: e.transpose(out=pt[:, r * 16:(r + 1) * 16], in_=cs[:, r, :], identity=C.ident[0:16, 0:16]),
             reads=[r_cs, C.r_ident], writes=[r_pt])
    for r in range(2):
        P.op("dve", lambda e, r=r: e.tensor_copy(out=scT[:, :, r], in_=pt[:, r * 16:(r + 1) * 16]), reads=[r_pt], writes=[r_scT])
    pb_, r_pb_ = C.pb[4], C.r_pb[4]
    P.op("pe", lambda e: e.transpose(out=pb_[:, 0:2 * NQ], in_=bs[:], identity=C.ident[0:2 * NQ, 0:2 * NQ]), reads=[r_bs, C.r_ident], writes=[r_pb_])
    P.op("dve", lambda e: e.tensor_copy(out=mbT[:], in_=pb_[:, 0:2 * NQ]), reads=[r_pb_], writes=[r_mbT])
    pT, r_pT = C.pb[2], C.r_pb[2]
    k = 0
    for i in range(2):
        for nb in range(6):
            si = k % 2
            P.dma("pool", lambda e, i=i, nb=nb, si=si: e.dma_start(out=ws[si][:], in_=w[i].rearrange("(c p) f -> p c f", p=128)[:, :, nb * 384:(nb + 1) * 384]),
                  s_ws[si], writes=[r_ws[si]])
            pm, r_pm = C.pb[k % 2], C.r_pb[k % 2]
            for c in range(16):
                P.op("pe", lambda e, c=c, si=si, pm=pm: e.matmul(pm[0:2, 0:384], lhsT=scT[:, c, :], rhs=ws[si][:, c, :], start=(c == 0), stop=(c == 15)),
                     reads=[r_scT, r_ws[si]], writes=[r_pm])
            P.op("act", lambda e, si=si, pm=pm: e.activation(out=mrow[si][:], in_=pm[0:2, 0:384], func=AF.Copy), reads=[r_pm], writes=[r_mrow[si]])
            for t in range(3):
                q = i * NQ + nb * 3 + t
                P.op("pe", lambda e, si=si, t=t, q=q: e.transpose(out=pT[:, q * 2:(q + 1) * 2], in_=mrow[si][:, t * 128:(t + 1) * 128], identity=C.ident[0:2, 0:2]),
                     reads=[r_mrow[si], C.r_ident], writes=[r_pT])
            k += 1
    for r in range(2):
        P.op("dve", lambda e, r=r: e.tensor_tensor(out=MVs[:].rearrange("p i q r -> p (i q) r")[:, :, r], in0=pT[:, 0:4 * NQ].rearrange("p (q r) -> p q r", r=2)[:, :, r],
                                                   in1=mbT[:], op=ALU.add), reads=[r_pT, r_mbT], writes=[r_MVs])
    P.dma("sp", lambda e: e.dma_start(out=mvb, in_=MVs[:].rearrange("p i q r -> p (i q r)")), s_o, reads=[r_MVs], writes=[r_mvb])


def build_fused_program():
    nc = bass.Bass("TRN2", target_bir_lowering=False)
    inp = lambda n, sh, dt=F32: nc.dram_tensor(n, sh, dt, kind="ExternalInput").ap()
    xslab = inp("xslab", [128, 16, 3264])
    cc = inp("cc", [2, 16, 128]); mw = inp("mw", [2, D, NQ * 128]); mbias = inp("mb", [2 * NQ, 128])
    vtab_d = inp("vtab", [128, NVT, 16])
    ffn_w_in = inp("ffn_w_in", [2, 2, D, 2 * DFF]); ffn_w_out = inp("ffn_w_out", [2, 2, DFF, D])
    pool_w = inp("pool_w", [4, 512, 512])
    valid = inp("valid", [128, 3008]); inv = inp("inv", [4, 128, 2048]); invc = inp("invc", [4, 128, 256])
    hgrn_w_in = inp("hgrn_w_in", [D, 5 * D]); hgrn_w_out = inp("hgrn_w_out", [D, D])
    maskd = inp("masks", [2, 128, 512]); chm_d = inp("chm", [128, 2, 2, 8])
    xout = nc.dram_tensor("xout", [128, 16, 2048], F32, kind="ExternalOutput").ap()
    scr = lambda n, sh, dt=F32: nc.dram_tensor(n, sh, dt).ap()
    x1 = scr("x1", [128, 16, 3264]); zz = scr("zz", [128, 16, 2304]); x2 = scr("x2", [128, 16, 2304]); x3 = scr("x3", [128, 16, 2304])
    ol = scr("ol", [128, 16, 2048]); sg = scr("sg", [128, 16, 2048]); qD = scr("qD", [2, 128, 16, 2048], BF16)
    segS = scr("segS", [2, 5, 128, 16, 128]); segD = scr("segD", [5, 128, 2, 16])
    cb = scr("cb", [128, CB_W]); cg = scr("cg", [8 * 128, CB_W])
    mvb = scr("mvb", [128, MVW]); mvg = scr("mvg", [8 * 128, MVW])
    x4 = scr("x4", [128, 16, 2048])
    with ExitStack() as st:
        P = Prog(nc, st)
        C = setup_common(P)
        A = Arena(P, 187 * 1024)
        MVc = P.sbuf("MVc", [128, 2, 144, 2], F32); r_MV = P.res()
        DV = P.sbuf("DV", [128, 24, 16], F32)
        VT = P.sbuf("VT", [128, NVT, 16], F32)
        chm = P.sbuf("chm_s", [128, 2, 2, 8], F32)
        r_vec = r_MV
        s_vt = P.dsem("s_vt"); s_chm = P.dsem("s_chm")
        P.dma("sp", lambda e: e.dma_start(out=VT[:], in_=vtab_d), s_vt, writes=[r_vec])
        P.dma("sp", lambda e: e.dma_start(out=chm[:], in_=chm_d), s_chm, writes=[r_vec])
        r_mvb = P.res(); r_mvg = P.res()
        emit_mod_shard(C, P, A, cc, mw, mbias, mvb, r_mvb)
        s_cc0 = P.csem("s_cc0"); s_mvl = P.dsem("s_mvl")
        P.dma("pool", lambda e: e.collective_compute("AllGather", ALU.bypass, replica_groups=[list(range(8))], ins=[mvb.opt()], outs=[mvg.opt()]),
              s_cc0, reads=[r_mvb], writes=[r_mvg], inc=1)
        for k in range(8):
            P.dma("sp", lambda e, k=k: e.dma_start(out=MVc[:, :, k * NQ:(k + 1) * NQ, :],
                                                   in_=mvg[k * 128:(k + 1) * 128, :].rearrange("p (i q r) -> p i q r", i=2, q=NQ)),
                  s_mvl, reads=[r_mvg], writes=[r_vec])
        mvrow = lambda i, s, j, r: MVc[:, i, (s * 3 + j) * 16:(s * 3 + j + 1) * 16, r]
        mvcol = lambda i, s, j, r: (lambda c: MVc[:, i, (s * 3 + j) * 16 + c:(s * 3 + j) * 16 + c + 1, r])
        dvcol = lambda k: (lambda c: DV[:, k, c:c + 1])
        vtcol = lambda k: (lambda c: VT[:, k, c:c + 1])
        for i in range(2):
            for s in range(3):
                u = i * 3 + s
                mul = (0.5 / ALPHA) if s != 1 else (1.0 / ALPHA)
                for r in range(2):
                    P.op("dve", lambda e, i=i, s=s, r=r, u=u: e.tensor_scalar(out=DV[:, u * 4 + r, :], in0=mvrow(i, s, 1, r), scalar1=1.0, scalar2=None, op0=ALU.add),
                         reads=[r_vec], writes=[r_vec])
                    P.op("dve", lambda e, i=i, s=s, r=r, u=u, mul=mul: e.tensor_scalar(out=DV[:, u * 4 + 2 + r, :], in0=mvrow(i, s, 2, r), scalar1=mul, scalar2=None, op0=ALU.mult),
                         reads=[r_vec], writes=[r_vec])
        for r in range(2):
            P.op("dve", lambda e, r=r: e.tensor_tensor(out=DV[:, 1 * 4 + 2 + r, :], in0=DV[:, 1 * 4 + 2 + r, :], in1=VT[:, VT_PS, :], op=ALU.mult), reads=[r_vec], writes=[r_vec])
        P.op("dve", lambda e: e.tensor_tensor(out=VT[:, VT_LB, :], in0=VT[:, VT_LBRAW + 1, :], in1=VT[:, VT_LBRAW, :], op=ALU.subtract), reads=[r_vec], writes=[r_vec])
        P.op("dve", lambda e: e.tensor_tensor(out=VT[:, VT_LB + 1, :], in0=VT[:, VT_LBRAW + 3, :], in1=VT[:, VT_LBRAW + 2, :], op=ALU.subtract), reads=[r_vec], writes=[r_vec])
        P.op("act", lambda e: e.activation(out=VT[:, VT_LB:VT_LB + 2, :], in_=VT[:, VT_LB:VT_LB + 2, :], func=AF.Sigmoid), reads=[r_vec], writes=[r_vec])
        P.op("dve", lambda e: e.tensor_scalar(out=VT[:, VT_OMLB:VT_OMLB + 2, :], in0=VT[:, VT_LB:VT_LB + 2, :], scalar1=-1.0, scalar2=1.0, op0=ALU.mult, op1=ALU.add),
             reads=[r_vec], writes=[r_vec])
        P.end_phase()

        def ffn_phase(xin, xo, tiles, i, s, pre=None):
            A.reset()
            setup_ffn(C, A)
            xs = A.alloc([16, 512], F32); r_xs = [P.res() for _ in range(16)]
            s_x = P.dsem("s_x"); s_o = P.dsem("s_o")
            u = i * 3 + s
            fi = 0 if s == 0 else 1
            for (t0, T, ms) in tiles:
                P.dma("sp", lambda e, t0=t0, T=T: e.dma_start(out=xs[:, :, :T], in_=xin[:, :, t0:t0 + T]), s_x, writes=r_xs)
                if pre is not None:
                    emit_ln(C, xs, r_xs, T, vtcol(VT_LNG + pre), vtcol(VT_LNB + pre), r_vec)
                emit_ffn(C, xs, r_xs, T, ffn_w_in[i, fi], ffn_w_out[i, fi], dvcol(u * 4 + ms), mvcol(i, s, 0, ms), dvcol(u * 4 + 2 + ms), r_vec)
                emit_ln(C, xs, r_xs, T, vtcol(VT_LNG + u), vtcol(VT_LNB + u), r_vec)
                P.dma("sp", lambda e, t0=t0, T=T: e.dma_start(out=xo[:, :, t0:t0 + T], in_=xs[:, :, :T]), s_o, reads=r_xs)
            P.end_phase()
            return s_o

        TA = [(i * 512, 512, 0) for i in range(5)] + [(2560, 448, 0), (3008, 256, 1)]
        TB = [(i * 512, 512, 0) for i in range(4)] + [(2048, 256, 1)]
        TD = [(i * 512, 512, 0) for i in range(4)]
        ffn_phase(xslab, x1, TA, 0, 0)
        A.reset()
        emit_pool_slab(C, A, x1[:, :, 0:3008], zz[:, :, 0:2048], 47, 64, 8, 32, True, valid, inv, pool_w, dvcol(4 + 0), mvcol(0, 1, 0, 0), dvcol(4 + 2), r_vec, "a")
        P.end_phase()
        A.reset()
        emit_pool_slab(C, A, x1[:, :, 3008:3264], zz[:, :, 2048:2304], 1, 256, 0, 1, False, None, invc, pool_w, dvcol(4 + 1), mvcol(0, 1, 0, 1), dvcol(4 + 3), r_vec, "c")
        P.end_phase()
        ffn_phase(zz, x2, TB, 0, 2, pre=1)
        ffn_phase(x2, x3, TB, 1, 0)
        A.reset()
        H = setup_hgrn1(C, A)
        s_m = P.dsem("s_m")
        for d in range(2):
            P.dma("sp", lambda e, d=d: e.dma_start(out=H.mask[d][:], in_=maskd[d]), s_m, writes=[H.r_mask])
        lbcol = lambda d, h: VT[:, VT_LB + d, h:h + 1]
        omlbcol = lambda d, h: VT[:, VT_OMLB + d, h:h + 1]
        segs = [(i * 512, 512, 0, True, i) for i in range(4)] + [(2048, 256, 1, False, 4)]
        r_cb = P.res(); r_cg = P.res()
        for (t0, T, ms, need_out, si) in segs:
            emit_fill_xm_dram(C, H, x3, t0, T, dvcol(4 * 4 + ms), mvcol(1, 1, 0, ms), r_vec)
            outs = {
                "sg": lambda h, T, t0=t0: sg[:, h, t0:t0 + T],
                "ol": lambda h, T, t0=t0: ol[:, h, t0:t0 + T],
                "qD": lambda d, h, T, t0=t0: qD[d, :, h, t0:t0 + T],
                "segS": lambda d, seg: segS[d, seg],
                "segD": lambda seg: segD[seg],
                "coreS": lambda d: cb[:, d * 2048:(d + 1) * 2048].rearrange("p (h v) -> p h v", v=128),
                "coreD": lambda: cb[:, 4096:4128].rearrange("p (d h) -> p d h", d=2),
            }
            emit_hgrn_local(C, H, T, hgrn_w_in, lbcol, omlbcol, r_vec, outs, si, need_out, first_seg=(si == 0), last_seg=(si == 3))
        P.end_phase()
        s_cc1 = P.csem("s_cc1")
        P.dma("pool", lambda e: e.collective_compute("AllGather", ALU.bypass, replica_groups=[list(range(8))], ins=[cb.opt()], outs=[cg.opt()]),
              s_cc1, reads=[r_cb], writes=[r_cg], inc=1)
        P.end_phase()
        A.reset()
        G = setup_hgrn2(C, A)
        emit_hgrn_combine_masked(C, G, lambda d, i: segS[d, i], segD,
                                 lambda d, r: cg[r * 128:(r + 1) * 128, d * 2048:(d + 1) * 2048].rearrange("p (h v) -> p h v", v=128),
                                 lambda d, r: cg[r * 128:(r + 1) * 128, 4096 + d * 16:4096 + (d + 1) * 16], chm, r_vec)
        for j in range(4):
            emit_hgrn_final(C, G, j, 512, j * 512, x3, ol, sg, qD, hgrn_w_out, x4, dvcol(4 * 4 + 2), lambda: VT[:, VT_NG, 0:1],
                            vtcol(VT_LNG + 4), vtcol(VT_LNB + 4), r_vec)
        P.end_phase()
        s_fin = ffn_phase(x4, xout, TD, 1, 2)
        P.emit(final_waits={"sp": [(s_fin.h, s_fin.count)]})
        build_fused_program.stats = (P.nsig, P._nsem, max(d.count for d in P._sem_free + P._sem_phase))
    return nc


def emit_hgrn_combine_masked(C, G, segS, segD, chainS, chainD, chm, r_vec, nlat=4, ictx=4):
    P = C.P
    P.dma("sp", lambda e: e.dma_start(out=G.segD[:], in_=segD.rearrange("s p d h -> p s d h")), G.s_segD, writes=[G.r_segD])

    def load(ap):
        i = G.ld_i % 2; G.ld_i += 1
        P.dma("sp", lambda e, i=i, ap=ap: e.dma_start(out=G.ld[i][:], in_=ap), G.s_ld[i], writes=[G.r_ld[i]])
        return G.ld[i], G.r_ld[i]

    for d in range(2):
        t, r_t = load(segS(d, ictx))
        P.op("dve", lambda e, t=t: e.tensor_copy(out=G.Sw[:], in_=t[:]), reads=[r_t], writes=[G.r_Sw])
        ranks = list(range(8)) if d == 0 else list(range(7, -1, -1))
        for n, r in enumerate(ranks):
            t, r_t = load(chainS(d, r))
            dd, r_dd, s_dd = G.ldD[n % 2], G.r_ldD[n % 2], G.s_ldD[n % 2]
            P.dma("sp", lambda e, d=d, r=r, dd=dd: e.dma_start(out=dd[:], in_=chainD(d, r)), s_dd, writes=[r_dd])
            P.op("dve", lambda e, d=d, r=r, dd=dd: e.tensor_scalar(out=dd[:], in0=dd[:], scalar1=chm[:, d, 0, r:r + 1], scalar2=chm[:, d, 1, r:r + 1],
                                                                   op0=ALU.mult, op1=ALU.add), reads=[r_dd, r_vec], writes=[r_dd])
            P.op("dve", lambda e, d=d, r=r, t=t: e.tensor_scalar(out=t[:], in0=t[:], scalar1=chm[:, d, 0, r:r + 1], scalar2=None, op0=ALU.mult),
                 reads=[r_t, r_vec], writes=[r_t])
            for h in range(16):
                P.op("dve", lambda e, h=h, t=t, dd=dd: e.scalar_tensor_tensor(out=G.Sw[:, h, :], in0=G.Sw[:, h, :], scalar=dd[:, h:h + 1], in1=t[:, h, :],
                                                                              op0=ALU.mult, op1=ALU.add), reads=[r_t, r_dd, G.r_Sw], writes=[G.r_Sw])
        order = list(range(nlat)) if d == 0 else list(range(nlat - 1, -1, -1))
        for n, j in enumerate(order):
            P.op("act", lambda e, d=d, j=j: e.activation(out=G.SinB[d][j][:], in_=G.Sw[:], func=AF.Copy), reads=[G.r_Sw], writes=[G.r_SinB[d][j]])
            if n < nlat - 1:
                t, r_t = load(segS(d, j))
                for h in range(16):
                    P.op("dve", lambda e, h=h, t=t, j=j, d=d: e.scalar_tensor_tensor(out=G.Sw[:, h, :], in0=G.Sw[:, h, :], scalar=G.segD[:, j, d, h:h + 1], in1=t[:, h, :],
                                                                                    op0=ALU.mult, op1=ALU.add), reads=[r_t, G.r_segD, G.r_Sw], writes=[G.r_Sw])


def chain_masks(core):
    m = np.zeros((128, 2, 2, 8), np.float32)
    for r in range(8):
        m[:, 0, 0, r] = 1.0 if r < core else 0.0
        m[:, 1, 0, r] = 1.0 if r > core else 0.0
    m[:, :, 1, :] = 1.0 - m[:, :, 0, :]
    return m


def kernel(x, c, ctx, c_ctx, mod_w, mod_b, ln_g, ln_b, ffn_w_in, ffn_w_out, pool_w, pool_scale,
           hgrn_w_in, hgrn_lb, hgrn_norm_g, hgrn_w_out):
    f32 = lambda a: np.ascontiguousarray(np.asarray(a, np.float32))
    x = f32(x)[0]; ctx = f32(ctx)[0]; c = f32(c); c_ctx = f32(c_ctx)
    mod_w = np.asarray(mod_w, np.float32); mod_b = f32(mod_b); ln_g = f32(ln_g); ln_b = f32(ln_b)
    ffn_w_in = f32(ffn_w_in); ffn_w_out = f32(ffn_w_out)
    pool_w = f32(pool_w)[0]; pool_scale = f32(pool_scale)[0]
    hgrn_w_in = f32(hgrn_w_in)[0]; hgrn_lb = f32(hgrn_lb); hgrn_norm_g = f32(hgrn_norm_g)[0]; hgrn_w_out = f32(hgrn_w_out)[0]
    cores = list(range(NCORES))
    cc = np.stack([c.reshape(16, 128), c_ctx.reshape(16, 128)])
    vt = np.zeros((128, NVT, 16), np.float32)
    for i in range(2):
        for s in range(3):
            vt[:, VT_LNG + i * 3 + s] = vrow(ln_g[i, s]); vt[:, VT_LNB + i * 3 + s] = vrow(ln_b[i, s])
    vt[:, VT_PS] = vrow(pool_scale)
    vt[:, VT_LBRAW + 0] = vrow(hgrn_lb[0, 0]); vt[:, VT_LBRAW + 1] = vrow(hgrn_lb[0, 1])
    vt[:, VT_LBRAW + 2] = vrow(hgrn_lb[1, 0]); vt[:, VT_LBRAW + 3] = vrow(hgrn_lb[1, 1])
    vt[:, VT_NG, 0] = hgrn_norm_g
    mf, mb = hgrn_masks()
    masks = np.stack([mf, mb])
    ins = []
    for k in cores:
        slab = np.zeros((3008 + 256, D), np.float32)
        lo, hi = 2048 * k - 512, 2048 * k + 2048 + 448
        a, b = max(lo, 0), min(hi, 16384)
        slab[a - lo:b - lo] = x[a:b]
        slab[3008:] = ctx
        valid, inv, invc = pool_consts(k)
        ins.append({
            "xslab": to_fm(slab), "cc": cc,
            "mw": np.ascontiguousarray(mod_w[:, :, k * 2304:(k + 1) * 2304]),
            "mb": np.ascontiguousarray(mod_b[:, k * 2304:(k + 1) * 2304].reshape(36, 128)),
            "vtab": vt, "ffn_w_in": ffn_w_in, "ffn_w_out": ffn_w_out, "pool_w": pool_w,
            "valid": valid, "inv": inv, "invc": invc, "hgrn_w_in": hgrn_w_in, "hgrn_w_out": hgrn_w_out,
            "masks": masks, "chm": chain_masks(k),
        })
    r = _run(_prog("fused", build_fused_program), ins)
    out = np.concatenate([from_fm(rk["xout"]) for rk in r], axis=0)
    return np.ascontiguousarray(out[None].astype(np.float32))
```

```python
import numpy as np
import concourse.bass as bass
import concourse.mybir as mybir

F32 = mybir.dt.float32
BF16 = mybir.dt.bfloat16
AF = mybir.ActivationFunctionType
ALU = mybir.AluOpType
AX = mybir.AxisListType

ENGS = ("pe", "act", "dve", "pool", "sp")


class Res:
    __slots__ = ("name", "w", "r")

    def __init__(self, name):
        self.name = name
        self.w = None
        self.r = {}


class Op:
    __slots__ = ("eng", "fn", "deps", "is_dma", "dsem", "dval", "sig", "idx", "inc")

    def __init__(self, eng, fn, is_dma=False):
        self.eng = eng
        self.fn = fn
        self.deps = []
        self.is_dma = is_dma
        self.dsem = None
        self.dval = 0
        self.sig = 0
        self.idx = 0


class DmaSem:
    def __init__(self, handle):
        self.h = handle
        self.count = 0


class Prog:
    def __init__(self, nc, stack):
        self.nc = nc
        self.stack = stack
        self.streams = {e: [] for e in ENGS}
        self.nres = 0
        self.esem = {}
        for e in ENGS:
            self.esem[e] = stack.enter_context(nc.semaphore("es_" + e))
        self._dma_ops_since_barrier = []
        self._barrier_deps = None
        self._sem_free = []
        self._sem_phase = []
        self._nsem = 0

    def sbuf(self, name, shape, dt):
        return self.stack.enter_context(self.nc.sbuf_tensor(name, list(shape), dt))

    def psum(self, name, shape, dt=F32):
        return self.stack.enter_context(self.nc.psum_tensor(name, list(shape), dt))

    def dsem(self, name):
        if self._sem_free:
            d = self._sem_free.pop()
        else:
            self._nsem += 1
            d = DmaSem(self.stack.enter_context(self.nc.semaphore("ds%d" % self._nsem)))
        self._sem_phase.append(d)
        return d

    def csem(self, name):
        return DmaSem(self.stack.enter_context(self.nc.semaphore(name)))

    def end_phase(self):
        self.barrier()
        self._sem_free.extend(self._sem_phase)
        self._sem_phase = []

    def res(self, name="r"):
        self.nres += 1
        return Res(name)

    def _track(self, o, reads, writes):
        deps = []
        for r in reads:
            if r.w is not None:
                deps.append(r.w)
        for w in writes:
            if w.w is not None:
                deps.append(w.w)
            deps.extend(w.r.values())
        o.deps = deps
        for r in reads:
            r.r[(id(o.dsem) if o.is_dma else o.eng)] = o
        for w in writes:
            w.w = o
            w.r = {}
        st = self.streams[o.eng]
        o.idx = len(st)
        st.append(o)
        return o

    def op(self, eng, fn, reads=(), writes=()):
        return self._track(Op(eng, fn), reads, writes)

    def dma(self, eng, fn, sem, reads=(), writes=(), inc=16):
        o = Op(eng, fn, is_dma=True)
        o.inc = inc
        sem.count += inc
        o.dsem = sem
        o.dval = sem.count
        return self._track(o, reads, writes)

    def emit(self, final_waits=()):
        nc = self.nc
        for e in ENGS:
            for o in self.streams[e]:
                for d in o.deps:
                    if not d.is_dma:
                        if d.eng == e and e in ("pe", "sp"):
                            continue
                        d.sig = -1
        self.nsig = {}
        for e in ENGS:
            n = 0
            for o in self.streams[e]:
                if o.sig == -1:
                    n += 1
                    o.sig = n
            self.nsig[e] = (n, len(self.streams[e]))
        streams = self.streams
        esem = self.esem

        def run(engname, engobj):
            known = {}
            for o in streams[engname]:
                waits = {}
                for d in o.deps:
                    if d.is_dma:
                        key = id(d.dsem)
                        h, v = d.dsem.h, d.dval
                    else:
                        if d.eng == engname and engname in ("pe", "sp"):
                            continue
                        key = d.eng
                        h, v = esem[d.eng], d.sig
                    if known.get(key, 0) >= v:
                        continue
                    if key not in waits or waits[key][1] < v:
                        waits[key] = (h, v)
                for key, (h, v) in waits.items():
                    engobj.wait_ge(h, v)
                    known[key] = v
                ins = o.fn(engobj)
                if o.is_dma:
                    ins.then_inc(o.dsem.h, o.inc)
                elif o.sig:
                    ins.then_inc(esem[engname], 1)
            for (h, v) in final_waits.get(engname, []) if isinstance(final_waits, dict) else []:
                engobj.wait_ge(h, v)

        with nc.Block() as block:
            @block.tensor
            def _(t):
                run("pe", t)

            @block.scalar
            def _(s):
                run("act", s)

            @block.vector
            def _(v):
                run("dve", v)

            @block.gpsimd
            def _(g):
                run("pool", g)

            @block.sync
            def _(s):
                run("sp", s)


class Arena:
    def __init__(self, P, nbytes):
        self.words = nbytes // 4
        self.t = P.sbuf("arena", [128, self.words], F32)
        self.off = 0

    def reset(self, to=0):
        self.off = to

    def mark(self):
        return self.off

    def alloc(self, shape, dt):
        n = 1
        for s in shape:
            n *= s
        if dt == BF16:
            w = (n + 1) // 2
        else:
            w = n
        w = (w + 7) // 8 * 8
        assert self.off + w <= self.words, ("arena overflow", self.off, w, self.words)
        v = self.t[:, self.off:self.off + w]
        self.off += w
        if dt == BF16:
            v = v.bitcast(BF16)[:, 0:n]
        else:
            v = v[:, 0:n]
        if len(shape) == 2:
            return v.rearrange("p (a b) -> p a b", b=shape[1])
        if len(shape) == 3:
            return v.rearrange("p (a b c) -> p a b c", b=shape[1], c=shape[2])
        return v


def _barrier(self):
    last = []
    for e in ENGS:
        if self.streams[e]:
            last.append(self.streams[e][-1])
    last.extend(self._dma_ops_since_barrier)
    self._dma_ops_since_barrier = []
    self._barrier_deps = {e: list(last) for e in ENGS}


def _track2(self, o, reads, writes):
    r = Prog._track_orig(self, o, reads, writes)
    bd = getattr(self, "_barrier_deps", None)
    if bd and bd.get(o.eng):
        o.deps = list(o.deps) + [d for d in bd[o.eng] if d is not o]
        bd[o.eng] = None
    if o.is_dma:
        self._dma_ops_since_barrier.append(o)
    return r


Prog._track_orig = Prog._track
Prog._track = _track2
Prog.barrier = _barrier


import numpy as np
from contextlib import ExitStack

D = 2048
NC16 = 16
DFF = 5632
NFC = 44
ALPHA = 4 ** 0.25
LN_EPS = 1e-5
RMS_EPS = 1e-6
EPS_P = LN_EPS / (ALPHA * ALPHA)


class Ctx:
    pass


def setup_common(P, TMAX=512):
    C = Ctx()
    C.P = P
    C.ones = P.sbuf("ones", [128, 128], F32); C.r_ones = P.res()
    C.ident = P.sbuf("ident", [128, 128], F32); C.r_ident = P.res()
    C.identb = P.sbuf("identb", [128, 128], BF16); C.r_identb = P.res()
    P.op("dve", lambda e: e.memset(C.ones[:], 1.0), writes=[C.r_ones])
    P.op("dve", lambda e: e.memset(C.ident[:], 0.0), writes=[C.r_ident])
    P.op("pool", lambda e: e.affine_select(out=C.ident[:], in_=C.ident[:], compare_op=ALU.not_equal, fill=1.0,
                                           base=0, pattern=[[-1, 128]], channel_multiplier=1),
         reads=[C.r_ident], writes=[C.r_ident])
    P.op("dve", lambda e: e.tensor_copy(out=C.identb[:], in_=C.ident[:]), reads=[C.r_ident], writes=[C.r_identb])
    C.pb = [P.psum("pb%d" % i, [128, 512]) for i in range(8)]
    C.r_pb = [P.res("pb%d" % i) for i in range(8)]
    C.sq = [P.sbuf("sq%d" % i, [128, TMAX], F32) for i in range(2)]
    C.r_sq = [P.res() for i in range(2)]
    C.st = [P.sbuf("st%d" % i, [128, TMAX], F32) for i in range(4)]
    C.r_st = [P.res() for i in range(4)]
    return C


def emit_ln(C, xs, r_xs, T, gcol, bcol, r_vec, eps=EPS_P, bank0=0):
    P = C.P
    pm, pq = C.pb[bank0], C.pb[bank0 + 1]
    r_pm, r_pq = C.r_pb[bank0], C.r_pb[bank0 + 1]
    for c in range(16):
        sq, r_sq = C.sq[c % 2], C.r_sq[c % 2]
        P.op("act", lambda e, c=c, sq=sq: e.activation(out=sq[:, :T], in_=xs[:, c, :T], func=AF.Square),
             reads=[r_xs[c]], writes=[r_sq])
        P.op("pe", lambda e, c=c: e.matmul(pm[:, :T], lhsT=C.ones[:], rhs=xs[:, c, :T], start=(c == 0), stop=(c == 15)),
             reads=[C.r_ones, r_xs[c]], writes=[r_pm])
        P.op("pe", lambda e, c=c, sq=sq: e.matmul(pq[:, :T], lhsT=C.ones[:], rhs=sq[:, :T], start=(c == 0), stop=(c == 15)),
             reads=[C.r_ones, r_sq], writes=[r_pq])
    m, msq, var, rstd = C.st
    r_m, r_msq, r_var, r_rstd = C.r_st
    P.op("act", lambda e: e.activation(out=m[:, :T], in_=pm[:, :T], func=AF.Copy, scale=1.0 / D), reads=[r_pm], writes=[r_m])
    P.op("dve", lambda e: e.tensor_tensor(out=msq[:, :T], in0=m[:, :T], in1=m[:, :T], op=ALU.mult), reads=[r_m], writes=[r_msq])
    P.op("dve", lambda e: e.scalar_tensor_tensor(out=var[:, :T], in0=pq[:, :T], scalar=1.0 / D, in1=msq[:, :T],
                                                 op0=ALU.mult, op1=ALU.subtract), reads=[r_pq, r_msq], writes=[r_var])
    P.op("dve", lambda e: e.tensor_scalar(out=var[:, :T], in0=var[:, :T], scalar1=eps, scalar2=None,
                                          op0=ALU.add), reads=[r_var], writes=[r_var])
    P.op("act", lambda e: e.activation(out=var[:, :T], in_=var[:, :T], func=AF.Sqrt), reads=[r_var], writes=[r_var])
    P.op("dve", lambda e: e.reciprocal(out=rstd[:, :T], in_=var[:, :T]), reads=[r_var], writes=[r_rstd])
    for c in range(16):
        P.op("dve", lambda e, c=c: e.tensor_tensor(out=xs[:, c, :T], in0=xs[:, c, :T], in1=m[:, :T], op=ALU.subtract),
             reads=[r_xs[c], r_m], writes=[r_xs[c]])
        P.op("dve", lambda e, c=c: e.tensor_tensor(out=xs[:, c, :T], in0=xs[:, c, :T], in1=rstd[:, :T], op=ALU.mult),
             reads=[r_xs[c], r_rstd], writes=[r_xs[c]])
        P.op("act", lambda e, c=c: e.activation(out=xs[:, c, :T], in_=xs[:, c, :T], func=AF.Identity,
                                                bias=bcol(c), scale=gcol(c)),
             reads=[r_xs[c], r_vec], writes=[r_xs[c]])


def setup_ffn(C, A=None):
    P = C.P
    if A is None:
        alloc = lambda name, shape, dt: P.sbuf(name, [128] + list(shape), dt)
    else:
        alloc = lambda name, shape, dt: A.alloc(list(shape), dt)
    C.xm = alloc("xm", [16, 512], BF16); C.r_xm = [P.res() for _ in range(16)]
    C.g = alloc("g", [NFC, 512], BF16); C.r_g = [P.res() for _ in range(NFC)]
    C.NWIN = 3
    C.FCB = 2
    C.win = [alloc("win%d" % i, [16, 2, C.FCB * 128], BF16) for i in range(C.NWIN)]
    C.r_win = [P.res() for _ in range(C.NWIN)]
    C.s_win = [P.dsem("s_win%d" % i) for i in range(C.NWIN)]
    C.NWOUT = 2
    C.WOB = 11
    C.wout = [alloc("wout%d" % i, [C.WOB, 512], BF16) for i in range(C.NWOUT)]
    C.r_wout = [P.res() for _ in range(C.NWOUT)]
    C.s_wout = [P.dsem("s_wout%d" % i) for i in range(C.NWOUT)]
    C.silu = [alloc("silu%d" % i, [512], F32) for i in range(2)]
    C.r_silu = [P.res() for _ in range(2)]
    C.win_i = 0
    C.wout_i = 0
    C.silu_i = 0


def emit_ffn(C, xs, r_xs, T, w_in, w_out, opscol, shcol, gpcol, r_vec):
    P = C.P
    xm, g = C.xm, C.g
    for c in range(16):
        P.op("act", lambda e, c=c: e.activation(out=xm[:, c, :T], in_=xs[:, c, :T], func=AF.Identity,
                                                bias=shcol(c), scale=opscol(c)),
             reads=[r_xs[c], r_vec], writes=[C.r_xm[c]])
    w_in_v = w_in.rearrange("(c p) f -> p c f", p=128)
    FCB = C.FCB
    for fb in range(NFC // FCB):
        si = C.win_i % C.NWIN; C.win_i += 1
        ws, r_ws, s_ws = C.win[si], C.r_win[si], C.s_win[si]
        for au in range(2):
            c0 = au * DFF + fb * FCB * 128
            P.dma("pool", lambda e, ws=ws, au=au, c0=c0: e.dma_start(out=ws[:, :, au, :], in_=w_in_v[:, :, c0:c0 + FCB * 128]),
                  s_ws, writes=[r_ws])
        for j in range(FCB):
            fc = fb * FCB + j
            pa, r_pa = C.pb[(fc % 2) * 2], C.r_pb[(fc % 2) * 2]
            pu, r_pu = C.pb[(fc % 2) * 2 + 1], C.r_pb[(fc % 2) * 2 + 1]
            for c in range(16):
                P.op("pe", lambda e, c=c, ws=ws, j=j, pa=pa: e.matmul(pa[:, :T], lhsT=ws[:, c, 0, j * 128:(j + 1) * 128],
                                                                       rhs=xm[:, c, :T], start=(c == 0), stop=(c == 15)),
                     reads=[r_ws, C.r_xm[c]], writes=[r_pa])
            for c in range(16):
                P.op("pe", lambda e, c=c, ws=ws, j=j, pu=pu: e.matmul(pu[:, :T], lhsT=ws[:, c, 1, j * 128:(j + 1) * 128],
                                                                       rhs=xm[:, c, :T], start=(c == 0), stop=(c == 15)),
                     reads=[r_ws, C.r_xm[c]], writes=[r_pu])
            sl = C.silu_i % 2; C.silu_i += 1
            sb, r_sb = C.silu[sl], C.r_silu[sl]
            P.op("act", lambda e, sb=sb, pa=pa: e.activation(out=sb[:, :T], in_=pa[:, :T], func=AF.Silu),
                 reads=[r_pa], writes=[r_sb])
            P.op("dve", lambda e, sb=sb, pu=pu, fc=fc: e.tensor_tensor(out=g[:, fc, :T], in0=sb[:, :T], in1=pu[:, :T], op=ALU.mult),
                 reads=[r_sb, r_pu], writes=[C.r_g[fc]])
    w_out_v = w_out.rearrange("(f p) d -> p f d", p=128)
    WOB = C.WOB
    for db in range(4):
        for fb in range(NFC // WOB):
            si = C.wout_i % C.NWOUT; C.wout_i += 1
            ws, r_ws, s_ws = C.wout[si], C.r_wout[si], C.s_wout[si]
            P.dma("pool", lambda e, ws=ws, fb=fb, db=db: e.dma_start(out=ws[:], in_=w_out_v[:, fb * WOB:(fb + 1) * WOB, db * 512:(db + 1) * 512]),
                  s_ws, writes=[r_ws])
            for j in range(WOB):
                fc = fb * WOB + j
                for dci in range(4):
                    py, r_py = C.pb[4 + dci], C.r_pb[4 + dci]
                    P.op("pe", lambda e, ws=ws, j=j, dci=dci, fc=fc, py=py: e.matmul(py[:, :T], lhsT=ws[:, j, dci * 128:(dci + 1) * 128],
                                                                                     rhs=g[:, fc, :T], start=(fc == 0), stop=(fc == NFC - 1)),
                         reads=[r_ws, C.r_g[fc]], writes=[r_py])
        for dci in range(4):
            c = db * 4 + dci
            py, r_py = C.pb[4 + dci], C.r_pb[4 + dci]
            P.op("dve", lambda e, c=c, py=py: e.scalar_tensor_tensor(out=xs[:, c, :T], in0=py[:, :T], scalar=gpcol(c), in1=xs[:, c, :T],
                                                                     op0=ALU.mult, op1=ALU.add),
                 reads=[r_py, r_xs[c], r_vec], writes=[r_xs[c]])


V_SC, V_SH, V_GT = 0, 1, 2
V_LNG, V_LNB, V_PLG, V_PLB = 6, 7, 8, 9
V_OPS, V_GP = 10, 12
NVROW = 14


def build_ffn_program(tiles, NT, pre_ln=False, gate_mul=0.5 / ALPHA):
    nc = bass.Bass("TRN2", target_bir_lowering=False)
    xin = nc.dram_tensor("xin", [128, 16, NT], F32, kind="ExternalInput").ap()
    vec = nc.dram_tensor("vec", [128, NVROW, 16], F32, kind="ExternalInput").ap()
    w_in = nc.dram_tensor("w_in", [D, 2 * DFF], F32, kind="ExternalInput").ap()
    w_out = nc.dram_tensor("w_out", [DFF, D], F32, kind="ExternalInput").ap()
    xout = nc.dram_tensor("xout", [128, 16, NT], F32, kind="ExternalOutput").ap()
    with ExitStack() as st:
        P = Prog(nc, st)
        C = setup_common(P)
        setup_ffn(C)
        V = P.sbuf("V", [128, NVROW, 16], F32); r_V = P.res()
        s_v = P.dsem("s_v"); s_x = P.dsem("s_x"); s_o = P.dsem("s_o")
        P.dma("sp", lambda e: e.dma_start(out=V[:], in_=vec), s_v, writes=[r_V])
        for ms in range(2):
            P.op("dve", lambda e, ms=ms: e.tensor_scalar(out=V[:, V_OPS + ms, :], in0=V[:, V_SC + 3 * ms, :], scalar1=1.0, scalar2=None,
                                                         op0=ALU.add), reads=[r_V], writes=[r_V])
            P.op("dve", lambda e, ms=ms: e.tensor_scalar(out=V[:, V_GP + ms, :], in0=V[:, V_GT + 3 * ms, :], scalar1=gate_mul, scalar2=None,
                                                         op0=ALU.mult), reads=[r_V], writes=[r_V])
        xs = P.sbuf("xs", [128, 16, 512], F32); r_xs = [P.res() for _ in range(16)]
        col = lambda k: (lambda c: V[:, k, c:c + 1])
        for (t0, T, ms) in tiles:
            P.dma("sp", lambda e, t0=t0, T=T: e.dma_start(out=xs[:, :, :T], in_=xin[:, :, t0:t0 + T]), s_x, writes=r_xs)
            if pre_ln:
                emit_ln(C, xs, r_xs, T, col(V_PLG), col(V_PLB), r_V)
            emit_ffn(C, xs, r_xs, T, w_in, w_out, col(V_OPS + ms), col(V_SH + 3 * ms), col(V_GP + ms), r_V)
            emit_ln(C, xs, r_xs, T, col(V_LNG), col(V_LNB), r_V)
            P.dma("sp", lambda e, t0=t0, T=T: e.dma_start(out=xout[:, :, t0:t0 + T], in_=xs[:, :, :T]), s_o, reads=r_xs)
        P.emit(final_waits={"sp": [(s_o.h, s_o.count)]})
    return nc


GRID_W = 64
POOL_WINDOWS = (2, 4, 8, 16)


def emit_pool_slab(C, A, xin, zout, R, W, r0, nown, vertical, valid_d, inv_d, pool_w, opscol, shcol, combcol, r_vec, tag):
    P = C.P
    PW = W + 16
    N = R * PW
    NO = nown * W
    mark = A.mark()
    xg = A.alloc([4, R * W], F32); r_xg = [P.res() for _ in range(4)]
    hpad = A.alloc([R, PW], F32); r_hpad = P.res()
    bufs = [A.alloc([R, PW], F32) for _ in range(2)]; r_bufs = [P.res() for _ in range(2)]
    inv = A.alloc([NO], F32); r_inv = P.res()
    tmp = A.alloc([NO], F32); r_tmp = P.res()
    pbuf = A.alloc([4, NO], BF16); r_pbuf = [P.res() for _ in range(4)]
    pw = A.alloc([4, 512], BF16); r_pw = P.res()
    if valid_d is not None:
        valid = A.alloc([R * W], F32); r_valid = P.res()
    s_in = P.dsem("s_pin" + tag); s_out = P.dsem("s_pout" + tag); s_w = P.dsem("s_pw" + tag)
    s_inv = P.dsem("s_pinv" + tag)
    if valid_d is not None:
        s_val = P.dsem("s_pval" + tag)
        P.dma("sp", lambda e: e.dma_start(out=valid[:], in_=valid_d), s_val, writes=[r_valid])
    P.op("dve", lambda e: e.memset(hpad[:], 0.0), writes=[r_hpad])
    for b in range(2):
        P.op("dve", lambda e, b=b: e.memset(bufs[b][:], 0.0), writes=[r_bufs[b]])
    flat = lambda t: t.rearrange("p r w -> p (r w)")
    TB = 512 if NO >= 512 else NO
    ntb = NO // TB
    for gi, win in enumerate(POOL_WINDOWS):
        hw = win // 2
        P.dma("sp", lambda e, gi=gi: e.dma_start(out=xg[:], in_=xin[:, gi * 4:(gi + 1) * 4, :]), s_in, writes=r_xg)
        P.dma("sp", lambda e, gi=gi: e.dma_start(out=inv[:], in_=inv_d[gi]), s_inv, writes=[r_inv])
        P.dma("pool", lambda e, gi=gi: e.dma_start(out=pw[:], in_=pool_w[gi].rearrange("(i p) o -> p i o", p=128)), s_w, writes=[r_pw])
        for ic in range(4):
            c = gi * 4 + ic
            hd = hpad[:, :, 8:8 + W]
            P.op("act", lambda e, ic=ic, c=c, hd=hd: e.activation(out=hd, in_=xg[:, ic, :].rearrange("p (r w) -> p r w", w=W), func=AF.Identity,
                                                                  bias=shcol(c), scale=opscol(c)),
                 reads=[r_xg[ic], r_vec], writes=[r_hpad])
            if valid_d is not None:
                P.op("dve", lambda e, hd=hd: e.tensor_tensor(out=hd, in0=hd, in1=valid[:].rearrange("p (r w) -> p r w", w=W), op=ALU.mult),
                     reads=[r_hpad, r_valid], writes=[r_hpad])
            src, r_src = hpad, r_hpad
            bi = 0
            shifts = []
            k = 1
            while k < win:
                shifts.append(k); k *= 2
            if vertical:
                k = 1
                while k < win:
                    shifts.append(k * PW); k *= 2
            for k in shifts:
                dst, r_dst = bufs[bi], r_bufs[bi]; bi ^= 1
                P.op("dve", lambda e, src=src, dst=dst, k=k: e.tensor_tensor(out=flat(dst)[:, k:N], in0=flat(src)[:, k:N], in1=flat(src)[:, 0:N - k], op=ALU.add),
                     reads=[r_src], writes=[r_dst])
                src, r_src = dst, r_dst
            ro = r0 + hw - 1 if vertical else r0
            vview = src[:, ro:ro + nown, 8 + hw - 1:8 + hw - 1 + W]
            P.op("dve", lambda e, vview=vview: e.tensor_tensor(out=tmp[:].rearrange("p (r w) -> p r w", w=W), in0=vview,
                                                               in1=inv[:].rearrange("p (r w) -> p r w", w=W), op=ALU.mult),
                 reads=[r_src, r_inv], writes=[r_tmp])
            P.op("dve", lambda e, ic=ic: e.tensor_tensor(out=pbuf[:, ic, :].rearrange("p (r w) -> p r w", w=W), in0=tmp[:].rearrange("p (r w) -> p r w", w=W),
                                                         in1=hpad[:, r0:r0 + nown, 8:8 + W], op=ALU.subtract),
                 reads=[r_tmp, r_hpad], writes=[r_pbuf[ic]])
        for oc in range(4):
            c = gi * 4 + oc
            for tb in range(ntb):
                bk = 4 + (oc * ntb + tb) % 4
                pk, r_pk = C.pb[bk], C.r_pb[bk]
                for ic in range(4):
                    P.op("pe", lambda e, ic=ic, oc=oc, tb=tb, pk=pk: e.matmul(pk[:, :TB], lhsT=pw[:, ic, oc * 128:(oc + 1) * 128],
                                                                              rhs=pbuf[:, ic, tb * TB:(tb + 1) * TB], start=(ic == 0), stop=(ic == 3)),
                         reads=[r_pw, r_pbuf[ic]], writes=[r_pk])
                xo = xg[:, oc, r0 * W + tb * TB: r0 * W + (tb + 1) * TB]
                P.op("dve", lambda e, xo=xo, pk=pk, c=c: e.scalar_tensor_tensor(out=xo, in0=pk[:, :TB], scalar=combcol(c), in1=xo,
                                                                                op0=ALU.mult, op1=ALU.add),
                     reads=[r_pk, r_xg[oc], r_vec], writes=[r_xg[oc]])
        P.dma("sp", lambda e, gi=gi: e.dma_start(out=zout[:, gi * 4:(gi + 1) * 4, :], in_=xg[:, :, r0 * W:(r0 + nown) * W]), s_out, reads=r_xg)
    A.reset(mark)
    return s_out


def pool_consts(core):
    R, W = 47, 64
    rows = 32 * core - 8 + np.arange(R)
    valid = ((rows >= 0) & (rows < 256)).astype(np.float32)
    valid = np.broadcast_to(valid[None, :, None], (128, R, W)).reshape(128, R * W).copy()
    inv = np.zeros((4, 128, 32 * W), np.float32)
    invc = np.zeros((4, 128, 256), np.float32)
    for gi, win in enumerate(POOL_WINDOWS):
        hw = win // 2
        r = 32 * core + np.arange(32)
        cr = np.minimum(r - hw + win, 256) - np.maximum(r - hw, 0)
        c = np.arange(W)
        cc = np.minimum(c - hw + win, W) - np.maximum(c - hw, 0)
        t = (1.0 / (cr[:, None] * cc[None, :]).astype(np.float64)).astype(np.float32).reshape(-1)
        inv[gi] = t[None, :]
        tt = np.arange(256)
        ct = np.minimum(tt - hw + win, 256) - np.maximum(tt - hw, 0)
        invc[gi] = (1.0 / ct.astype(np.float64)).astype(np.float32)[None, :]
    return valid, inv, invc


def build_pool_program():
    nc = bass.Bass("TRN2", target_bir_lowering=False)
    xin = nc.dram_tensor("xin", [128, 16, 3008], F32, kind="ExternalInput").ap()
    xctx = nc.dram_tensor("xctx", [128, 16, 256], F32, kind="ExternalInput").ap()
    vec = nc.dram_tensor("vec", [128, 7, 16], F32, kind="ExternalInput").ap()
    valid = nc.dram_tensor("valid", [128, 3008], F32, kind="ExternalInput").ap()
    inv = nc.dram_tensor("inv", [4, 128, 2048], F32, kind="ExternalInput").ap()
    invc = nc.dram_tensor("invc", [4, 128, 256], F32, kind="ExternalInput").ap()
    pool_w = nc.dram_tensor("pool_w", [4, 512, 512], F32, kind="ExternalInput").ap()
    z = nc.dram_tensor("z", [128, 16, 2048], F32, kind="ExternalOutput").ap()
    zc = nc.dram_tensor("zc", [128, 16, 256], F32, kind="ExternalOutput").ap()
    with ExitStack() as st:
        P = Prog(nc, st)
        C = setup_common(P)
        A = Arena(P, 150 * 1024)
        V = P.sbuf("V", [128, 11, 16], F32); r_V = P.res()
        s_v = P.dsem("s_v")
        P.dma("sp", lambda e: e.dma_start(out=V[:, 0:7, :], in_=vec), s_v, writes=[r_V])
        for ms in range(2):
            P.op("dve", lambda e, ms=ms: e.tensor_scalar(out=V[:, 7 + 2 * ms, :], in0=V[:, 3 * ms, :], scalar1=1.0, scalar2=None, op0=ALU.add), reads=[r_V], writes=[r_V])
            P.op("dve", lambda e, ms=ms: e.scalar_tensor_tensor(out=V[:, 8 + 2 * ms, :], in0=V[:, 2 + 3 * ms, :], scalar=1.0 / ALPHA, in1=V[:, 6, :], op0=ALU.mult, op1=ALU.mult),
                 reads=[r_V], writes=[r_V])
        col = lambda k: (lambda c: V[:, k, c:c + 1])
        s1 = emit_pool_slab(C, A, xin, z, 47, 64, 8, 32, True, valid, inv, pool_w, col(7), col(1), col(8), r_V, "a")
        P.barrier()
        s2 = emit_pool_slab(C, A, xctx, zc, 1, 256, 0, 1, False, None, invc, pool_w, col(9), col(4), col(10), r_V, "c")
        P.emit(final_waits={"sp": [(s1.h, s1.count), (s2.h, s2.count)]})
    return nc


HK = 128
NH = 16


def hgrn_masks():
    p = np.arange(128)
    half, sl = p // 64, p % 64
    j = np.arange(8)[None, :, None]
    t = np.arange(64)[None, None, :]
    ok = (j % 2 == half[:, None, None])
    mf = (ok & (sl[:, None, None] <= t)).astype(np.float32).reshape(128, 512)
    mb = (ok & (sl[:, None, None] >= t)).astype(np.float32).reshape(128, 512)
    return mf, mb


def setup_hgrn1(C, A):
    P = C.P
    H = Ctx()
    H.xm = A.alloc([16, 512], BF16); H.r_xm = [P.res() for _ in range(16)]
    H.vt = A.alloc([4, 2048], BF16); H.r_vt = [P.res() for _ in range(4)]
    H.NW = 6
    H.ws = [A.alloc([16, 256], BF16) for _ in range(H.NW)]; H.r_ws = [P.res() for _ in range(H.NW)]
    H.s_ws = [P.dsem("s_hw%d" % i) for i in range(H.NW)]
    H.wi = 0
    names = ["sg", "lg", "kk", "F", "Fs", "e1", "e2", "t1"]
    H.t = {}; H.r_t = {}
    H.t[("qs", 0)] = A.alloc([512], F32); H.r_t[("qs", 0)] = P.res()
    for d in range(2):
        for n in names:
            H.t[(n, d)] = A.alloc([512], F32); H.r_t[(n, d)] = P.res()
        H.t[("kgf", d)] = H.t[("sg", d)]; H.r_t[("kgf", d)] = H.r_t[("sg", d)]
    H.sgo = A.alloc([512], F32); H.r_sgo = P.res()
    H.osum = A.alloc([512], F32); H.r_osum = P.res()
    for n in ["qg", "kg", "kd", "qD", "At"]:
        for d in range(2):
            H.t[(n, d)] = A.alloc([512], BF16); H.r_t[(n, d)] = P.res()
    H.kdT = [A.alloc([4, 128], BF16) for _ in range(2)]; H.r_kdT = [P.res() for _ in range(2)]
    H.S = [A.alloc([16, 128], F32) for _ in range(2)]; H.r_S = [[P.res() for _ in range(16)] for _ in range(2)]
    H.Sb = [A.alloc([128], BF16) for _ in range(2)]; H.r_Sb = [P.res() for _ in range(2)]
    H.Dsum = A.alloc([2, 16], F32); H.r_Dsum = P.res()
    H.mask = [A.alloc([512], F32) for _ in range(2)]; H.r_mask = P.res()
    H.qDo = [A.alloc([512], BF16) for _ in range(2)]; H.r_qDo = [P.res() for _ in range(2)]
    H.xst = [A.alloc([4, 512], F32) for _ in range(2)]; H.r_xst = [P.res() for _ in range(2)]
    H.s_xst = [P.dsem("s_xst%d" % i) for i in range(2)]; H.xst_i = 0
    H.Sc = [A.alloc([16, 128], F32) for _ in range(2)]; H.r_Sc = [P.res() for _ in range(2)]
    H.Dc = A.alloc([2, 16], F32); H.r_Dc = P.res()
    H.s_sg = P.dsem("s_h1sg"); H.s_ol = P.dsem("s_h1ol"); H.s_q = [P.dsem("s_h1q%d" % d) for d in range(2)]
    H.s_S = [P.dsem("s_h1S%d" % d) for d in range(2)]; H.s_D = P.dsem("s_h1D"); H.s_core = P.dsem("s_h1c")
    return H


def emit_fill_xm_dram(C, H, xin, t0, T, opscol, shcol, r_vec):
    P = C.P
    for q in range(4):
        i = H.xst_i % 2; H.xst_i += 1
        xst, r_xst, s_xst = H.xst[i], H.r_xst[i], H.s_xst[i]
        P.dma("sp", lambda e, q=q, xst=xst: e.dma_start(out=xst[:, :, :T], in_=xin[:, q * 4:(q + 1) * 4, t0:t0 + T]), s_xst, writes=[r_xst])
        for cc in range(4):
            c = q * 4 + cc
            P.op("act", lambda e, c=c, cc=cc, xst=xst: e.activation(out=H.xm[:, c, :T], in_=xst[:, cc, :T], func=AF.Identity, bias=shcol(c), scale=opscol(c)),
                 reads=[r_xst, r_vec], writes=[H.r_xm[c]])


def emit_hgrn_local(C, H, T, w_in, lbcol, omlbcol, r_vec, outs, seg, need_out, first_seg, last_seg):
    P = C.P
    xm = H.xm
    nt = T // 128
    nch = T // 64
    w_v = w_in.rearrange("(c p) f -> p c f", p=128)

    def load_w(col0):
        si = H.wi % H.NW; H.wi += 1
        ws, r_ws, s_ws = H.ws[si], H.r_ws[si], H.s_ws[si]
        P.dma("pool", lambda e: e.dma_start(out=ws[:], in_=w_v[:, :, col0:col0 + 256]), s_ws, writes=[r_ws])
        return ws, r_ws

    for cb in range(8):
        ws, r_ws = load_w(D + cb * 256)
        for tt in range(nt):
            pk, r_pk = C.pb[tt], C.r_pb[tt]
            for c in range(16):
                P.op("pe", lambda e, c=c, tt=tt, ws=ws, pk=pk: e.matmul(pk[:, :256], lhsT=xm[:, c, tt * 128:(tt + 1) * 128], rhs=ws[:, c, :],
                                                                        start=(c == 0), stop=(c == 15)),
                     reads=[H.r_xm[c], r_ws], writes=[r_pk])
            P.op("act", lambda e, tt=tt, cb=cb, pk=pk: e.activation(out=H.vt[:, tt, cb * 256:(cb + 1) * 256], in_=pk[:, :256], func=AF.Copy),
                 reads=[r_pk], writes=[H.r_vt[tt]])
    types = ["q", "af", "ab", "og"] if need_out else ["af", "ab"]
    tcol = {"q": 0, "af": 2 * D, "ab": 3 * D, "og": 4 * D}
    pbank = 0
    for hg in range(8):
        slots = {ty: load_w(tcol[ty] + hg * 256) for ty in types}
        for hh in range(2):
            h = hg * 2 + hh
            pr = {}
            for ty in types:
                bk = pbank % 3; pbank += 1
                pk, r_pk = C.pb[bk], C.r_pb[bk]
                ws, r_ws = slots[ty]
                for c in range(16):
                    P.op("pe", lambda e, c=c, ws=ws, hh=hh, pk=pk: e.matmul(pk[:, :T], lhsT=ws[:, c, hh * 128:(hh + 1) * 128], rhs=xm[:, c, :T],
                                                                            start=(c == 0), stop=(c == 15)),
                         reads=[H.r_xm[c], r_ws], writes=[r_pk])
                pr[ty] = (pk, r_pk)
                if ty == "q":
                    qs, r_qs = H.t[("qs", 0)], H.r_t[("qs", 0)]
                    P.op("act", lambda e, pk=pk, qs=qs: e.activation(out=qs[:, :T], in_=pk[:, :T], func=AF.Silu), reads=[r_pk], writes=[r_qs])
                    P.op("dve", lambda e, qs=qs: e.tensor_scalar(out=qs[:, :T], in0=qs[:, :T], scalar1=HK ** -0.5, scalar2=None, op0=ALU.mult),
                         reads=[r_qs], writes=[r_qs])
                elif ty == "og":
                    P.op("act", lambda e, pk=pk: e.activation(out=H.sgo[:, :T], in_=pk[:, :T], func=AF.Silu), reads=[r_pk], writes=[H.r_sgo])
                    P.dma("sp", lambda e, h=h: e.dma_start(out=outs["sg"](h, T), in_=H.sgo[:, :T]), H.s_sg, reads=[H.r_sgo])
                else:
                    d = 0 if ty == "af" else 1
                    sg, r_sg = H.t[("sg", d)], H.r_t[("sg", d)]
                    P.op("act", lambda e, pk=pk, sg=sg: e.activation(out=sg[:, :T], in_=pk[:, :T], func=AF.Sigmoid), reads=[r_pk], writes=[r_sg])
            qs, r_qs = H.t[("qs", 0)], H.r_t[("qs", 0)]
            for d in range(2):
                tg = lambda n: (H.t[(n, d)], H.r_t[(n, d)])
                sg, r_sg = tg("sg"); lg, r_lg = tg("lg"); kk, r_kk = tg("kk"); F, r_F = tg("F"); Fs, r_Fs = tg("Fs")
                e1, r_e1 = tg("e1"); e2, r_e2 = tg("e2"); kgf, r_kgf = tg("kgf"); t1, r_t1 = tg("t1")
                qg, r_qg = tg("qg"); kg, r_kg = tg("kg"); kd, r_kd = tg("kd"); qD, r_qD = tg("qD"); At, r_At = tg("At")
                P.op("dve", lambda e, sg=sg, d=d, h=h: e.tensor_scalar(out=sg[:, :T], in0=sg[:, :T], scalar1=omlbcol(d, h), scalar2=lbcol(d, h),
                                                                        op0=ALU.mult, op1=ALU.add), reads=[r_sg, r_vec], writes=[r_sg])
                P.op("act", lambda e, sg=sg, lg=lg: e.activation(out=lg[:, :T], in_=sg[:, :T], func=AF.Ln), reads=[r_sg], writes=[r_lg])
                P.op("dve", lambda e, sg=sg, kk=kk: e.tensor_scalar(out=kk[:, :T], in0=sg[:, :T], scalar1=-1.0, scalar2=1.0, op0=ALU.mult, op1=ALU.add),
                     reads=[r_sg], writes=[r_kk])
                for j in range(nch):
                    P.op("dve", lambda e, j=j, F=F, lg=lg: e.tensor_tensor_scan(out=F[:, j * 64:(j + 1) * 64], data0=C.ones[:, 0:64], data1=lg[:, j * 64:(j + 1) * 64],
                                                                               initial=0.0, op0=ALU.mult, op1=ALU.add),
                         reads=[r_lg, C.r_ones], writes=[r_F])
                if need_out:
                    for q4 in range(T // 128):
                        init = 0.0 if q4 == 0 else Fs[:, q4 * 128 - 1:q4 * 128]
                        P.op("dve", lambda e, q4=q4, Fs=Fs, lg=lg, init=init: e.tensor_tensor_scan(out=Fs[:, q4 * 128:(q4 + 1) * 128], data0=C.ones[:, 0:128],
                                                                                                 data1=lg[:, q4 * 128:(q4 + 1) * 128], initial=init, op0=ALU.mult, op1=ALU.add),
                             reads=[r_lg, C.r_ones, r_Fs], writes=[r_Fs])
                F3 = F[:, :T].rearrange("p (j t) -> p j t", t=64)
                if d == 0:
                    P.op("act", lambda e, e1=e1, F=F: e.activation(out=e1[:, :T], in_=F[:, :T], func=AF.Exp), reads=[r_F], writes=[r_e1])
                    P.op("act", lambda e, e2=e2, F=F: e.activation(out=e2[:, :T], in_=F[:, :T], func=AF.Exp, scale=-1.0), reads=[r_F], writes=[r_e2])
                    dec = lambda j, e1=e1: e1[:, j * 64 + 63:j * 64 + 64]
                    decb = e1[:, :T].rearrange("p (j t) -> p j t", t=64)[:, :, 63:64].to_broadcast([128, nch, 64])
                else:
                    P.op("dve", lambda e, t1=t1, F=F, lg=lg: e.tensor_tensor(out=t1[:, :T], in0=F[:, :T], in1=lg[:, :T], op=ALU.subtract), reads=[r_F, r_lg], writes=[r_t1])
                    P.op("dve", lambda e, t1=t1, F3=F3: e.tensor_tensor(out=t1[:, :T].rearrange("p (j t) -> p j t", t=64), in0=t1[:, :T].rearrange("p (j t) -> p j t", t=64),
                                                                        in1=F3[:, :, 63:64].to_broadcast([128, nch, 64]), op=ALU.subtract), reads=[r_t1, r_F], writes=[r_t1])
                    P.op("act", lambda e, e1=e1, t1=t1: e.activation(out=e1[:, :T], in_=t1[:, :T], func=AF.Exp, scale=-1.0), reads=[r_t1], writes=[r_e1])
                    P.op("act", lambda e, e2=e2, t1=t1: e.activation(out=e2[:, :T], in_=t1[:, :T], func=AF.Exp), reads=[r_t1], writes=[r_e2])
                    dec = lambda j, e1=e1: e1[:, j * 64:j * 64 + 1]
                    decb = e1[:, :T].rearrange("p (j t) -> p j t", t=64)[:, :, 0:1].to_broadcast([128, nch, 64])
                P.op("dve", lambda e, kgf=kgf, kk=kk, e2=e2: e.tensor_tensor(out=kgf[:, :T], in0=kk[:, :T], in1=e2[:, :T], op=ALU.mult), reads=[r_kk, r_e2], writes=[r_kgf])
                P.op("dve", lambda e, kd=kd, kgf=kgf, decb=decb: e.tensor_tensor(out=kd[:, :T].rearrange("p (j t) -> p j t", t=64), in0=kgf[:, :T].rearrange("p (j t) -> p j t", t=64),
                                                                                in1=decb, op=ALU.mult), reads=[r_kgf, r_e1], writes=[r_kd])
                if need_out:
                    P.op("act", lambda e, kg=kg, kgf=kgf: e.activation(out=kg[:, :T], in_=kgf[:, :T], func=AF.Copy), reads=[r_kgf], writes=[r_kg])
                    P.op("dve", lambda e, qg=qg, qs=qs, e1=e1: e.tensor_tensor(out=qg[:, :T], in0=qs[:, :T], in1=e1[:, :T], op=ALU.mult), reads=[r_qs, r_e1], writes=[r_qg])
                    if d == 0:
                        P.op("act", lambda e, t1=t1, Fs=Fs: e.activation(out=t1[:, :T], in_=Fs[:, :T], func=AF.Exp), reads=[r_Fs], writes=[r_t1])
                        dsrc = t1[:, T - 1:T]
                    else:
                        P.op("dve", lambda e, t1=t1, Fs=Fs, lg=lg: e.tensor_tensor(out=t1[:, :T], in0=Fs[:, :T], in1=lg[:, :T], op=ALU.subtract), reads=[r_Fs, r_lg], writes=[r_t1])
                        P.op("dve", lambda e, t1=t1, Fs=Fs: e.tensor_scalar(out=t1[:, :T], in0=t1[:, :T], scalar1=Fs[:, T - 1:T], scalar2=None, op0=ALU.subtract),
                             reads=[r_t1, r_Fs], writes=[r_t1])
                        P.op("act", lambda e, t1=t1: e.activation(out=t1[:, :T], in_=t1[:, :T], func=AF.Exp, scale=-1.0), reads=[r_t1], writes=[r_t1])
                        dsrc = t1[:, 0:1]
                    P.op("dve", lambda e, d=d, h=h, t1=t1, qs=qs: e.tensor_tensor(out=H.qDo[d][:, :T], in0=qs[:, :T], in1=t1[:, :T], op=ALU.mult),
                         reads=[r_qs, r_t1], writes=[H.r_qDo[d]])
                    P.dma("sp", lambda e, d=d, h=h: e.dma_start(out=outs["qD"](d, h, T), in_=H.qDo[d][:, :T]), H.s_q[d], reads=[H.r_qDo[d]])
                else:
                    for q4 in range(T // 128):
                        init = 0.0 if q4 == 0 else Fs[:, q4 * 128 - 1:q4 * 128]
                        P.op("dve", lambda e, q4=q4, Fs=Fs, lg=lg, init=init: e.tensor_tensor_scan(out=Fs[:, q4 * 128:(q4 + 1) * 128], data0=C.ones[:, 0:128],
                                                                                                 data1=lg[:, q4 * 128:(q4 + 1) * 128], initial=init, op0=ALU.mult, op1=ALU.add),
                             reads=[r_lg, C.r_ones, r_Fs], writes=[r_Fs])
                    P.op("act", lambda e, t1=t1, Fs=Fs: e.activation(out=t1[:, T - 1:T], in_=Fs[:, T - 1:T], func=AF.Exp), reads=[r_Fs], writes=[r_t1])
                    dsrc = t1[:, T - 1:T]
                P.op("dve", lambda e, d=d, h=h, dsrc=dsrc: e.tensor_copy(out=H.Dsum[:, d, h:h + 1], in_=dsrc), reads=[r_t1], writes=[H.r_Dsum])
                pT = C.pb[3][:].bitcast(BF16)
                r_pT = C.r_pb[3]
                for tt in range(nt):
                    P.op("pe", lambda e, tt=tt, kd=kd, pT=pT, d=d: e.transpose(out=pT[:, d * 512 + tt * 128: d * 512 + (tt + 1) * 128], in_=kd[:, tt * 128:(tt + 1) * 128], identity=C.identb[:]),
                         reads=[r_kd, C.r_identb], writes=[r_pT])
                P.op("act", lambda e, d=d, pT=pT: e.activation(out=H.kdT[d][:, 0:nt, :].rearrange("p a b -> p (a b)"), in_=pT[:, d * 512:d * 512 + nt * 128], func=AF.Copy),
                     reads=[r_pT], writes=[H.r_kdT[d]])
                pA, r_pA = C.pb[4], C.r_pb[4]
                po, r_po = C.pb[5 + d], C.r_pb[5 + d]
                pS, r_pS = C.pb[7], C.r_pb[7]
                if need_out:
                    for j in range(nch):
                        hf = (j % 2) * 64
                        P.op("pe", lambda e, j=j, hf=hf, kg=kg, qg=qg: e.matmul(pA[hf:hf + 64, j * 64:(j + 1) * 64], lhsT=kg[:, j * 64:(j + 1) * 64], rhs=qg[:, j * 64:(j + 1) * 64],
                                                                                start=True, stop=True), reads=[r_kg, r_qg], writes=[r_pA])
                    P.op("dve", lambda e, At=At, d=d: e.tensor_tensor(out=At[:, :T], in0=pA[:, :T], in1=H.mask[d][:, :T], op=ALU.mult),
                         reads=[r_pA, H.r_mask], writes=[r_At])
                order = list(range(nch)) if d == 0 else list(range(nch - 1, -1, -1))
                S = H.S[d]; r_S = H.r_S[d][h]; Sb = H.Sb[d]; r_Sb = H.r_Sb[d]
                for n, j in enumerate(order):
                    hf = (j % 2) * 64; tt = j // 2
                    if need_out:
                        P.op("pe", lambda e, j=j, hf=hf, tt=tt, At=At, h=h, n=n, po=po: e.matmul(po[:, j * 64:(j + 1) * 64], lhsT=H.vt[hf:hf + 64, tt, h * 128:(h + 1) * 128],
                                                                                         rhs=At[hf:hf + 64, j * 64:(j + 1) * 64], start=True, stop=(n == 0)),
                             reads=[H.r_vt[tt], r_At], writes=[r_po])
                        if n > 0:
                            P.op("pe", lambda e, j=j, Sb=Sb, qg=qg, po=po: e.matmul(po[:, j * 64:(j + 1) * 64], lhsT=Sb[:, :], rhs=qg[:, j * 64:(j + 1) * 64], start=False, stop=True),
                                 reads=[r_Sb, r_qg], writes=[r_po])
                    P.op("pe", lambda e, hf=hf, tt=tt, d=d, h=h: e.matmul(pS[:, 0:128], lhsT=H.kdT[d][hf:hf + 64, tt, :], rhs=H.vt[hf:hf + 64, tt, h * 128:(h + 1) * 128],
                                                                         start=True, stop=True), reads=[H.r_kdT[d], H.r_vt[tt]], writes=[r_pS])
                    if n == 0:
                        P.op("dve", lambda e, S=S, h=h: e.tensor_copy(out=S[:, h, :], in_=pS[:, 0:128]), reads=[r_pS], writes=[r_S])
                    else:
                        P.op("dve", lambda e, S=S, h=h, j=j, dec=dec: e.scalar_tensor_tensor(out=S[:, h, :], in0=S[:, h, :], scalar=dec(j), in1=pS[:, 0:128],
                                                                                            op0=ALU.mult, op1=ALU.add), reads=[r_pS, r_S, r_e1], writes=[r_S])
                    if need_out and n < nch - 1:
                        P.op("act", lambda e, S=S, h=h, Sb=Sb: e.activation(out=Sb[:, :], in_=S[:, h, :], func=AF.Copy), reads=[r_S], writes=[r_Sb])
            if need_out:
                P.op("act", lambda e: e.activation(out=H.osum[:, :T], in_=C.pb[5][:, :T], func=AF.Copy), reads=[C.r_pb[5]], writes=[H.r_osum])
                P.op("dve", lambda e: e.tensor_tensor(out=H.osum[:, :T], in0=H.osum[:, :T], in1=C.pb[6][:, :T], op=ALU.add), reads=[H.r_osum, C.r_pb[6]], writes=[H.r_osum])
                P.dma("sp", lambda e, h=h: e.dma_start(out=outs["ol"](h, T), in_=H.osum[:, :T]), H.s_ol, reads=[H.r_osum])
    allS = lambda d: [r for r in H.r_S[d]]
    for d in range(2):
        P.dma("sp", lambda e, d=d: e.dma_start(out=outs["segS"](d, seg), in_=H.S[d][:]), H.s_S[d], reads=allS(d))
    P.dma("sp", lambda e: e.dma_start(out=outs["segD"](seg), in_=H.Dsum[:]), H.s_D, reads=[H.r_Dsum])
    if need_out:
        for h in range(16):
            if first_seg:
                P.op("dve", lambda e, h=h: e.tensor_copy(out=H.Sc[0][:, h, :], in_=H.S[0][:, h, :]), reads=[H.r_S[0][h]], writes=[H.r_Sc[0]])
                P.op("dve", lambda e, h=h: e.tensor_copy(out=H.Sc[1][:, h, :], in_=H.S[1][:, h, :]), reads=[H.r_S[1][h]], writes=[H.r_Sc[1]])
            else:
                P.op("dve", lambda e, h=h: e.scalar_tensor_tensor(out=H.Sc[0][:, h, :], in0=H.Sc[0][:, h, :], scalar=H.Dsum[:, 0, h:h + 1], in1=H.S[0][:, h, :],
                                                                  op0=ALU.mult, op1=ALU.add), reads=[H.r_S[0][h], H.r_Dsum], writes=[H.r_Sc[0]])
                P.op("dve", lambda e, h=h: e.scalar_tensor_tensor(out=H.Sc[1][:, h, :], in0=H.S[1][:, h, :], scalar=H.Dc[:, 1, h:h + 1], in1=H.Sc[1][:, h, :],
                                                                  op0=ALU.mult, op1=ALU.add), reads=[H.r_S[1][h], H.r_Dc], writes=[H.r_Sc[1]])
        if first_seg:
            P.op("dve", lambda e: e.tensor_copy(out=H.Dc[:], in_=H.Dsum[:]), reads=[H.r_Dsum], writes=[H.r_Dc])
        else:
            P.op("dve", lambda e: e.tensor_tensor(out=H.Dc[:], in0=H.Dc[:], in1=H.Dsum[:], op=ALU.mult), reads=[H.r_Dsum, H.r_Dc], writes=[H.r_Dc])
        if last_seg:
            for d in range(2):
                P.dma("sp", lambda e, d=d: e.dma_start(out=outs["coreS"](d), in_=H.Sc[d][:]), H.s_core, reads=[H.r_Sc[d]])
            P.dma("sp", lambda e: e.dma_start(out=outs["coreD"](), in_=H.Dc[:]), H.s_core, reads=[H.r_Dc])


def hgrn1_final_sems(H):
    return [(s.h, s.count) for s in [H.s_sg, H.s_ol, H.s_q[0], H.s_q[1], H.s_S[0], H.s_S[1], H.s_D, H.s_core] if s.count]


def build_hgrn1_program(segs, NT, nseg):
    nc = bass.Bass("TRN2", target_bir_lowering=False)
    xin = nc.dram_tensor("xin", [128, 16, NT], F32, kind="ExternalInput").ap()
    vec = nc.dram_tensor("vec", [128, 8, 16], F32, kind="ExternalInput").ap()
    w_in = nc.dram_tensor("w_in", [D, 5 * D], F32, kind="ExternalInput").ap()
    maskd = nc.dram_tensor("masks", [2, 128, 512], F32, kind="ExternalInput").ap()
    NO = 2048
    ol = nc.dram_tensor("ol", [128, 16, NO], F32, kind="ExternalOutput").ap()
    sg = nc.dram_tensor("sg", [128, 16, NO], F32, kind="ExternalOutput").ap()
    qD = nc.dram_tensor("qD", [2, 128, 16, NO], BF16, kind="ExternalOutput").ap()
    segS = nc.dram_tensor("segS", [2, nseg, 128, 16, 128], F32, kind="ExternalOutput").ap()
    segD = nc.dram_tensor("segD", [nseg, 128, 2, 16], F32, kind="ExternalOutput").ap()
    coreS = nc.dram_tensor("coreS", [2, 128, 16, 128], F32, kind="ExternalOutput").ap()
    coreD = nc.dram_tensor("coreD", [128, 2, 16], F32, kind="ExternalOutput").ap()
    with ExitStack() as st:
        P = Prog(nc, st)
        C = setup_common(P)
        A = Arena(P, 190 * 1024)
        H = setup_hgrn1(C, A)
        V = P.sbuf("V", [128, 8, 16], F32); r_V = P.res()
        s_v = P.dsem("s_v"); s_x = P.dsem("s_x"); s_m = P.dsem("s_m")
        P.dma("sp", lambda e: e.dma_start(out=V[:], in_=vec), s_v, writes=[r_V])
        for k in (0, 2):
            P.op("dve", lambda e, k=k: e.tensor_scalar(out=V[:, k, :], in0=V[:, k, :], scalar1=1.0, scalar2=None, op0=ALU.add), reads=[r_V], writes=[r_V])
        P.op("dve", lambda e: e.tensor_tensor(out=V[:, 4, :], in0=V[:, 5, :], in1=V[:, 4, :], op=ALU.subtract), reads=[r_V], writes=[r_V])
        P.op("dve", lambda e: e.tensor_tensor(out=V[:, 5, :], in0=V[:, 7, :], in1=V[:, 6, :], op=ALU.subtract), reads=[r_V], writes=[r_V])
        P.op("act", lambda e: e.activation(out=V[:, 4:6, :], in_=V[:, 4:6, :], func=AF.Sigmoid), reads=[r_V], writes=[r_V])
        P.op("dve", lambda e: e.tensor_scalar(out=V[:, 6:8, :], in0=V[:, 4:6, :], scalar1=-1.0, scalar2=1.0, op0=ALU.mult, op1=ALU.add), reads=[r_V], writes=[r_V])
        for d in range(2):
            P.dma("sp", lambda e, d=d: e.dma_start(out=H.mask[d][:], in_=maskd[d]), s_m, writes=[H.r_mask])
        col = lambda k: (lambda c: V[:, k, c:c + 1])
        lbcol = lambda d, h: V[:, 4 + d, h:h + 1]
        omlbcol = lambda d, h: V[:, 6 + d, h:h + 1]
        lat = [s for s in segs if s[3]]
        for (t0, T, ms, need_out, si) in segs:
            emit_fill_xm_dram(C, H, xin, t0, T, col(0 + 2 * ms), col(1 + 2 * ms), r_V)
            outs = {
                "sg": lambda h, T, t0=t0: sg[:, h, t0:t0 + T],
                "ol": lambda h, T, t0=t0: ol[:, h, t0:t0 + T],
                "qD": lambda d, h, T, t0=t0: qD[d, :, h, t0:t0 + T],
                "segS": lambda d, seg: segS[d, seg],
                "segD": lambda seg: segD[seg],
                "coreS": lambda d: coreS[d],
                "coreD": lambda: coreD,
            }
            emit_hgrn_local(C, H, T, w_in, lbcol, omlbcol, r_V, outs, si, need_out,
                            first_seg=(need_out and si == lat[0][4]), last_seg=(need_out and si == lat[-1][4]))
        P.emit(final_waits={"sp": hgrn1_final_sems(H)})
    return nc


def setup_hgrn2(C, A):
    P = C.P
    G = Ctx()
    G.SinB = [[A.alloc([16, 128], BF16) for _ in range(4)] for _ in range(2)]
    G.r_SinB = [[P.res() for _ in range(4)] for _ in range(2)]
    G.Sw = A.alloc([16, 128], F32); G.r_Sw = P.res()
    G.ld = [A.alloc([16, 128], F32) for _ in range(2)]; G.r_ld = [P.res() for _ in range(2)]
    G.s_ld = [P.dsem("s_g2ld%d" % i) for i in range(2)]; G.ld_i = 0
    G.ldD = [A.alloc([16], F32) for _ in range(2)]; G.r_ldD = [P.res() for _ in range(2)]
    G.s_ldD = [P.dsem("s_g2ldD%d" % i) for i in range(2)]
    G.segD = A.alloc([5, 2, 16], F32); G.r_segD = P.res(); G.s_segD = P.dsem("s_g2sd")
    G.xs = A.alloc([16, 512], F32); G.r_xs = [P.res() for _ in range(16)]
    G.onb = A.alloc([16, 512], BF16); G.r_onb = [P.res() for _ in range(16)]
    G.NB = 2
    G.olh = [A.alloc([512], F32) for _ in range(G.NB)]; G.r_olh = [P.res() for _ in range(G.NB)]; G.s_olh = [P.dsem("s_g2ol%d" % i) for i in range(G.NB)]
    G.sgh = [A.alloc([512], F32) for _ in range(G.NB)]; G.r_sgh = [P.res() for _ in range(G.NB)]; G.s_sgh = [P.dsem("s_g2sg%d" % i) for i in range(G.NB)]
    G.qDh = [[A.alloc([512], BF16) for _ in range(G.NB)] for _ in range(2)]
    G.r_qDh = [[P.res() for _ in range(G.NB)] for _ in range(2)]
    G.s_qDh = [[P.dsem("s_g2q%d%d" % (d, i)) for i in range(G.NB)] for d in range(2)]
    G.hb = 0
    G.t = [A.alloc([512], F32) for _ in range(2)]; G.r_t = [P.res() for _ in range(2)]
    G.ws = [A.alloc([16, 512], BF16) for _ in range(2)]; G.r_ws = [P.res() for _ in range(2)]; G.s_ws = [P.dsem("s_g2w%d" % i) for i in range(2)]
    G.wi = 0
    G.s_x = P.dsem("s_g2x"); G.s_out = P.dsem("s_g2out")
    return G


def emit_hgrn_combine(C, G, segS, segD, chainS, chainD, nlat=4, ictx=4):
    P = C.P
    P.dma("sp", lambda e: e.dma_start(out=G.segD[:], in_=segD.rearrange("s p d h -> p s d h")), G.s_segD, writes=[G.r_segD])

    def load(ap):
        i = G.ld_i % 2; G.ld_i += 1
        P.dma("sp", lambda e, i=i, ap=ap: e.dma_start(out=G.ld[i][:], in_=ap), G.s_ld[i], writes=[G.r_ld[i]])
        return G.ld[i], G.r_ld[i]

    for d in range(2):
        t, r_t = load(segS(d, ictx))
        P.op("dve", lambda e, t=t: e.tensor_copy(out=G.Sw[:], in_=t[:]), reads=[r_t], writes=[G.r_Sw])
        for i in range(7):
            t, r_t = load(chainS(d, i))
            P.dma("sp", lambda e, d=d, i=i: e.dma_start(out=G.ldD[i % 2][:], in_=chainD(d, i)), G.s_ldD[i % 2], writes=[G.r_ldD[i % 2]])
            for h in range(16):
                P.op("dve", lambda e, h=h, t=t, i=i: e.scalar_tensor_tensor(out=G.Sw[:, h, :], in0=G.Sw[:, h, :], scalar=G.ldD[i % 2][:, h:h + 1], in1=t[:, h, :],
                                                                            op0=ALU.mult, op1=ALU.add), reads=[r_t, G.r_ldD[i % 2], G.r_Sw], writes=[G.r_Sw])
        order = list(range(nlat)) if d == 0 else list(range(nlat - 1, -1, -1))
        for n, j in enumerate(order):
            P.op("act", lambda e, d=d, j=j: e.activation(out=G.SinB[d][j][:], in_=G.Sw[:], func=AF.Copy), reads=[G.r_Sw], writes=[G.r_SinB[d][j]])
            if n < nlat - 1:
                t, r_t = load(segS(d, j))
                for h in range(16):
                    P.op("dve", lambda e, h=h, t=t, j=j, d=d: e.scalar_tensor_tensor(out=G.Sw[:, h, :], in0=G.Sw[:, h, :], scalar=G.segD[:, j, d, h:h + 1], in1=t[:, h, :],
                                                                                    op0=ALU.mult, op1=ALU.add), reads=[r_t, G.r_segD, G.r_Sw], writes=[G.r_Sw])


def emit_hgrn_final(C, G, j, T, t0, xin, ol, sg, qD, w_out, xout, gpcol, ngcol, lngcol, lnbcol, r_vec):
    P = C.P
    xs, r_xs = G.xs, G.r_xs
    P.dma("sp", lambda e: e.dma_start(out=xs[:, :, :T], in_=xin[:, :, t0:t0 + T]), G.s_x, writes=r_xs)
    for h in range(16):
        b = G.hb % G.NB; G.hb += 1
        P.dma("sp", lambda e, h=h, b=b: e.dma_start(out=G.olh[b][:, :T], in_=ol[:, h, t0:t0 + T]), G.s_olh[b], writes=[G.r_olh[b]])
        P.dma("sp", lambda e, h=h, b=b: e.dma_start(out=G.sgh[b][:, :T], in_=sg[:, h, t0:t0 + T]), G.s_sgh[b], writes=[G.r_sgh[b]])
        for d in range(2):
            P.dma("sp", lambda e, h=h, b=b, d=d: e.dma_start(out=G.qDh[d][b][:, :T], in_=qD[d, :, h, t0:t0 + T]), G.s_qDh[d][b], writes=[G.r_qDh[d][b]])
        po, r_po = C.pb[2 + (h % 2)], C.r_pb[2 + (h % 2)]
        for d in range(2):
            P.op("pe", lambda e, d=d, h=h, b=b, po=po: e.matmul(po[:, :T], lhsT=G.SinB[d][j][:, h, :], rhs=G.qDh[d][b][:, :T], start=(d == 0), stop=(d == 1)),
                 reads=[G.r_SinB[d][j], G.r_qDh[d][b]], writes=[r_po])
        o = G.olh[b]; r_o = G.r_olh[b]
        P.op("dve", lambda e, o=o, po=po: e.tensor_tensor(out=o[:, :T], in0=o[:, :T], in1=po[:, :T], op=ALU.add), reads=[r_po, r_o], writes=[r_o])
        sq, r_sq = C.sq[h % 2], C.r_sq[h % 2]
        P.op("act", lambda e, o=o, sq=sq: e.activation(out=sq[:, :T], in_=o[:, :T], func=AF.Square), reads=[r_o], writes=[r_sq])
        pr, r_pr = C.pb[h % 2], C.r_pb[h % 2]
        P.op("pe", lambda e, sq=sq, pr=pr: e.matmul(pr[:, :T], lhsT=C.ones[:], rhs=sq[:, :T], start=True, stop=True), reads=[C.r_ones, r_sq], writes=[r_pr])
        t, r_t = G.t[h % 2], G.r_t[h % 2]
        P.op("dve", lambda e, t=t, pr=pr: e.tensor_scalar(out=t[:, :T], in0=pr[:, :T], scalar1=1.0 / HK, scalar2=RMS_EPS, op0=ALU.mult, op1=ALU.add),
             reads=[r_pr], writes=[r_t])
        P.op("act", lambda e, t=t: e.activation(out=t[:, :T], in_=t[:, :T], func=AF.Sqrt), reads=[r_t], writes=[r_t])
        P.op("dve", lambda e, t=t: e.reciprocal(out=t[:, :T], in_=t[:, :T]), reads=[r_t], writes=[r_t])
        P.op("dve", lambda e, t=t, o=o: e.tensor_tensor(out=o[:, :T], in0=o[:, :T], in1=t[:, :T], op=ALU.mult), reads=[r_t, r_o], writes=[r_o])
        P.op("dve", lambda e, o=o, h=h, b=b: e.scalar_tensor_tensor(out=G.onb[:, h, :T], in0=o[:, :T], scalar=ngcol(), in1=G.sgh[b][:, :T], op0=ALU.mult, op1=ALU.mult),
             reads=[r_o, G.r_sgh[b], r_vec], writes=[G.r_onb[h]])
    w_v = w_out.rearrange("(h p) d -> p h d", p=128)
    for db in range(4):
        si = G.wi % 2; G.wi += 1
        ws, r_ws, s_ws = G.ws[si], G.r_ws[si], G.s_ws[si]
        P.dma("pool", lambda e, ws=ws, db=db: e.dma_start(out=ws[:], in_=w_v[:, :, db * 512:(db + 1) * 512]), s_ws, writes=[r_ws])
        for dci in range(4):
            py, r_py = C.pb[4 + dci], C.r_pb[4 + dci]
            for h in range(16):
                P.op("pe", lambda e, ws=ws, h=h, dci=dci, py=py: e.matmul(py[:, :T], lhsT=ws[:, h, dci * 128:(dci + 1) * 128], rhs=G.onb[:, h, :T],
                                                                         start=(h == 0), stop=(h == 15)), reads=[r_ws, G.r_onb[h]], writes=[r_py])
        for dci in range(4):
            c = db * 4 + dci
            py, r_py = C.pb[4 + dci], C.r_pb[4 + dci]
            P.op("dve", lambda e, c=c, py=py: e.scalar_tensor_tensor(out=xs[:, c, :T], in0=py[:, :T], scalar=gpcol(c), in1=xs[:, c, :T], op0=ALU.mult, op1=ALU.add),
                 reads=[r_py, r_xs[c], r_vec], writes=[r_xs[c]])
    emit_ln(C, xs, r_xs, T, lngcol, lnbcol, r_vec)
    P.dma("sp", lambda e: e.dma_start(out=xout[:, :, t0:t0 + T], in_=xs[:, :, :T]), G.s_out, reads=r_xs)


def build_hgrn2_program():
    nc = bass.Bass("TRN2", target_bir_lowering=False)
    NO = 2048
    xin = nc.dram_tensor("xin", [128, 16, NO], F32, kind="ExternalInput").ap()
    vec = nc.dram_tensor("vec", [128, 5, 16], F32, kind="ExternalInput").ap()
    ol = nc.dram_tensor("ol", [128, 16, NO], F32, kind="ExternalInput").ap()
    sg = nc.dram_tensor("sg", [128, 16, NO], F32, kind="ExternalInput").ap()
    qD = nc.dram_tensor("qD", [2, 128, 16, NO], BF16, kind="ExternalInput").ap()
    segS = nc.dram_tensor("segS", [2, 5, 128, 16, 128], F32, kind="ExternalInput").ap()
    segD = nc.dram_tensor("segD", [5, 128, 2, 16], F32, kind="ExternalInput").ap()
    chainS = nc.dram_tensor("chainS", [2, 7, 128, 16, 128], F32, kind="ExternalInput").ap()
    chainD = nc.dram_tensor("chainD", [2, 7, 128, 16], F32, kind="ExternalInput").ap()
    w_out = nc.dram_tensor("w_out", [D, D], F32, kind="ExternalInput").ap()
    xout = nc.dram_tensor("xout", [128, 16, NO], F32, kind="ExternalOutput").ap()
    with ExitStack() as st:
        P = Prog(nc, st)
        C = setup_common(P)
        A = Arena(P, 185 * 1024)
        G = setup_hgrn2(C, A)
        V = P.sbuf("V", [128, 5, 16], F32); r_V = P.res()
        s_v = P.dsem("s_v")
        P.dma("sp", lambda e: e.dma_start(out=V[:], in_=vec), s_v, writes=[r_V])
        P.op("dve", lambda e: e.tensor_scalar(out=V[:, 4, :], in0=V[:, 0, :], scalar1=1.0 / ALPHA, scalar2=None, op0=ALU.mult), reads=[r_V], writes=[r_V])
        emit_hgrn_combine(C, G, lambda d, i: segS[d, i], segD, lambda d, i: chainS[d, i], lambda d, i: chainD[d, i])
        col = lambda k: (lambda c: V[:, k, c:c + 1])
        for j in range(4):
            emit_hgrn_final(C, G, j, 512, j * 512, xin, ol, sg, qD, w_out, xout, col(4), lambda: V[:, 3, 0:1], col(1), col(2), r_V)
        P.emit(final_waits={"sp": [(G.s_out.h, G.s_out.count)]})
    return nc


NQ = 18


def build_mod_program():
    nc = bass.Bass("TRN2", target_bir_lowering=False)
    cc = nc.dram_tensor("cc", [2, 16, 128], F32, kind="ExternalInput").ap()
    w = nc.dram_tensor("w", [2, D, NQ * 128], F32, kind="ExternalInput").ap()
    b = nc.dram_tensor("b", [2 * NQ, 128], F32, kind="ExternalInput").ap()
    mv = nc.dram_tensor("mv", [128, 2, NQ, 2], F32, kind="ExternalOutput").ap()
    with ExitStack() as st:
        P = Prog(nc, st)
        C = setup_common(P)
        cs = P.sbuf("cs", [16, 2, 128], F32); r_cs = P.res()
        bs = P.sbuf("bs", [2 * NQ, 128], F32); r_bs = P.res()
        scT = P.sbuf("scT", [128, 16, 2], BF16); r_scT = P.res()
        mbT = P.sbuf("mbT", [128, 2 * NQ], F32); r_mbT = P.res()
        mrow = [P.sbuf("mrow%d" % i, [2, 384], F32) for i in range(2)]; r_mrow = [P.res() for _ in range(2)]
        MVs = P.sbuf("MVs", [128, 2, NQ, 2], F32); r_MVs = P.res()
        ws = [P.sbuf("mws%d" % i, [128, 16, 384], BF16) for i in range(2)]; r_ws = [P.res() for _ in range(2)]
        s_ws = [P.dsem("s_mws%d" % i) for i in range(2)]
        s_c = P.dsem("s_c"); s_b = P.dsem("s_b"); s_o = P.dsem("s_o")
        for r in range(2):
            P.dma("sp", lambda e, r=r: e.dma_start(out=cs[:, r, :], in_=cc[r]), s_c, writes=[r_cs])
        P.dma("sp", lambda e: e.dma_start(out=bs[:], in_=b), s_b, writes=[r_bs])
        P.op("act", lambda e: e.activation(out=cs[:], in_=cs[:], func=AF.Silu), reads=[r_cs], writes=[r_cs])
        pt, r_pt = C.pb[3], C.r_pb[3]
        for r in range(2):
            P.op("pe", lambda e, r=r: e.transpose(out=pt[:, r * 16:(r + 1) * 16], in_=cs[:, r, :], identity=C.ident[0:16, 0:16]),
                 reads=[r_cs, C.r_ident], writes=[r_pt])
        for r in range(2):
            P.op("dve", lambda e, r=r: e.tensor_copy(out=scT[:, :, r], in_=pt[:, r * 16:(r + 1) * 16]), reads=[r_pt], writes=[r_scT])
        pb_, r_pb_ = C.pb[4], C.r_pb[4]
        P.op("pe", lambda e: e.transpose(out=pb_[:, 0:2 * NQ], in_=bs[:], identity=C.ident[0:2 * NQ, 0:2 * NQ]), reads=[r_bs, C.r_ident], writes=[r_pb_])
        P.op("dve", lambda e: e.tensor_copy(out=mbT[:], in_=pb_[:, 0:2 * NQ]), reads=[r_pb_], writes=[r_mbT])
        pT, r_pT = C.pb[2], C.r_pb[2]
        k = 0
        for i in range(2):
            for nb in range(6):
                si = k % 2
                P.dma("pool", lambda e, i=i, nb=nb, si=si: e.dma_start(out=ws[si][:], in_=w[i].rearrange("(c p) f -> p c f", p=128)[:, :, nb * 384:(nb + 1) * 384]),
                      s_ws[si], writes=[r_ws[si]])
                pm, r_pm = C.pb[k % 2], C.r_pb[k % 2]
                for c in range(16):
                    P.op("pe", lambda e, c=c, si=si, pm=pm: e.matmul(pm[0:2, 0:384], lhsT=scT[:, c, :], rhs=ws[si][:, c, :], start=(c == 0), stop=(c == 15)),
                         reads=[r_scT, r_ws[si]], writes=[r_pm])
                P.op("act", lambda e, si=si, pm=pm: e.activation(out=mrow[si][:], in_=pm[0:2, 0:384], func=AF.Copy), reads=[r_pm], writes=[r_mrow[si]])
                for t in range(3):
                    q = i * NQ + nb * 3 + t
                    P.op("pe", lambda e, si=si, t=t, q=q: e.transpose(out=pT[:, q * 2:(q + 1) * 2], in_=mrow[si][:, t * 128:(t + 1) * 128], identity=C.ident[0:2, 0:2]),
                         reads=[r_mrow[si], C.r_ident], writes=[r_pT])
                k += 1
        for r in range(2):
            P.op("dve", lambda e, r=r: e.tensor_tensor(out=MVs[:].rearrange("p i q r -> p (i q) r")[:, :, r], in0=pT[:, 0:4 * NQ].rearrange("p (q r) -> p q r", r=2)[:, :, r],
                                                       in1=mbT[:], op=ALU.add), reads=[r_pT, r_mbT], writes=[r_MVs])
        P.dma("sp", lambda e: e.dma_start(out=mv, in_=MVs[:]), s_o, reads=[r_MVs])
        P.emit(final_waits={"sp": [(s_o.h, s_o.count)]})
    return nc


from concourse.bass_utils import run_bass_kernel_spmd

NCORES = 8
_PROGS = {}


def _prog(name, fn):
    if name not in _PROGS:
        _PROGS[name] = fn()
    return _PROGS[name]


def to_fm(a):
    return np.ascontiguousarray(np.asarray(a, np.float32).T.reshape(16, 128, -1).transpose(1, 0, 2))


def from_fm(a):
    return np.asarray(a).transpose(1, 0, 2).reshape(D, -1).T


def vrow(v):
    return np.asarray(v, np.float32).reshape(16, 128).T


def _run(nc, in_maps):
    res = run_bass_kernel_spmd(nc, in_maps, core_ids=list(range(NCORES)))
    return res.results


LAT4 = [(i * 512, 512, 0) for i in range(4)]
TILES_A = [(i * 512, 512, 0) for i in range(5)] + [(2560, 448, 0), (3008, 256, 1)]
TILES_B = LAT4 + [(2048, 256, 1)]


def ffn_vec(MV, ln_g, ln_b, i, s, pre=None):
    vec = np.zeros((128, NVROW, 16), np.float32)
    mrow = lambda j, r: MV[:, i, (s * 3 + j) * 16:(s * 3 + j + 1) * 16, r]
    vec[:, 0] = mrow(1, 0); vec[:, 1] = mrow(0, 0); vec[:, 2] = mrow(2, 0)
    vec[:, 3] = mrow(1, 1); vec[:, 4] = mrow(0, 1); vec[:, 5] = mrow(2, 1)
    vec[:, 6] = vrow(ln_g[i, s]); vec[:, 7] = vrow(ln_b[i, s])
    if pre is not None:
        vec[:, 8] = vrow(ln_g[pre]); vec[:, 9] = vrow(ln_b[pre])
    return vec


def kernel(x, c, ctx, c_ctx, mod_w, mod_b, ln_g, ln_b, ffn_w_in, ffn_w_out, pool_w, pool_scale,
           hgrn_w_in, hgrn_lb, hgrn_norm_g, hgrn_w_out):
    f32 = lambda a: np.ascontiguousarray(np.asarray(a, np.float32))
    x = f32(x)[0]; ctx = f32(ctx)[0]; c = f32(c); c_ctx = f32(c_ctx)
    mod_w = np.asarray(mod_w, np.float32); mod_b = f32(mod_b); ln_g = f32(ln_g); ln_b = f32(ln_b)
    ffn_w_in = np.asarray(ffn_w_in, np.float32); ffn_w_out = np.asarray(ffn_w_out, np.float32)
    pool_w = f32(pool_w)[0]; pool_scale = f32(pool_scale)[0]
    hgrn_w_in = f32(hgrn_w_in)[0]; hgrn_lb = f32(hgrn_lb); hgrn_norm_g = f32(hgrn_norm_g)[0]; hgrn_w_out = f32(hgrn_w_out)[0]
    cores = list(range(NCORES))
    cc = np.stack([c.reshape(16, 128), c_ctx.reshape(16, 128)])
    ins = [{"cc": cc, "w": np.ascontiguousarray(mod_w[:, :, k * 2304:(k + 1) * 2304]),
            "b": np.ascontiguousarray(mod_b[:, k * 2304:(k + 1) * 2304].reshape(36, 128))} for k in cores]
    r = _run(_prog("mod", build_mod_program), ins)
    MV = np.concatenate([rk["mv"] for rk in r], axis=2)
    ins = []
    for k in cores:
        slab = np.zeros((3008 + 256, D), np.float32)
        lo, hi = 2048 * k - 512, 2048 * k + 2048 + 448
        a, b = max(lo, 0), min(hi, 16384)
        slab[a - lo:b - lo] = x[a:b]
        slab[3008:] = ctx
        ins.append({"xin": to_fm(slab), "vec": ffn_vec(MV, ln_g, ln_b, 0, 0), "w_in": f32(ffn_w_in[0, 0]), "w_out": f32(ffn_w_out[0, 0])})
    r = _run(_prog("ffnA", lambda: build_ffn_program(TILES_A, 3264, pre_ln=False)), ins)
    x1 = [rk["xout"] for rk in r]
    vecp = np.zeros((128, 7, 16), np.float32)
    mrow = lambda i, s, j, rr: MV[:, i, (s * 3 + j) * 16:(s * 3 + j + 1) * 16, rr]
    vecp[:, 0] = mrow(0, 1, 1, 0); vecp[:, 1] = mrow(0, 1, 0, 0); vecp[:, 2] = mrow(0, 1, 2, 0)
    vecp[:, 3] = mrow(0, 1, 1, 1); vecp[:, 4] = mrow(0, 1, 0, 1); vecp[:, 5] = mrow(0, 1, 2, 1)
    vecp[:, 6] = vrow(pool_scale)
    ins = []
    for k in cores:
        valid, inv, invc = pool_consts(k)
        ins.append({"xin": np.ascontiguousarray(x1[k][:, :, :3008]), "xctx": np.ascontiguousarray(x1[k][:, :, 3008:]), "vec": vecp,
                    "valid": valid, "inv": inv, "invc": invc, "pool_w": pool_w})
    r = _run(_prog("pool", build_pool_program), ins)
    zz = [np.ascontiguousarray(np.concatenate([rk["z"], rk["zc"]], axis=2)) for rk in r]
    vec = ffn_vec(MV, ln_g, ln_b, 0, 2, pre=(0, 1))
    ins = [{"xin": zz[k], "vec": vec, "w_in": f32(ffn_w_in[0, 1]), "w_out": f32(ffn_w_out[0, 1])} for k in cores]
    r = _run(_prog("ffnB", lambda: build_ffn_program(TILES_B, 2304, pre_ln=True)), ins)
    x2 = [rk["xout"] for rk in r]
    vec = ffn_vec(MV, ln_g, ln_b, 1, 0)
    ins = [{"xin": x2[k], "vec": vec, "w_in": f32(ffn_w_in[1, 0]), "w_out": f32(ffn_w_out[1, 0])} for k in cores]
    r = _run(_prog("ffnC", lambda: build_ffn_program(TILES_B, 2304, pre_ln=False)), ins)
    x3 = [rk["xout"] for rk in r]
    vech = np.zeros((128, 8, 16), np.float32)
    vech[:, 0] = mrow(1, 1, 1, 0); vech[:, 1] = mrow(1, 1, 0, 0); vech[:, 2] = mrow(1, 1, 1, 1); vech[:, 3] = mrow(1, 1, 0, 1)
    vech[:, 4] = vrow(hgrn_lb[0, 0]); vech[:, 5] = vrow(hgrn_lb[0, 1]); vech[:, 6] = vrow(hgrn_lb[1, 0]); vech[:, 7] = vrow(hgrn_lb[1, 1])
    mf, mb = hgrn_masks()
    masks = np.stack([mf, mb])
    segs = [(i * 512, 512, 0, True, i) for i in range(4)] + [(2048, 256, 1, False, 4)]
    ins = [{"xin": x3[k], "vec": vech, "w_in": hgrn_w_in, "masks": masks} for k in cores]
    r1 = _run(_prog("hgrn1", lambda: build_hgrn1_program(segs, 2304, 5)), ins)
    vec2 = np.zeros((128, 5, 16), np.float32)
    vec2[:, 0] = mrow(1, 1, 2, 0); vec2[:, 1] = vrow(ln_g[1, 1]); vec2[:, 2] = vrow(ln_b[1, 1]); vec2[:, 3, 0] = hgrn_norm_g
    ins = []
    for k in cores:
        chS = np.zeros((2, 7, 128, 16, 128), np.float32); chD = np.ones((2, 7, 128, 16), np.float32)
        for i, cc_ in enumerate(range(0, k)):
            chS[0, i] = r1[cc_]["coreS"][0]; chD[0, i] = r1[cc_]["coreD"][:, 0, :]
        for i, cc_ in enumerate(range(7, k, -1)):
            chS[1, i] = r1[cc_]["coreS"][1]; chD[1, i] = r1[cc_]["coreD"][:, 1, :]
        ins.append({"xin": np.ascontiguousarray(x3[k][:, :, :2048]), "vec": vec2, "ol": r1[k]["ol"], "sg": r1[k]["sg"], "qD": r1[k]["qD"],
                    "segS": r1[k]["segS"], "segD": r1[k]["segD"], "chainS": chS, "chainD": chD, "w_out": hgrn_w_out})
    r = _run(_prog("hgrn2", build_hgrn2_program), ins)
    x4 = [rk["xout"] for rk in r]
    vec = ffn_vec(MV, ln_g, ln_b, 1, 2)
    ins = [{"xin": x4[k], "vec": vec, "w_in": f32(ffn_w_in[1, 1]), "w_out": f32(ffn_w_out[1, 1])} for k in cores]
    r = _run(_prog("ffnD", lambda: build_ffn_program(LAT4, 2048, pre_ln=False)), ins)
    out = np.concatenate([from_fm(rk["xout"]) for rk in r], axis=0)
    return np.ascontiguousarray(out[None].astype(np.float32))
```

```python
import numpy as np
import concourse.bass as bass
import concourse.mybir as mybir

F32 = mybir.dt.float32
BF16 = mybir.dt.bfloat16
AF = mybir.ActivationFunctionType
ALU = mybir.AluOpType
AX = mybir.AxisListType

ENGS = ("pe", "act", "dve", "pool", "sp")


class Res:
    __slots__ = ("name", "w", "r")

    def __init__(self, name):
        self.name = name
        self.w = None
        self.r = {}


class Op:
    __slots__ = ("eng", "fn", "deps", "is_dma", "dsem", "dval", "sig", "idx", "inc")

    def __init__(self, eng, fn, is_dma=False):
        self.eng = eng
        self.fn = fn
        self.deps = []
        self.is_dma = is_dma
        self.dsem = None
        self.dval = 0
        self.sig = 0
        self.idx = 0


class DmaSem:
    def __init__(self, handle):
        self.h = handle
        self.count = 0


class Prog:
    def __init__(self, nc, stack):
        self.nc = nc
        self.stack = stack
        self.streams = {e: [] for e in ENGS}
        self.nres = 0
        self.esem = {}
        for e in ENGS:
            self.esem[e] = stack.enter_context(nc.semaphore("es_" + e))
        self._dma_ops_since_barrier = []
        self._barrier_deps = None
        self._sem_free = []
        self._sem_phase = []
        self._nsem = 0

    def sbuf(self, name, shape, dt):
        return self.stack.enter_context(self.nc.sbuf_tensor(name, list(shape), dt))

    def psum(self, name, shape, dt=F32):
        return self.stack.enter_context(self.nc.psum_tensor(name, list(shape), dt))

    def dsem(self, name):
        if self._sem_free:
            d = self._sem_free.pop()
        else:
            self._nsem += 1
            d = DmaSem(self.stack.enter_context(self.nc.semaphore("ds%d" % self._nsem)))
        self._sem_phase.append(d)
        return d

    def csem(self, name):
        return DmaSem(self.stack.enter_context(self.nc.semaphore(name)))

    def end_phase(self):
        self.barrier()
        self._sem_free.extend(self._sem_phase)
        self._sem_phase = []

    def res(self, name="r"):
        self.nres += 1
        return Res(name)

    def _track(self, o, reads, writes):
        deps = []
        for r in reads:
            if r.w is not None:
                deps.append(r.w)
        for w in writes:
            if w.w is not None:
                deps.append(w.w)
            deps.extend(w.r.values())
        o.deps = deps
        for r in reads:
            r.r[(id(o.dsem) if o.is_dma else o.eng)] = o
        for w in writes:
            w.w = o
            w.r = {}
        st = self.streams[o.eng]
        o.idx = len(st)
        st.append(o)
        return o

    def op(self, eng, fn, reads=(), writes=()):
        return self._track(Op(eng, fn), reads, writes)

    def dma(self, eng, fn, sem, reads=(), writes=(), inc=16):
        o = Op(eng, fn, is_dma=True)
        o.inc = inc
        sem.count += inc
        o.dsem = sem
        o.dval = sem.count
        return self._track(o, reads, writes)

    def emit(self, final_waits=()):
        nc = self.nc
        for e in ENGS:
            for o in self.streams[e]:
                for d in o.deps:
                    if not d.is_dma:
                        if d.eng == e and e in ("pe", "sp"):
                            continue
                        d.sig = -1
        self.nsig = {}
        for e in ENGS:
            n = 0
            for o in self.streams[e]:
                if o.sig == -1:
                    n += 1
                    o.sig = n
            self.nsig[e] = (n, len(self.streams[e]))
        streams = self.streams
        esem = self.esem

        def run(engname, engobj):
            known = {}
            for o in streams[engname]:
                waits = {}
                for d in o.deps:
                    if d.is_dma:
                        key = id(d.dsem)
                        h, v = d.dsem.h, d.dval
                    else:
                        if d.eng == engname and engname in ("pe", "sp"):
                            continue
                        key = d.eng
                        h, v = esem[d.eng], d.sig
                    if known.get(key, 0) >= v:
                        continue
                    if key not in waits or waits[key][1] < v:
                        waits[key] = (h, v)
                for key, (h, v) in waits.items():
                    engobj.wait_ge(h, v)
                    known[key] = v
                ins = o.fn(engobj)
                if o.is_dma:
                    ins.then_inc(o.dsem.h, o.inc)
                elif o.sig:
                    ins.then_inc(esem[engname], 1)
            for (h, v) in final_waits.get(engname, []) if isinstance(final_waits, dict) else []:
                engobj.wait_ge(h, v)

        with nc.Block() as block:
            @block.tensor
            def _(t):
                run("pe", t)

            @block.scalar
            def _(s):
                run("act", s)

            @block.vector
            def _(v):
                run("dve", v)

            @block.gpsimd
            def _(g):
                run("pool", g)

            @block.sync
            def _(s):
                run("sp", s)


class Arena:
    def __init__(self, P, nbytes):
        self.words = nbytes // 4
        self.t = P.sbuf("arena", [128, self.words], F32)
        self.off = 0

    def reset(self, to=0):
        self.off = to

    def mark(self):
        return self.off

    def alloc(self, shape, dt):
        n = 1
        for s in shape:
            n *= s
        if dt == BF16:
            w = (n + 1) // 2
        else:
            w = n
        w = (w + 7) // 8 * 8
        assert self.off + w <= self.words, ("arena overflow", self.off, w, self.words)
        v = self.t[:, self.off:self.off + w]
        self.off += w
        if dt == BF16:
            v = v.bitcast(BF16)[:, 0:n]
        else:
            v = v[:, 0:n]
        if len(shape) == 2:
            return v.rearrange("p (a b) -> p a b", b=shape[1])
        if len(shape) == 3:
            return v.rearrange("p (a b c) -> p a b c", b=shape[1], c=shape[2])
        return v


def _barrier(self):
    last = []
    for e in ENGS:
        if self.streams[e]:
            last.append(self.streams[e][-1])
    last.extend(self._dma_ops_since_barrier)
    self._dma_ops_since_barrier = []
    self._barrier_deps = {e: list(last) for e in ENGS}


def _track2(self, o, reads, writes):
    r = Prog._track_orig(self, o, reads, writes)
    bd = getattr(self, "_barrier_deps", None)
    if bd and bd.get(o.eng):
        o.deps = list(o.deps) + [d for d in bd[o.eng] if d is not o]
        bd[o.eng] = None
    if o.is_dma:
        self._dma_ops_since_barrier.append(o)
    return r


Prog._track_orig = Prog._track
Prog._track = _track2
Prog.barrier = _barrier


import numpy as np
from contextlib import ExitStack

D = 2048
NC16 = 16
DFF = 5632
NFC = 44
ALPHA = 4 ** 0.25
LN_EPS = 1e-5
RMS_EPS = 1e-6
EPS_P = LN_EPS / (ALPHA * ALPHA)


class Ctx:
    pass


def setup_common(P, TMAX=512):
    C = Ctx()
    C.P = P
    C.ones = P.sbuf("ones", [128, 128], F32); C.r_ones = P.res()
    C.ident = P.sbuf("ident", [128, 128], F32); C.r_ident = P.res()
    C.identb = P.sbuf("identb", [128, 128], BF16); C.r_identb = P.res()
    P.op("dve", lambda e: e.memset(C.ones[:], 1.0), writes=[C.r_ones])
    P.op("dve", lambda e: e.memset(C.ident[:], 0.0), writes=[C.r_ident])
    P.op("pool", lambda e: e.affine_select(out=C.ident[:], in_=C.ident[:], compare_op=ALU.not_equal, fill=1.0,
                                           base=0, pattern=[[-1, 128]], channel_multiplier=1),
         reads=[C.r_ident], writes=[C.r_ident])
    P.op("dve", lambda e: e.tensor_copy(out=C.identb[:], in_=C.ident[:]), reads=[C.r_ident], writes=[C.r_identb])
    C.pb = [P.psum("pb%d" % i, [128, 512]) for i in range(8)]
    C.r_pb = [P.res("pb%d" % i) for i in range(8)]
    C.sq = [P.sbuf("sq%d" % i, [128, TMAX], F32) for i in range(2)]
    C.r_sq = [P.res() for i in range(2)]
    C.st = [P.sbuf("st%d" % i, [128, TMAX], F32) for i in range(4)]
    C.r_st = [P.res() for i in range(4)]
    return C


def emit_ln(C, xs, r_xs, T, gcol, bcol, r_vec, eps=EPS_P, bank0=0):
    P = C.P
    pm, pq = C.pb[bank0], C.pb[bank0 + 1]
    r_pm, r_pq = C.r_pb[bank0], C.r_pb[bank0 + 1]
    for c in range(16):
        sq, r_sq = C.sq[c % 2], C.r_sq[c % 2]
        P.op("act", lambda e, c=c, sq=sq: e.activation(out=sq[:, :T], in_=xs[:, c, :T], func=AF.Square),
             reads=[r_xs[c]], writes=[r_sq])
        P.op("pe", lambda e, c=c: e.matmul(pm[:, :T], lhsT=C.ones[:], rhs=xs[:, c, :T], start=(c == 0), stop=(c == 15)),
             reads=[C.r_ones, r_xs[c]], writes=[r_pm])
        P.op("pe", lambda e, c=c, sq=sq: e.matmul(pq[:, :T], lhsT=C.ones[:], rhs=sq[:, :T], start=(c == 0), stop=(c == 15)),
             reads=[C.r_ones, r_sq], writes=[r_pq])
    m, msq, var, rstd = C.st
    r_m, r_msq, r_var, r_rstd = C.r_st
    P.op("act", lambda e: e.activation(out=m[:, :T], in_=pm[:, :T], func=AF.Copy, scale=1.0 / D), reads=[r_pm], writes=[r_m])
    P.op("dve", lambda e: e.tensor_tensor(out=msq[:, :T], in0=m[:, :T], in1=m[:, :T], op=ALU.mult), reads=[r_m], writes=[r_msq])
    P.op("dve", lambda e: e.scalar_tensor_tensor(out=var[:, :T], in0=pq[:, :T], scalar=1.0 / D, in1=msq[:, :T],
                                                 op0=ALU.mult, op1=ALU.subtract), reads=[r_pq, r_msq], writes=[r_var])
    P.op("dve", lambda e: e.tensor_scalar(out=var[:, :T], in0=var[:, :T], scalar1=eps, scalar2=None,
                                          op0=ALU.add), reads=[r_var], writes=[r_var])
    P.op("act", lambda e: e.activation(out=var[:, :T], in_=var[:, :T], func=AF.Sqrt), reads=[r_var], writes=[r_var])
    P.op("dve", lambda e: e.reciprocal(out=rstd[:, :T], in_=var[:, :T]), reads=[r_var], writes=[r_rstd])
    for c in range(16):
        P.op("dve", lambda e, c=c: e.tensor_tensor(out=xs[:, c, :T], in0=xs[:, c, :T], in1=m[:, :T], op=ALU.subtract),
             reads=[r_xs[c], r_m], writes=[r_xs[c]])
        P.op("dve", lambda e, c=c: e.tensor_tensor(out=xs[:, c, :T], in0=xs[:, c, :T], in1=rstd[:, :T], op=ALU.mult),
             reads=[r_xs[c], r_rstd], writes=[r_xs[c]])
        P.op("act", lambda e, c=c: e.activation(out=xs[:, c, :T], in_=xs[:, c, :T], func=AF.Identity,
                                                bias=bcol(c), scale=gcol(c)),
             reads=[r_xs[c], r_vec], writes=[r_xs[c]])


def setup_ffn(C, A=None):
    P = C.P
    if A is None:
        alloc = lambda name, shape, dt: P.sbuf(name, [128] + list(shape), dt)
    else:
        alloc = lambda name, shape, dt: A.alloc(list(shape), dt)
    C.xm = alloc("xm", [16, 512], BF16); C.r_xm = [P.res() for _ in range(16)]
    C.g = alloc("g", [NFC, 512], BF16); C.r_g = [P.res() for _ in range(NFC)]
    C.NWIN = 3
    C.FCB = 2
    C.win = [alloc("win%d" % i, [16, 2, C.FCB * 128], BF16) for i in range(C.NWIN)]
    C.r_win = [P.res() for _ in range(C.NWIN)]
    C.s_win = [P.dsem("s_win%d" % i) for i in range(C.NWIN)]
    C.NWOUT = 2
    C.WOB = 11
    C.wout = [alloc("wout%d" % i, [C.WOB, 512], BF16) for i in range(C.NWOUT)]
    C.r_wout = [P.res() for _ in range(C.NWOUT)]
    C.s_wout = [P.dsem("s_wout%d" % i) for i in range(C.NWOUT)]
    C.silu = [alloc("silu%d" % i, [512], F32) for i in range(2)]
    C.r_silu = [P.res() for _ in range(2)]
    C.win_i = 0
    C.wout_i = 0
    C.silu_i = 0


def emit_ffn(C, xs, r_xs, T, w_in, w_out, opscol, shcol, gpcol, r_vec, wcache=None):
    P = C.P
    xm, g = C.xm, C.g
    for c in range(16):
        P.op("act", lambda e, c=c: e.activation(out=xm[:, c, :T], in_=xs[:, c, :T], func=AF.Identity,
                                                bias=shcol(c), scale=opscol(c)),
             reads=[r_xs[c], r_vec], writes=[C.r_xm[c]])
    w_in_v = w_in.rearrange("(c p) f -> p c f", p=128)
    FCB = C.FCB
    for fb in range(NFC // FCB):
        si = C.win_i % C.NWIN; C.win_i += 1
        ws, r_ws, s_ws = C.win[si], C.r_win[si], C.s_win[si]
        if wcache is not None and not wcache["first"]:
            P.dma("sp", lambda e, ws=ws, fb=fb: e.dma_start(out=ws[:].rearrange("p c a f -> p (c a f)"), in_=wcache["win"][fb]), s_ws, reads=[wcache["res"]], writes=[r_ws])
        else:
            for au in range(2):
                c0 = au * DFF + fb * FCB * 128
                P.dma("pool", lambda e, ws=ws, au=au, c0=c0: e.dma_start(out=ws[:, :, au, :], in_=w_in_v[:, :, c0:c0 + FCB * 128]),
                      s_ws, writes=[r_ws])
            if wcache is not None:
                P.dma("sp", lambda e, ws=ws, fb=fb: e.dma_start(out=wcache["win"][fb], in_=ws[:].rearrange("p c a f -> p (c a f)")), wcache["s_st"], reads=[r_ws], writes=[wcache["res"]])
        for j in range(FCB):
            fc = fb * FCB + j
            pa, r_pa = C.pb[(fc % 2) * 2], C.r_pb[(fc % 2) * 2]
            pu, r_pu = C.pb[(fc % 2) * 2 + 1], C.r_pb[(fc % 2) * 2 + 1]
            for c in range(16):
                P.op("pe", lambda e, c=c, ws=ws, j=j, pa=pa: e.matmul(pa[:, :T], lhsT=ws[:, c, 0, j * 128:(j + 1) * 128],
                                                                       rhs=xm[:, c, :T], start=(c == 0), stop=(c == 15)),
                     reads=[r_ws, C.r_xm[c]], writes=[r_pa])
            for c in range(16):
                P.op("pe", lambda e, c=c, ws=ws, j=j, pu=pu: e.matmul(pu[:, :T], lhsT=ws[:, c, 1, j * 128:(j + 1) * 128],
                                                                       rhs=xm[:, c, :T], start=(c == 0), stop=(c == 15)),
                     reads=[r_ws, C.r_xm[c]], writes=[r_pu])
            sl = C.silu_i % 2; C.silu_i += 1
            sb, r_sb = C.silu[sl], C.r_silu[sl]
            P.op("act", lambda e, sb=sb, pa=pa: e.activation(out=sb[:, :T], in_=pa[:, :T], func=AF.Silu),
                 reads=[r_pa], writes=[r_sb])
            P.op("dve", lambda e, sb=sb, pu=pu, fc=fc: e.tensor_tensor(out=g[:, fc, :T], in0=sb[:, :T], in1=pu[:, :T], op=ALU.mult),
                 reads=[r_sb, r_pu], writes=[C.r_g[fc]])
    w_out_v = w_out.rearrange("(f p) d -> p f d", p=128)
    WOB = C.WOB
    for db in range(4):
        for fb in range(NFC // WOB):
            si = C.wout_i % C.NWOUT; C.wout_i += 1
            ws, r_ws, s_ws = C.wout[si], C.r_wout[si], C.s_wout[si]
            if wcache is not None and not wcache["first"]:
                P.dma("sp", lambda e, ws=ws, fb=fb, db=db: e.dma_start(out=ws[:].rearrange("p f d -> p (f d)"), in_=wcache["wout"][db * (NFC // WOB) + fb]), s_ws, reads=[wcache["res"]], writes=[r_ws])
            else:
                P.dma("pool", lambda e, ws=ws, fb=fb, db=db: e.dma_start(out=ws[:], in_=w_out_v[:, fb * WOB:(fb + 1) * WOB, db * 512:(db + 1) * 512]),
                      s_ws, writes=[r_ws])
                if wcache is not None:
                    P.dma("sp", lambda e, ws=ws, fb=fb, db=db: e.dma_start(out=wcache["wout"][db * (NFC // WOB) + fb], in_=ws[:].rearrange("p f d -> p (f d)")),
                          wcache["s_st"], reads=[r_ws], writes=[wcache["res"]])
            for j in range(WOB):
                fc = fb * WOB + j
                for dci in range(4):
                    py, r_py = C.pb[4 + dci], C.r_pb[4 + dci]
                    P.op("pe", lambda e, ws=ws, j=j, dci=dci, fc=fc, py=py: e.matmul(py[:, :T], lhsT=ws[:, j, dci * 128:(dci + 1) * 128],
                                                                                     rhs=g[:, fc, :T], start=(fc == 0), stop=(fc == NFC - 1)),
                         reads=[r_ws, C.r_g[fc]], writes=[r_py])
        for dci in range(4):
            c = db * 4 + dci
            py, r_py = C.pb[4 + dci], C.r_pb[4 + dci]
            P.op("dve", lambda e, c=c, py=py: e.scalar_tensor_tensor(out=xs[:, c, :T], in0=py[:, :T], scalar=gpcol(c), in1=xs[:, c, :T],
                                                                     op0=ALU.mult, op1=ALU.add),
                 reads=[r_py, r_xs[c], r_vec], writes=[r_xs[c]])


def make_wcache(nc, P, scr=None):
    if scr is None:
        win = nc.dram_tensor("wc_in", [NFC // 2, 128, 16 * 2 * 256], BF16).ap()
        wout = nc.dram_tensor("wc_out", [16, 128, 11 * 512], BF16).ap()
    else:
        win, wout = scr
    return {"win": win, "wout": wout, "s_st": P.dsem("s_wcst"), "first": True, "res": P.res()}


V_SC, V_SH, V_GT = 0, 1, 2
V_LNG, V_LNB, V_PLG, V_PLB = 6, 7, 8, 9
V_OPS, V_GP = 10, 12
NVROW = 14


def build_ffn_program(tiles, NT, pre_ln=False, gate_mul=0.5 / ALPHA):
    nc = bass.Bass("TRN2", target_bir_lowering=False)
    xin = nc.dram_tensor("xin", [128, 16, NT], F32, kind="ExternalInput").ap()
    vec = nc.dram_tensor("vec", [128, NVROW, 16], F32, kind="ExternalInput").ap()
    w_in = nc.dram_tensor("w_in", [D, 2 * DFF], F32, kind="ExternalInput").ap()
    w_out = nc.dram_tensor("w_out", [DFF, D], F32, kind="ExternalInput").ap()
    xout = nc.dram_tensor("xout", [128, 16, NT], F32, kind="ExternalOutput").ap()
    with ExitStack() as st:
        P = Prog(nc, st)
        C = setup_common(P)
        setup_ffn(C)
        V = P.sbuf("V", [128, NVROW, 16], F32); r_V = P.res()
        s_v = P.dsem("s_v"); s_x = P.dsem("s_x"); s_o = P.dsem("s_o")
        P.dma("sp", lambda e: e.dma_start(out=V[:], in_=vec), s_v, writes=[r_V])
        for ms in range(2):
            P.op("dve", lambda e, ms=ms: e.tensor_scalar(out=V[:, V_OPS + ms, :], in0=V[:, V_SC + 3 * ms, :], scalar1=1.0, scalar2=None,
                                                         op0=ALU.add), reads=[r_V], writes=[r_V])
            P.op("dve", lambda e, ms=ms: e.tensor_scalar(out=V[:, V_GP + ms, :], in0=V[:, V_GT + 3 * ms, :], scalar1=gate_mul, scalar2=None,
                                                         op0=ALU.mult), reads=[r_V], writes=[r_V])
        xs = P.sbuf("xs", [128, 16, 512], F32); r_xs = [P.res() for _ in range(16)]
        col = lambda k: (lambda c: V[:, k, c:c + 1])
        wcache = make_wcache(nc, P)
        for ti, (t0, T, ms) in enumerate(tiles):
            P.dma("sp", lambda e, t0=t0, T=T: e.dma_start(out=xs[:, :, :T], in_=xin[:, :, t0:t0 + T]), s_x, writes=r_xs)
            if pre_ln:
                emit_ln(C, xs, r_xs, T, col(V_PLG), col(V_PLB), r_V)
            wcache["first"] = (ti == 0)
            emit_ffn(C, xs, r_xs, T, w_in, w_out, col(V_OPS + ms), col(V_SH + 3 * ms), col(V_GP + ms), r_V, wcache=wcache)
            emit_ln(C, xs, r_xs, T, col(V_LNG), col(V_LNB), r_V)
            P.dma("sp", lambda e, t0=t0, T=T: e.dma_start(out=xout[:, :, t0:t0 + T], in_=xs[:, :, :T]), s_o, reads=r_xs)
        P.emit(final_waits={"sp": [(s_o.h, s_o.count)]})
    return nc


GRID_W = 64
POOL_WINDOWS = (2, 4, 8, 16)


def emit_pool_slab(C, A, xin, zout, R, W, r0, nown, vertical, valid_d, inv_d, pool_w, opscol, shcol, combcol, r_vec, tag):
    P = C.P
    PW = W + 16
    N = R * PW
    NO = nown * W
    mark = A.mark()
    xg = A.alloc([4, R * W], F32); r_xg = [P.res() for _ in range(4)]
    hpad = A.alloc([R, PW], F32); r_hpad = P.res()
    bufs = [A.alloc([R, PW], F32) for _ in range(2)]; r_bufs = [P.res() for _ in range(2)]
    inv = A.alloc([NO], F32); r_inv = P.res()
    tmp = A.alloc([NO], F32); r_tmp = P.res()
    pbuf = A.alloc([4, NO], BF16); r_pbuf = [P.res() for _ in range(4)]
    pw = A.alloc([4, 512], BF16); r_pw = P.res()
    if valid_d is not None:
        valid = A.alloc([R * W], F32); r_valid = P.res()
    s_in = P.dsem("s_pin" + tag); s_out = P.dsem("s_pout" + tag); s_w = P.dsem("s_pw" + tag)
    s_inv = P.dsem("s_pinv" + tag)
    if valid_d is not None:
        s_val = P.dsem("s_pval" + tag)
        P.dma("sp", lambda e: e.dma_start(out=valid[:], in_=valid_d), s_val, writes=[r_valid])
    P.op("dve", lambda e: e.memset(hpad[:], 0.0), writes=[r_hpad])
    for b in range(2):
        P.op("dve", lambda e, b=b: e.memset(bufs[b][:], 0.0), writes=[r_bufs[b]])
    flat = lambda t: t.rearrange("p r w -> p (r w)")
    TB = 512 if NO >= 512 else NO
    ntb = NO // TB
    for gi, win in enumerate(POOL_WINDOWS):
        hw = win // 2
        P.dma("sp", lambda e, gi=gi: e.dma_start(out=xg[:], in_=xin[:, gi * 4:(gi + 1) * 4, :]), s_in, writes=r_xg)
        P.dma("sp", lambda e, gi=gi: e.dma_start(out=inv[:], in_=inv_d[gi]), s_inv, writes=[r_inv])
        P.dma("pool", lambda e, gi=gi: e.dma_start(out=pw[:], in_=pool_w[gi].rearrange("(i p) o -> p i o", p=128)), s_w, writes=[r_pw])
        for ic in range(4):
            c = gi * 4 + ic
            hd = hpad[:, :, 8:8 + W]
            P.op("act", lambda e, ic=ic, c=c, hd=hd: e.activation(out=hd, in_=xg[:, ic, :].rearrange("p (r w) -> p r w", w=W), func=AF.Identity,
                                                                  bias=shcol(c), scale=opscol(c)),
                 reads=[r_xg[ic], r_vec], writes=[r_hpad])
            if valid_d is not None:
                P.op("dve", lambda e, hd=hd: e.tensor_tensor(out=hd, in0=hd, in1=valid[:].rearrange("p (r w) -> p r w", w=W), op=ALU.mult),
                     reads=[r_hpad, r_valid], writes=[r_hpad])
            src, r_src = hpad, r_hpad
            bi = 0
            shifts = []
            k = 1
            while k < win:
                shifts.append(k); k *= 2
            if vertical:
                k = 1
                while k < win:
                    shifts.append(k * PW); k *= 2
            for k in shifts:
                dst, r_dst = bufs[bi], r_bufs[bi]; bi ^= 1
                P.op("dve", lambda e, src=src, dst=dst, k=k: e.tensor_tensor(out=flat(dst)[:, k:N], in0=flat(src)[:, k:N], in1=flat(src)[:, 0:N - k], op=ALU.add),
                     reads=[r_src], writes=[r_dst])
                src, r_src = dst, r_dst
            ro = r0 + hw - 1 if vertical else r0
            vview = src[:, ro:ro + nown, 8 + hw - 1:8 + hw - 1 + W]
            P.op("dve", lambda e, vview=vview: e.tensor_tensor(out=tmp[:].rearrange("p (r w) -> p r w", w=W), in0=vview,
                                                               in1=inv[:].rearrange("p (r w) -> p r w", w=W), op=ALU.mult),
                 reads=[r_src, r_inv], writes=[r_tmp])
            P.op("dve", lambda e, ic=ic: e.tensor_tensor(out=pbuf[:, ic, :].rearrange("p (r w) -> p r w", w=W), in0=tmp[:].rearrange("p (r w) -> p r w", w=W),
                                                         in1=hpad[:, r0:r0 + nown, 8:8 + W], op=ALU.subtract),
                 reads=[r_tmp, r_hpad], writes=[r_pbuf[ic]])
        for oc in range(4):
            c = gi * 4 + oc
            for tb in range(ntb):
                bk = 4 + (oc * ntb + tb) % 4
                pk, r_pk = C.pb[bk], C.r_pb[bk]
                for ic in range(4):
                    P.op("pe", lambda e, ic=ic, oc=oc, tb=tb, pk=pk: e.matmul(pk[:, :TB], lhsT=pw[:, ic, oc * 128:(oc + 1) * 128],
                                                                              rhs=pbuf[:, ic, tb * TB:(tb + 1) * TB], start=(ic == 0), stop=(ic == 3)),
                         reads=[r_pw, r_pbuf[ic]], writes=[r_pk])
                xo = xg[:, oc, r0 * W + tb * TB: r0 * W + (tb + 1) * TB]
                P.op("dve", lambda e, xo=xo, pk=pk, c=c: e.scalar_tensor_tensor(out=xo, in0=pk[:, :TB], scalar=combcol(c), in1=xo,
                                                                                op0=ALU.mult, op1=ALU.add),
                     reads=[r_pk, r_xg[oc], r_vec], writes=[r_xg[oc]])
        P.dma("sp", lambda e, gi=gi: e.dma_start(out=zout[:, gi * 4:(gi + 1) * 4, :], in_=xg[:, :, r0 * W:(r0 + nown) * W]), s_out, reads=r_xg)
    A.reset(mark)
    return s_out


def pool_consts(core):
    R, W = 47, 64
    rows = 32 * core - 8 + np.arange(R)
    valid = ((rows >= 0) & (rows < 256)).astype(np.float32)
    valid = np.broadcast_to(valid[None, :, None], (128, R, W)).reshape(128, R * W).copy()
    inv = np.zeros((4, 128, 32 * W), np.float32)
    invc = np.zeros((4, 128, 256), np.float32)
    for gi, win in enumerate(POOL_WINDOWS):
        hw = win // 2
        r = 32 * core + np.arange(32)
        cr = np.minimum(r - hw + win, 256) - np.maximum(r - hw, 0)
        c = np.arange(W)
        cc = np.minimum(c - hw + win, W) - np.maximum(c - hw, 0)
        t = (1.0 / (cr[:, None] * cc[None, :]).astype(np.float64)).astype(np.float32).reshape(-1)
        inv[gi] = t[None, :]
        tt = np.arange(256)
        ct = np.minimum(tt - hw + win, 256) - np.maximum(tt - hw, 0)
        invc[gi] = (1.0 / ct.astype(np.float64)).astype(np.float32)[None, :]
    return valid, inv, invc


def build_pool_program():
    nc = bass.Bass("TRN2", target_bir_lowering=False)
    xin = nc.dram_tensor("xin", [128, 16, 3008], F32, kind="ExternalInput").ap()
    xctx = nc.dram_tensor("xctx", [128, 16, 256], F32, kind="ExternalInput").ap()
    vec = nc.dram_tensor("vec", [128, 7, 16], F32, kind="ExternalInput").ap()
    valid = nc.dram_tensor("valid", [128, 3008], F32, kind="ExternalInput").ap()
    inv = nc.dram_tensor("inv", [4, 128, 2048], F32, kind="ExternalInput").ap()
    invc = nc.dram_tensor("invc", [4, 128, 256], F32, kind="ExternalInput").ap()
    pool_w = nc.dram_tensor("pool_w", [4, 512, 512], F32, kind="ExternalInput").ap()
    z = nc.dram_tensor("z", [128, 16, 2048], F32, kind="ExternalOutput").ap()
    zc = nc.dram_tensor("zc", [128, 16, 256], F32, kind="ExternalOutput").ap()
    with ExitStack() as st:
        P = Prog(nc, st)
        C = setup_common(P)
        A = Arena(P, 150 * 1024)
        V = P.sbuf("V", [128, 11, 16], F32); r_V = P.res()
        s_v = P.dsem("s_v")
        P.dma("sp", lambda e: e.dma_start(out=V[:, 0:7, :], in_=vec), s_v, writes=[r_V])
        for ms in range(2):
            P.op("dve", lambda e, ms=ms: e.tensor_scalar(out=V[:, 7 + 2 * ms, :], in0=V[:, 3 * ms, :], scalar1=1.0, scalar2=None, op0=ALU.add), reads=[r_V], writes=[r_V])
            P.op("dve", lambda e, ms=ms: e.scalar_tensor_tensor(out=V[:, 8 + 2 * ms, :], in0=V[:, 2 + 3 * ms, :], scalar=1.0 / ALPHA, in1=V[:, 6, :], op0=ALU.mult, op1=ALU.mult),
                 reads=[r_V], writes=[r_V])
        col = lambda k: (lambda c: V[:, k, c:c + 1])
        s1 = emit_pool_slab(C, A, xin, z, 47, 64, 8, 32, True, valid, inv, pool_w, col(7), col(1), col(8), r_V, "a")
        P.barrier()
        s2 = emit_pool_slab(C, A, xctx, zc, 1, 256, 0, 1, False, None, invc, pool_w, col(9), col(4), col(10), r_V, "c")
        P.emit(final_waits={"sp": [(s1.h, s1.count), (s2.h, s2.count)]})
    return nc


HK = 128
NH = 16


def hgrn_masks():
    p = np.arange(128)
    half, sl = p // 64, p % 64
    j = np.arange(8)[None, :, None]
    t = np.arange(64)[None, None, :]
    ok = (j % 2 == half[:, None, None])
    mf = (ok & (sl[:, None, None] <= t)).astype(np.float32).reshape(128, 512)
    mb = (ok & (sl[:, None, None] >= t)).astype(np.float32).reshape(128, 512)
    return mf, mb


def setup_hgrn1(C, A):
    P = C.P
    H = Ctx()
    H.xm = A.alloc([16, 512], BF16); H.r_xm = [P.res() for _ in range(16)]
    H.vt = A.alloc([4, 2048], BF16); H.r_vt = [P.res() for _ in range(4)]
    H.NW = 6
    H.ws = [A.alloc([16, 256], BF16) for _ in range(H.NW)]; H.r_ws = [P.res() for _ in range(H.NW)]
    H.s_ws = [P.dsem("s_hw%d" % i) for i in range(H.NW)]
    H.wi = 0
    names = ["sg", "lg", "kk", "F", "Fs", "e1", "e2", "t1"]
    H.t = {}; H.r_t = {}
    H.t[("qs", 0)] = A.alloc([512], F32); H.r_t[("qs", 0)] = P.res()
    for d in range(2):
        for n in names:
            H.t[(n, d)] = A.alloc([512], F32); H.r_t[(n, d)] = P.res()
        H.t[("kgf", d)] = H.t[("sg", d)]; H.r_t[("kgf", d)] = H.r_t[("sg", d)]
    H.sgo = A.alloc([512], F32); H.r_sgo = P.res()
    H.osum = A.alloc([512], F32); H.r_osum = P.res()
    for n in ["qg", "kg", "kd", "qD", "At"]:
        for d in range(2):
            H.t[(n, d)] = A.alloc([512], BF16); H.r_t[(n, d)] = P.res()
    H.kdT = [A.alloc([4, 128], BF16) for _ in range(2)]; H.r_kdT = [P.res() for _ in range(2)]
    H.S = [A.alloc([16, 128], F32) for _ in range(2)]; H.r_S = [[P.res() for _ in range(16)] for _ in range(2)]
    H.Sb = [A.alloc([128], BF16) for _ in range(2)]; H.r_Sb = [P.res() for _ in range(2)]
    H.Dsum = A.alloc([2, 16], F32); H.r_Dsum = P.res()
    H.mask = [A.alloc([512], F32) for _ in range(2)]; H.r_mask = P.res()
    H.qDo = [A.alloc([512], BF16) for _ in range(2)]; H.r_qDo = [P.res() for _ in range(2)]
    H.xst = [A.alloc([4, 512], F32) for _ in range(2)]; H.r_xst = [P.res() for _ in range(2)]
    H.s_xst = [P.dsem("s_xst%d" % i) for i in range(2)]; H.xst_i = 0
    H.Sc = [A.alloc([16, 128], F32) for _ in range(2)]; H.r_Sc = [P.res() for _ in range(2)]
    H.Dc = A.alloc([2, 16], F32); H.r_Dc = P.res()
    H.s_sg = P.dsem("s_h1sg"); H.s_ol = P.dsem("s_h1ol"); H.s_q = [P.dsem("s_h1q%d" % d) for d in range(2)]
    H.s_S = [P.dsem("s_h1S%d" % d) for d in range(2)]; H.s_D = P.dsem("s_h1D"); H.s_core = P.dsem("s_h1c")
    return H


def emit_fill_xm_dram(C, H, xin, t0, T, opscol, shcol, r_vec):
    P = C.P
    for q in range(4):
        i = H.xst_i % 2; H.xst_i += 1
        xst, r_xst, s_xst = H.xst[i], H.r_xst[i], H.s_xst[i]
        P.dma("sp", lambda e, q=q, xst=xst: e.dma_start(out=xst[:, :, :T], in_=xin[:, q * 4:(q + 1) * 4, t0:t0 + T]), s_xst, writes=[r_xst])
        for cc in range(4):
            c = q * 4 + cc
            P.op("act", lambda e, c=c, cc=cc, xst=xst: e.activation(out=H.xm[:, c, :T], in_=xst[:, cc, :T], func=AF.Identity, bias=shcol(c), scale=opscol(c)),
                 reads=[r_xst, r_vec], writes=[H.r_xm[c]])


def emit_hgrn_local(C, H, T, w_in, lbcol, omlbcol, r_vec, outs, seg, need_out, first_seg, last_seg):
    P = C.P
    xm = H.xm
    nt = T // 128
    nch = T // 64
    w_v = w_in.rearrange("(c p) f -> p c f", p=128)

    def load_w(col0):
        si = H.wi % H.NW; H.wi += 1
        ws, r_ws, s_ws = H.ws[si], H.r_ws[si], H.s_ws[si]
        P.dma("pool", lambda e: e.dma_start(out=ws[:], in_=w_v[:, :, col0:col0 + 256]), s_ws, writes=[r_ws])
        return ws, r_ws

    for cb in range(8):
        ws, r_ws = load_w(D + cb * 256)
        for tt in range(nt):
            pk, r_pk = C.pb[tt], C.r_pb[tt]
            for c in range(16):
                P.op("pe", lambda e, c=c, tt=tt, ws=ws, pk=pk: e.matmul(pk[:, :256], lhsT=xm[:, c, tt * 128:(tt + 1) * 128], rhs=ws[:, c, :],
                                                                        start=(c == 0), stop=(c == 15)),
                     reads=[H.r_xm[c], r_ws], writes=[r_pk])
            P.op("act", lambda e, tt=tt, cb=cb, pk=pk: e.activation(out=H.vt[:, tt, cb * 256:(cb + 1) * 256], in_=pk[:, :256], func=AF.Copy),
                 reads=[r_pk], writes=[H.r_vt[tt]])
    types = ["q", "af", "ab", "og"] if need_out else ["af", "ab"]
    tcol = {"q": 0, "af": 2 * D, "ab": 3 * D, "og": 4 * D}
    pbank = 0
    for hg in range(8):
        slots = {ty: load_w(tcol[ty] + hg * 256) for ty in types}
        for hh in range(2):
            h = hg * 2 + hh
            pr = {}
            for ty in types:
                bk = pbank % 3; pbank += 1
                pk, r_pk = C.pb[bk], C.r_pb[bk]
                ws, r_ws = slots[ty]
                for c in range(16):
                    P.op("pe", lambda e, c=c, ws=ws, hh=hh, pk=pk: e.matmul(pk[:, :T], lhsT=ws[:, c, hh * 128:(hh + 1) * 128], rhs=xm[:, c, :T],
                                                                            start=(c == 0), stop=(c == 15)),
                         reads=[H.r_xm[c], r_ws], writes=[r_pk])
                pr[ty] = (pk, r_pk)
                if ty == "q":
                    qs, r_qs = H.t[("qs", 0)], H.r_t[("qs", 0)]
                    P.op("act", lambda e, pk=pk, qs=qs: e.activation(out=qs[:, :T], in_=pk[:, :T], func=AF.Silu), reads=[r_pk], writes=[r_qs])
                    P.op("dve", lambda e, qs=qs: e.tensor_scalar(out=qs[:, :T], in0=qs[:, :T], scalar1=HK ** -0.5, scalar2=None, op0=ALU.mult),
                         reads=[r_qs], writes=[r_qs])
                elif ty == "og":
                    P.op("act", lambda e, pk=pk: e.activation(out=H.sgo[:, :T], in_=pk[:, :T], func=AF.Silu), reads=[r_pk], writes=[H.r_sgo])
                    P.dma("sp", lambda e, h=h: e.dma_start(out=outs["sg"](h, T), in_=H.sgo[:, :T]), H.s_sg, reads=[H.r_sgo])
                else:
                    d = 0 if ty == "af" else 1
                    sg, r_sg = H.t[("sg", d)], H.r_t[("sg", d)]
                    P.op("act", lambda e, pk=pk, sg=sg: e.activation(out=sg[:, :T], in_=pk[:, :T], func=AF.Sigmoid), reads=[r_pk], writes=[r_sg])
            qs, r_qs = H.t[("qs", 0)], H.r_t[("qs", 0)]
            for d in range(2):
                tg = lambda n: (H.t[(n, d)], H.r_t[(n, d)])
                sg, r_sg = tg("sg"); lg, r_lg = tg("lg"); kk, r_kk = tg("kk"); F, r_F = tg("F"); Fs, r_Fs = tg("Fs")
                e1, r_e1 = tg("e1"); e2, r_e2 = tg("e2"); kgf, r_kgf = tg("kgf"); t1, r_t1 = tg("t1")
                qg, r_qg = tg("qg"); kg, r_kg = tg("kg"); kd, r_kd = tg("kd"); qD, r_qD = tg("qD"); At, r_At = tg("At")
                P.op("dve", lambda e, sg=sg, d=d, h=h: e.tensor_scalar(out=sg[:, :T], in0=sg[:, :T], scalar1=omlbcol(d, h), scalar2=lbcol(d, h),
                                                                        op0=ALU.mult, op1=ALU.add), reads=[r_sg, r_vec], writes=[r_sg])
                P.op("act", lambda e, sg=sg, lg=lg: e.activation(out=lg[:, :T], in_=sg[:, :T], func=AF.Ln), reads=[r_sg], writes=[r_lg])
                P.op("dve", lambda e, sg=sg, kk=kk: e.tensor_scalar(out=kk[:, :T], in0=sg[:, :T], scalar1=-1.0, scalar2=1.0, op0=ALU.mult, op1=ALU.add),
                     reads=[r_sg], writes=[r_kk])
                for j in range(nch):
                    P.op("dve", lambda e, j=j, F=F, lg=lg: e.tensor_tensor_scan(out=F[:, j * 64:(j + 1) * 64], data0=C.ones[:, 0:64], data1=lg[:, j * 64:(j + 1) * 64],
                                                                               initial=0.0, op0=ALU.mult, op1=ALU.add),
                         reads=[r_lg, C.r_ones], writes=[r_F])
                if need_out:
                    for q4 in range(T // 128):
                        init = 0.0 if q4 == 0 else Fs[:, q4 * 128 - 1:q4 * 128]
                        P.op("dve", lambda e, q4=q4, Fs=Fs, lg=lg, init=init: e.tensor_tensor_scan(out=Fs[:, q4 * 128:(q4 + 1) * 128], data0=C.ones[:, 0:128],
                                                                                                 data1=lg[:, q4 * 128:(q4 + 1) * 128], initial=init, op0=ALU.mult, op1=ALU.add),
                             reads=[r_lg, C.r_ones, r_Fs], writes=[r_Fs])
                F3 = F[:, :T].rearrange("p (j t) -> p j t", t=64)
                if d == 0:
                    P.op("act", lambda e, e1=e1, F=F: e.activation(out=e1[:, :T], in_=F[:, :T], func=AF.Exp), reads=[r_F], writes=[r_e1])
                    P.op("act", lambda e, e2=e2, F=F: e.activation(out=e2[:, :T], in_=F[:, :T], func=AF.Exp, scale=-1.0), reads=[r_F], writes=[r_e2])
                    dec = lambda j, e1=e1: e1[:, j * 64 + 63:j * 64 + 64]
                    decb = e1[:, :T].rearrange("p (j t) -> p j t", t=64)[:, :, 63:64].to_broadcast([128, nch, 64])
                else:
                    P.op("dve", lambda e, t1=t1, F=F, lg=lg: e.tensor_tensor(out=t1[:, :T], in0=F[:, :T], in1=lg[:, :T], op=ALU.subtract), reads=[r_F, r_lg], writes=[r_t1])
                    P.op("dve", lambda e, t1=t1, F3=F3: e.tensor_tensor(out=t1[:, :T].rearrange("p (j t) -> p j t", t=64), in0=t1[:, :T].rearrange("p (j t) -> p j t", t=64),
                                                                        in1=F3[:, :, 63:64].to_broadcast([128, nch, 64]), op=ALU.subtract), reads=[r_t1, r_F], writes=[r_t1])
                    P.op("act", lambda e, e1=e1, t1=t1: e.activation(out=e1[:, :T], in_=t1[:, :T], func=AF.Exp, scale=-1.0), reads=[r_t1], writes=[r_e1])
                    P.op("act", lambda e, e2=e2, t1=t1: e.activation(out=e2[:, :T], in_=t1[:, :T], func=AF.Exp), reads=[r_t1], writes=[r_e2])
                    dec = lambda j, e1=e1: e1[:, j * 64:j * 64 + 1]
                    decb = e1[:, :T].rearrange("p (j t) -> p j t", t=64)[:, :, 0:1].to_broadcast([128, nch, 64])
                P.op("dve", lambda e, kgf=kgf, kk=kk, e2=e2: e.tensor_tensor(out=kgf[:, :T], in0=kk[:, :T], in1=e2[:, :T], op=ALU.mult), reads=[r_kk, r_e2], writes=[r_kgf])
                P.op("dve", lambda e, kd=kd, kgf=kgf, decb=decb: e.tensor_tensor(out=kd[:, :T].rearrange("p (j t) -> p j t", t=64), in0=kgf[:, :T].rearrange("p (j t) -> p j t", t=64),
                                                                                in1=decb, op=ALU.mult), reads=[r_kgf, r_e1], writes=[r_kd])
                if need_out:
                    P.op("act", lambda e, kg=kg, kgf=kgf: e.activation(out=kg[:, :T], in_=kgf[:, :T], func=AF.Copy), reads=[r_kgf], writes=[r_kg])
                    P.op("dve", lambda e, qg=qg, qs=qs, e1=e1: e.tensor_tensor(out=qg[:, :T], in0=qs[:, :T], in1=e1[:, :T], op=ALU.mult), reads=[r_qs, r_e1], writes=[r_qg])
                    if d == 0:
                        P.op("act", lambda e, t1=t1, Fs=Fs: e.activation(out=t1[:, :T], in_=Fs[:, :T], func=AF.Exp), reads=[r_Fs], writes=[r_t1])
                        dsrc = t1[:, T - 1:T]
                    else:
                        P.op("dve", lambda e, t1=t1, Fs=Fs, lg=lg: e.tensor_tensor(out=t1[:, :T], in0=Fs[:, :T], in1=lg[:, :T], op=ALU.subtract), reads=[r_Fs, r_lg], writes=[r_t1])
                        P.op("dve", lambda e, t1=t1, Fs=Fs: e.tensor_scalar(out=t1[:, :T], in0=t1[:, :T], scalar1=Fs[:, T - 1:T], scalar2=None, op0=ALU.subtract),
                             reads=[r_t1, r_Fs], writes=[r_t1])
                        P.op("act", lambda e, t1=t1: e.activation(out=t1[:, :T], in_=t1[:, :T], func=AF.Exp, scale=-1.0), reads=[r_t1], writes=[r_t1])
                        dsrc = t1[:, 0:1]
                    P.op("dve", lambda e, d=d, h=h, t1=t1, qs=qs: e.tensor_tensor(out=H.qDo[d][:, :T], in0=qs[:, :T], in1=t1[:, :T], op=ALU.mult),
                         reads=[r_qs, r_t1], writes=[H.r_qDo[d]])
                    P.dma("sp", lambda e, d=d, h=h: e.dma_start(out=outs["qD"](d, h, T), in_=H.qDo[d][:, :T]), H.s_q[d], reads=[H.r_qDo[d]])
                else:
                    for q4 in range(T // 128):
                        init = 0.0 if q4 == 0 else Fs[:, q4 * 128 - 1:q4 * 128]
                        P.op("dve", lambda e, q4=q4, Fs=Fs, lg=lg, init=init: e.tensor_tensor_scan(out=Fs[:, q4 * 128:(q4 + 1) * 128], data0=C.ones[:, 0:128],
                                                                                                 data1=lg[:, q4 * 128:(q4 + 1) * 128], initial=init, op0=ALU.mult, op1=ALU.add),
                             reads=[r_lg, C.r_ones, r_Fs], writes=[r_Fs])
                    P.op("act", lambda e, t1=t1, Fs=Fs: e.activation(out=t1[:, T - 1:T], in_=Fs[:, T - 1:T], func=AF.Exp), reads=[r_Fs], writes=[r_t1])
                    dsrc = t1[:, T - 1:T]
                P.op("dve", lambda e, d=d, h=h, dsrc=dsrc: e.tensor_copy(out=H.Dsum[:, d, h:h + 1], in_=dsrc), reads=[r_t1], writes=[H.r_Dsum])
                pT = C.pb[3][:].bitcast(BF16)
                r_pT = C.r_pb[3]
                for tt in range(nt):
                    P.op("pe", lambda e, tt=tt, kd=kd, pT=pT, d=d: e.transpose(out=pT[:, d * 512 + tt * 128: d * 512 + (tt + 1) * 128], in_=kd[:, tt * 128:(tt + 1) * 128], identity=C.identb[:]),
                         reads=[r_kd, C.r_identb], writes=[r_pT])
                P.op("act", lambda e, d=d, pT=pT: e.activation(out=H.kdT[d][:, 0:nt, :].rearrange("p a b -> p (a b)"), in_=pT[:, d * 512:d * 512 + nt * 128], func=AF.Copy),
                     reads=[r_pT], writes=[H.r_kdT[d]])
                pA, r_pA = C.pb[4], C.r_pb[4]
                po, r_po = C.pb[5 + d], C.r_pb[5 + d]
                pS, r_pS = C.pb[7], C.r_pb[7]
                if need_out:
                    for j in range(nch):
                        hf = (j % 2) * 64
                        P.op("pe", lambda e, j=j, hf=hf, kg=kg, qg=qg: e.matmul(pA[hf:hf + 64, j * 64:(j + 1) * 64], lhsT=kg[:, j * 64:(j + 1) * 64], rhs=qg[:, j * 64:(j + 1) * 64],
                                                                                start=True, stop=True), reads=[r_kg, r_qg], writes=[r_pA])
                    P.op("dve", lambda e, At=At, d=d: e.tensor_tensor(out=At[:, :T], in0=pA[:, :T], in1=H.mask[d][:, :T], op=ALU.mult),
                         reads=[r_pA, H.r_mask], writes=[r_At])
                order = list(range(nch)) if d == 0 else list(range(nch - 1, -1, -1))
                S = H.S[d]; r_S = H.r_S[d][h]; Sb = H.Sb[d]; r_Sb = H.r_Sb[d]
                for n, j in enumerate(order):
                    hf = (j % 2) * 64; tt = j // 2
                    if need_out:
                        P.op("pe", lambda e, j=j, hf=hf, tt=tt, At=At, h=h, n=n, po=po: e.matmul(po[:, j * 64:(j + 1) * 64], lhsT=H.vt[hf:hf + 64, tt, h * 128:(h + 1) * 128],
                                                                                         rhs=At[hf:hf + 64, j * 64:(j + 1) * 64], start=True, stop=(n == 0)),
                             reads=[H.r_vt[tt], r_At], writes=[r_po])
                        if n > 0:
                            P.op("pe", lambda e, j=j, Sb=Sb, qg=qg, po=po: e.matmul(po[:, j * 64:(j + 1) * 64], lhsT=Sb[:, :], rhs=qg[:, j * 64:(j + 1) * 64], start=False, stop=True),
                                 reads=[r_Sb, r_qg], writes=[r_po])
                    P.op("pe", lambda e, hf=hf, tt=tt, d=d, h=h: e.matmul(pS[:, 0:128], lhsT=H.kdT[d][hf:hf + 64, tt, :], rhs=H.vt[hf:hf + 64, tt, h * 128:(h + 1) * 128],
                                                                         start=True, stop=True), reads=[H.r_kdT[d], H.r_vt[tt]], writes=[r_pS])
                    if n == 0:
                        P.op("dve", lambda e, S=S, h=h: e.tensor_copy(out=S[:, h, :], in_=pS[:, 0:128]), reads=[r_pS], writes=[r_S])
                    else:
                        P.op("dve", lambda e, S=S, h=h, j=j, dec=dec: e.scalar_tensor_tensor(out=S[:, h, :], in0=S[:, h, :], scalar=dec(j), in1=pS[:, 0:128],
                                                                                            op0=ALU.mult, op1=ALU.add), reads=[r_pS, r_S, r_e1], writes=[r_S])
                    if need_out and n < nch - 1:
                        P.op("act", lambda e, S=S, h=h, Sb=Sb: e.activation(out=Sb[:, :], in_=S[:, h, :], func=AF.Copy), reads=[r_S], writes=[r_Sb])
            if need_out:
                P.op("act", lambda e: e.activation(out=H.osum[:, :T], in_=C.pb[5][:, :T], func=AF.Copy), reads=[C.r_pb[5]], writes=[H.r_osum])
                P.op("dve", lambda e: e.tensor_tensor(out=H.osum[:, :T], in0=H.osum[:, :T], in1=C.pb[6][:, :T], op=ALU.add), reads=[H.r_osum, C.r_pb[6]], writes=[H.r_osum])
                P.dma("sp", lambda e, h=h: e.dma_start(out=outs["ol"](h, T), in_=H.osum[:, :T]), H.s_ol, reads=[H.r_osum])
    allS = lambda d: [r for r in H.r_S[d]]
    for d in range(2):
        P.dma("sp", lambda e, d=d: e.dma_start(out=outs["segS"](d, seg), in_=H.S[d][:]), H.s_S[d], reads=allS(d))
    P.dma("sp", lambda e: e.dma_start(out=outs["segD"](seg), in_=H.Dsum[:]), H.s_D, reads=[H.r_Dsum])
    if need_out:
        for h in range(16):
            if first_seg:
                P.op("dve", lambda e, h=h: e.tensor_copy(out=H.Sc[0][:, h, :], in_=H.S[0][:, h, :]), reads=[H.r_S[0][h]], writes=[H.r_Sc[0]])
                P.op("dve", lambda e, h=h: e.tensor_copy(out=H.Sc[1][:, h, :], in_=H.S[1][:, h, :]), reads=[H.r_S[1][h]], writes=[H.r_Sc[1]])
            else:
                P.op("dve", lambda e, h=h: e.scalar_tensor_tensor(out=H.Sc[0][:, h, :], in0=H.Sc[0][:, h, :], scalar=H.Dsum[:, 0, h:h + 1], in1=H.S[0][:, h, :],
                                                                  op0=ALU.mult, op1=ALU.add), reads=[H.r_S[0][h], H.r_Dsum], writes=[H.r_Sc[0]])
                P.op("dve", lambda e, h=h: e.scalar_tensor_tensor(out=H.Sc[1][:, h, :], in0=H.S[1][:, h, :], scalar=H.Dc[:, 1, h:h + 1], in1=H.Sc[1][:, h, :],
                                                                  op0=ALU.mult, op1=ALU.add), reads=[H.r_S[1][h], H.r_Dc], writes=[H.r_Sc[1]])
        if first_seg:
            P.op("dve", lambda e: e.tensor_copy(out=H.Dc[:], in_=H.Dsum[:]), reads=[H.r_Dsum], writes=[H.r_Dc])
        else:
            P.op("dve", lambda e: e.tensor_tensor(out=H.Dc[:], in0=H.Dc[:], in1=H.Dsum[:], op=ALU.mult), reads=[H.r_Dsum, H.r_Dc], writes=[H.r_Dc])
        if last_seg:
            for d in range(2):
                P.dma("sp", lambda e, d=d: e.dma_start(out=outs["coreS"](d), in_=H.Sc[d][:]), H.s_core, reads=[H.r_Sc[d]])
            P.dma("sp", lambda e: e.dma_start(out=outs["coreD"](), in_=H.Dc[:]), H.s_core, reads=[H.r_Dc])


def hgrn1_final_sems(H):
    return [(s.h, s.count) for s in [H.s_sg, H.s_ol, H.s_q[0], H.s_q[1], H.s_S[0], H.s_S[1], H.s_D, H.s_core] if s.count]


def build_hgrn1_program(segs, NT, nseg):
    nc = bass.Bass("TRN2", target_bir_lowering=False)
    xin = nc.dram_tensor("xin", [128, 16, NT], F32, kind="ExternalInput").ap()
    vec = nc.dram_tensor("vec", [128, 8, 16], F32, kind="ExternalInput").ap()
    w_in = nc.dram_tensor("w_in", [D, 5 * D], F32, kind="ExternalInput").ap()
    maskd = nc.dram_tensor("masks", [2, 128, 512], F32, kind="ExternalInput").ap()
    NO = 2048
    ol = nc.dram_tensor("ol", [128, 16, NO], F32, kind="ExternalOutput").ap()
    sg = nc.dram_tensor("sg", [128, 16, NO], F32, kind="ExternalOutput").ap()
    qD = nc.dram_tensor("qD", [2, 128, 16, NO], BF16, kind="ExternalOutput").ap()
    segS = nc.dram_tensor("segS", [2, nseg, 128, 16, 128], F32, kind="ExternalOutput").ap()
    segD = nc.dram_tensor("segD", [nseg, 128, 2, 16], F32, kind="ExternalOutput").ap()
    coreS = nc.dram_tensor("coreS", [2, 128, 16, 128], F32, kind="ExternalOutput").ap()
    coreD = nc.dram_tensor("coreD", [128, 2, 16], F32, kind="ExternalOutput").ap()
    with ExitStack() as st:
        P = Prog(nc, st)
        C = setup_common(P)
        A = Arena(P, 190 * 1024)
        H = setup_hgrn1(C, A)
        V = P.sbuf("V", [128, 8, 16], F32); r_V = P.res()
        s_v = P.dsem("s_v"); s_x = P.dsem("s_x"); s_m = P.dsem("s_m")
        P.dma("sp", lambda e: e.dma_start(out=V[:], in_=vec), s_v, writes=[r_V])
        for k in (0, 2):
            P.op("dve", lambda e, k=k: e.tensor_scalar(out=V[:, k, :], in0=V[:, k, :], scalar1=1.0, scalar2=None, op0=ALU.add), reads=[r_V], writes=[r_V])
        P.op("dve", lambda e: e.tensor_tensor(out=V[:, 4, :], in0=V[:, 5, :], in1=V[:, 4, :], op=ALU.subtract), reads=[r_V], writes=[r_V])
        P.op("dve", lambda e: e.tensor_tensor(out=V[:, 5, :], in0=V[:, 7, :], in1=V[:, 6, :], op=ALU.subtract), reads=[r_V], writes=[r_V])
        P.op("act", lambda e: e.activation(out=V[:, 4:6, :], in_=V[:, 4:6, :], func=AF.Sigmoid), reads=[r_V], writes=[r_V])
        P.op("dve", lambda e: e.tensor_scalar(out=V[:, 6:8, :], in0=V[:, 4:6, :], scalar1=-1.0, scalar2=1.0, op0=ALU.mult, op1=ALU.add), reads=[r_V], writes=[r_V])
        for d in range(2):
            P.dma("sp", lambda e, d=d: e.dma_start(out=H.mask[d][:], in_=maskd[d]), s_m, writes=[H.r_mask])
        col = lambda k: (lambda c: V[:, k, c:c + 1])
        lbcol = lambda d, h: V[:, 4 + d, h:h + 1]
        omlbcol = lambda d, h: V[:, 6 + d, h:h + 1]
        lat = [s for s in segs if s[3]]
        for (t0, T, ms, need_out, si) in segs:
            emit_fill_xm_dram(C, H, xin, t0, T, col(0 + 2 * ms), col(1 + 2 * ms), r_V)
            outs = {
                "sg": lambda h, T, t0=t0: sg[:, h, t0:t0 + T],
                "ol": lambda h, T, t0=t0: ol[:, h, t0:t0 + T],
                "qD": lambda d, h, T, t0=t0: qD[d, :, h, t0:t0 + T],
                "segS": lambda d, seg: segS[d, seg],
                "segD": lambda seg: segD[seg],
                "coreS": lambda d: coreS[d],
                "coreD": lambda: coreD,
            }
            emit_hgrn_local(C, H, T, w_in, lbcol, omlbcol, r_V, outs, si, need_out,
                            first_seg=(need_out and si == lat[0][4]), last_seg=(need_out and si == lat[-1][4]))
        P.emit(final_waits={"sp": hgrn1_final_sems(H)})
    return nc


def setup_hgrn2(C, A):
    P = C.P
    G = Ctx()
    G.SinB = [[A.alloc([16, 128], BF16) for _ in range(4)] for _ in range(2)]
    G.r_SinB = [[P.res() for _ in range(4)] for _ in range(2)]
    G.Sw = A.alloc([16, 128], F32); G.r_Sw = P.res()
    G.ld = [A.alloc([16, 128], F32) for _ in range(2)]; G.r_ld = [P.res() for _ in range(2)]
    G.s_ld = [P.dsem("s_g2ld%d" % i) for i in range(2)]; G.ld_i = 0
    G.ldD = [A.alloc([16], F32) for _ in range(2)]; G.r_ldD = [P.res() for _ in range(2)]
    G.s_ldD = [P.dsem("s_g2ldD%d" % i) for i in range(2)]
    G.segD = A.alloc([5, 2, 16], F32); G.r_segD = P.res(); G.s_segD = P.dsem("s_g2sd")
    G.xs = A.alloc([16, 512], F32); G.r_xs = [P.res() for _ in range(16)]
    G.onb = A.alloc([16, 512], BF16); G.r_onb = [P.res() for _ in range(16)]
    G.NB = 2
    G.olh = [A.alloc([512], F32) for _ in range(G.NB)]; G.r_olh = [P.res() for _ in range(G.NB)]; G.s_olh = [P.dsem("s_g2ol%d" % i) for i in range(G.NB)]
    G.sgh = [A.alloc([512], F32) for _ in range(G.NB)]; G.r_sgh = [P.res() for _ in range(G.NB)]; G.s_sgh = [P.dsem("s_g2sg%d" % i) for i in range(G.NB)]
    G.qDh = [[A.alloc([512], BF16) for _ in range(G.NB)] for _ in range(2)]
    G.r_qDh = [[P.res() for _ in range(G.NB)] for _ in range(2)]
    G.s_qDh = [[P.dsem("s_g2q%d%d" % (d, i)) for i in range(G.NB)] for d in range(2)]
    G.hb = 0
    G.t = [A.alloc([512], F32) for _ in range(2)]; G.r_t = [P.res() for _ in range(2)]
    G.ws = [A.alloc([16, 512], BF16) for _ in range(2)]; G.r_ws = [P.res() for _ in range(2)]; G.s_ws = [P.dsem("s_g2w%d" % i) for i in range(2)]
    G.wi = 0
    G.s_x = P.dsem("s_g2x"); G.s_out = P.dsem("s_g2out")
    return G


def emit_hgrn_combine(C, G, segS, segD, chainS, chainD, nlat=4, ictx=4):
    P = C.P
    P.dma("sp", lambda e: e.dma_start(out=G.segD[:], in_=segD.rearrange("s p d h -> p s d h")), G.s_segD, writes=[G.r_segD])

    def load(ap):
        i = G.ld_i % 2; G.ld_i += 1
        P.dma("sp", lambda e, i=i, ap=ap: e.dma_start(out=G.ld[i][:], in_=ap), G.s_ld[i], writes=[G.r_ld[i]])
        return G.ld[i], G.r_ld[i]

    for d in range(2):
        t, r_t = load(segS(d, ictx))
        P.op("dve", lambda e, t=t: e.tensor_copy(out=G.Sw[:], in_=t[:]), reads=[r_t], writes=[G.r_Sw])
        for i in range(7):
            t, r_t = load(chainS(d, i))
            P.dma("sp", lambda e, d=d, i=i: e.dma_start(out=G.ldD[i % 2][:], in_=chainD(d, i)), G.s_ldD[i % 2], writes=[G.r_ldD[i % 2]])
            for h in range(16):
                P.op("dve", lambda e, h=h, t=t, i=i: e.scalar_tensor_tensor(out=G.Sw[:, h, :], in0=G.Sw[:, h, :], scalar=G.ldD[i % 2][:, h:h + 1], in1=t[:, h, :],
                                                                            op0=ALU.mult, op1=ALU.add), reads=[r_t, G.r_ldD[i % 2], G.r_Sw], writes=[G.r_Sw])
        order = list(range(nlat)) if d == 0 else list(range(nlat - 1, -1, -1))
        for n, j in enumerate(order):
            P.op("act", lambda e, d=d, j=j: e.activation(out=G.SinB[d][j][:], in_=G.Sw[:], func=AF.Copy), reads=[G.r_Sw], writes=[G.r_SinB[d][j]])
            if n < nlat - 1:
                t, r_t = load(segS(d, j))
                for h in range(16):
                    P.op("dve", lambda e, h=h, t=t, j=j, d=d: e.scalar_tensor_tensor(out=G.Sw[:, h, :], in0=G.Sw[:, h, :], scalar=G.segD[:, j, d, h:h + 1], in1=t[:, h, :],
                                                                                    op0=ALU.mult, op1=ALU.add), reads=[r_t, G.r_segD, G.r_Sw], writes=[G.r_Sw])


def emit_hgrn_final(C, G, j, T, t0, xin, ol, sg, qD, w_out, xout, gpcol, ngcol, lngcol, lnbcol, r_vec):
    P = C.P
    xs, r_xs = G.xs, G.r_xs
    P.dma("sp", lambda e: e.dma_start(out=xs[:, :, :T], in_=xin[:, :, t0:t0 + T]), G.s_x, writes=r_xs)
    for h in range(16):
        b = G.hb % G.NB; G.hb += 1
        P.dma("sp", lambda e, h=h, b=b: e.dma_start(out=G.olh[b][:, :T], in_=ol[:, h, t0:t0 + T]), G.s_olh[b], writes=[G.r_olh[b]])
        P.dma("sp", lambda e, h=h, b=b: e.dma_start(out=G.sgh[b][:, :T], in_=sg[:, h, t0:t0 + T]), G.s_sgh[b], writes=[G.r_sgh[b]])
        for d in range(2):
            P.dma("sp", lambda e, h=h, b=b, d=d: e.dma_start(out=G.qDh[d][b][:, :T], in_=qD[d, :, h, t0:t0 + T]), G.s_qDh[d][b], writes=[G.r_qDh[d][b]])
        po, r_po = C.pb[2 + (h % 2)], C.r_pb[2 + (h % 2)]
        for d in range(2):
            P.op("pe", lambda e, d=d, h=h, b=b, po=po: e.matmul(po[:, :T], lhsT=G.SinB[d][j][:, h, :], rhs=G.qDh[d][b][:, :T], start=(d == 0), stop=(d == 1)),
                 reads=[G.r_SinB[d][j], G.r_qDh[d][b]], writes=[r_po])
        o = G.olh[b]; r_o = G.r_olh[b]
        P.op("dve", lambda e, o=o, po=po: e.tensor_tensor(out=o[:, :T], in0=o[:, :T], in1=po[:, :T], op=ALU.add), reads=[r_po, r_o], writes=[r_o])
        sq, r_sq = C.sq[h % 2], C.r_sq[h % 2]
        P.op("act", lambda e, o=o, sq=sq: e.activation(out=sq[:, :T], in_=o[:, :T], func=AF.Square), reads=[r_o], writes=[r_sq])
        pr, r_pr = C.pb[h % 2], C.r_pb[h % 2]
        P.op("pe", lambda e, sq=sq, pr=pr: e.matmul(pr[:, :T], lhsT=C.ones[:], rhs=sq[:, :T], start=True, stop=True), reads=[C.r_ones, r_sq], writes=[r_pr])
        t, r_t = G.t[h % 2], G.r_t[h % 2]
        P.op("dve", lambda e, t=t, pr=pr: e.tensor_scalar(out=t[:, :T], in0=pr[:, :T], scalar1=1.0 / HK, scalar2=RMS_EPS, op0=ALU.mult, op1=ALU.add),
             reads=[r_pr], writes=[r_t])
        P.op("act", lambda e, t=t: e.activation(out=t[:, :T], in_=t[:, :T], func=AF.Sqrt), reads=[r_t], writes=[r_t])
        P.op("dve", lambda e, t=t: e.reciprocal(out=t[:, :T], in_=t[:, :T]), reads=[r_t], writes=[r_t])
        P.op("dve", lambda e, t=t, o=o: e.tensor_tensor(out=o[:, :T], in0=o[:, :T], in1=t[:, :T], op=ALU.mult), reads=[r_t, r_o], writes=[r_o])
        P.op("dve", lambda e, o=o, h=h, b=b: e.scalar_tensor_tensor(out=G.onb[:, h, :T], in0=o[:, :T], scalar=ngcol(), in1=G.sgh[b][:, :T], op0=ALU.mult, op1=ALU.mult),
             reads=[r_o, G.r_sgh[b], r_vec], writes=[G.r_onb[h]])
    w_v = w_out.rearrange("(h p) d -> p h d", p=128)
    for db in range(4):
        si = G.wi % 2; G.wi += 1
        ws, r_ws, s_ws = G.ws[si], G.r_ws[si], G.s_ws[si]
        P.dma("pool", lambda e, ws=ws, db=db: e.dma_start(out=ws[:], in_=w_v[:, :, db * 512:(db + 1) * 512]), s_ws, writes=[r_ws])
        for dci in range(4):
            py, r_py = C.pb[4 + dci], C.r_pb[4 + dci]
            for h in range(16):
                P.op("pe", lambda e, ws=ws, h=h, dci=dci, py=py: e.matmul(py[:, :T], lhsT=ws[:, h, dci * 128:(dci + 1) * 128], rhs=G.onb[:, h, :T],
                                                                         start=(h == 0), stop=(h == 15)), reads=[r_ws, G.r_onb[h]], writes=[r_py])
        for dci in range(4):
            c = db * 4 + dci
            py, r_py = C.pb[4 + dci], C.r_pb[4 + dci]
            P.op("dve", lambda e, c=c, py=py: e.scalar_tensor_tensor(out=xs[:, c, :T], in0=py[:, :T], scalar=gpcol(c), in1=xs[:, c, :T], op0=ALU.mult, op1=ALU.add),
                 reads=[r_py, r_xs[c], r_vec], writes=[r_xs[c]])
    emit_ln(C, xs, r_xs, T, lngcol, lnbcol, r_vec)
    P.dma("sp", lambda e: e.dma_start(out=xout[:, :, t0:t0 + T], in_=xs[:, :, :T]), G.s_out, reads=r_xs)


def build_hgrn2_program():
    nc = bass.Bass("TRN2", target_bir_lowering=False)
    NO = 2048
    xin = nc.dram_tensor("xin", [128, 16, NO], F32, kind="ExternalInput").ap()
    vec = nc.dram_tensor("vec", [128, 5, 16], F32, kind="ExternalInput").ap()
    ol = nc.dram_tensor("ol", [128, 16, NO], F32, kind="ExternalInput").ap()
    sg = nc.dram_tensor("sg", [128, 16, NO], F32, kind="ExternalInput").ap()
    qD = nc.dram_tensor("qD", [2, 128, 16, NO], BF16, kind="ExternalInput").ap()
    segS = nc.dram_tensor("segS", [2, 5, 128, 16, 128], F32, kind="ExternalInput").ap()
    segD = nc.dram_tensor("segD", [5, 128, 2, 16], F32, kind="ExternalInput").ap()
    chainS = nc.dram_tensor("chainS", [2, 7, 128, 16, 128], F32, kind="ExternalInput").ap()
    chainD = nc.dram_tensor("chainD", [2, 7, 128, 16], F32, kind="ExternalInput").ap()
    w_out = nc.dram_tensor("w_out", [D, D], F32, kind="ExternalInput").ap()
    xout = nc.dram_tensor("xout", [128, 16, NO], F32, kind="ExternalOutput").ap()
    with ExitStack() as st:
        P = Prog(nc, st)
        C = setup_common(P)
        A = Arena(P, 185 * 1024)
        G = setup_hgrn2(C, A)
        V = P.sbuf("V", [128, 5, 16], F32); r_V = P.res()
        s_v = P.dsem("s_v")
        P.dma("sp", lambda e: e.dma_start(out=V[:], in_=vec), s_v, writes=[r_V])
        P.op("dve", lambda e: e.tensor_scalar(out=V[:, 4, :], in0=V[:, 0, :], scalar1=1.0 / ALPHA, scalar2=None, op0=ALU.mult), reads=[r_V], writes=[r_V])
        emit_hgrn_combine(C, G, lambda d, i: segS[d, i], segD, lambda d, i: chainS[d, i], lambda d, i: chainD[d, i])
        col = lambda k: (lambda c: V[:, k, c:c + 1])
        for j in range(4):
            emit_hgrn_final(C, G, j, 512, j * 512, xin, ol, sg, qD, w_out, xout, col(4), lambda: V[:, 3, 0:1], col(1), col(2), r_V)
        P.emit(final_waits={"sp": [(G.s_out.h, G.s_out.count)]})
    return nc


NQ = 18


def build_mod_program():
    nc = bass.Bass("TRN2", target_bir_lowering=False)
    cc = nc.dram_tensor("cc", [2, 16, 128], F32, kind="ExternalInput").ap()
    w = nc.dram_tensor("w", [2, D, NQ * 128], F32, kind="ExternalInput").ap()
    b = nc.dram_tensor("b", [2 * NQ, 128], F32, kind="ExternalInput").ap()
    mv = nc.dram_tensor("mv", [128, 2, NQ, 2], F32, kind="ExternalOutput").ap()
    with ExitStack() as st:
        P = Prog(nc, st)
        C = setup_common(P)
        cs = P.sbuf("cs", [16, 2, 128], F32); r_cs = P.res()
        bs = P.sbuf("bs", [2 * NQ, 128], F32); r_bs = P.res()
        scT = P.sbuf("scT", [128, 16, 2], BF16); r_scT = P.res()
        mbT = P.sbuf("mbT", [128, 2 * NQ], F32); r_mbT = P.res()
        mrow = [P.sbuf("mrow%d" % i, [2, 384], F32) for i in range(2)]; r_mrow = [P.res() for _ in range(2)]
        MVs = P.sbuf("MVs", [128, 2, NQ, 2], F32); r_MVs = P.res()
        ws = [P.sbuf("mws%d" % i, [128, 16, 384], BF16) for i in range(2)]; r_ws = [P.res() for _ in range(2)]
        s_ws = [P.dsem("s_mws%d" % i) for i in range(2)]
        s_c = P.dsem("s_c"); s_b = P.dsem("s_b"); s_o = P.dsem("s_o")
        for r in range(2):
            P.dma("sp", lambda e, r=r: e.dma_start(out=cs[:, r, :], in_=cc[r]), s_c, writes=[r_cs])
        P.dma("sp", lambda e: e.dma_start(out=bs[:], in_=b), s_b, writes=[r_bs])
        P.op("act", lambda e: e.activation(out=cs[:], in_=cs[:], func=AF.Silu), reads=[r_cs], writes=[r_cs])
        pt, r_pt = C.pb[3], C.r_pb[3]
        for r in range(2):
            P.op("pe", lambda e, r=r: e.transpose(out=pt[:, r * 16:(r + 1) * 16], in_=cs[:, r, :], identity=C.ident[0:16, 0:16]),
                 reads=[r_cs, C.r_ident], writes=[r_pt])
        for r in range(2):
            P.op("dve", lambda e, r=r: e.tensor_copy(out=scT[:, :, r], in_=pt[:, r * 16:(r + 1) * 16]), reads=[r_pt], writes=[r_scT])
        pb_, r_pb_ = C.pb[4], C.r_pb[4]
        P.op("pe", lambda e: e.transpose(out=pb_[:, 0:2 * NQ], in_=bs[:], identity=C.ident[0:2 * NQ, 0:2 * NQ]), reads=[r_bs, C.r_ident], writes=[r_pb_])
        P.op("dve", lambda e: e.tensor_copy(out=mbT[:], in_=pb_[:, 0:2 * NQ]), reads=[r_pb_], writes=[r_mbT])
        pT, r_pT = C.pb[2], C.r_pb[2]
        k = 0
        for i in range(2):
            for nb in range(6):
                si = k % 2
                P.dma("pool", lambda e, i=i, nb=nb, si=si: e.dma_start(out=ws[si][:], in_=w[i].rearrange("(c p) f -> p c f", p=128)[:, :, nb * 384:(nb + 1) * 384]),
                      s_ws[si], writes=[r_ws[si]])
                pm, r_pm = C.pb[k % 2], C.r_pb[k % 2]
                for c in range(16):
                    P.op("pe", lambda e, c=c, si=si, pm=pm: e.matmul(pm[0:2, 0:384], lhsT=scT[:, c, :], rhs=ws[si][:, c, :], start=(c == 0), stop=(c == 15)),
                         reads=[r_scT, r_ws[si]], writes=[r_pm])
                P.op("act", lambda e, si=si, pm=pm: e.activation(out=mrow[si][:], in_=pm[0:2, 0:384], func=AF.Copy), reads=[r_pm], writes=[r_mrow[si]])
                for t in range(3):
                    q = i * NQ + nb * 3 + t
                    P.op("pe", lambda e, si=si, t=t, q=q: e.transpose(out=pT[:, q * 2:(q + 1) * 2], in_=mrow[si][:, t * 128:(t + 1) * 128], identity=C.ident[0:2, 0:2]),
                         reads=[r_mrow[si], C.r_ident], writes=[r_pT])
                k += 1
        for r in range(2):
            P.op("dve", lambda e, r=r: e.tensor_tensor(out=MVs[:].rearrange("p i q r -> p (i q) r")[:, :, r], in0=pT[:, 0:4 * NQ].rearrange("p (q r) -> p q r", r=2)[:, :, r],
                                                       in1=mbT[:], op=ALU.add), reads=[r_pT, r_mbT], writes=[r_MVs])
        P.dma("sp", lambda e: e.dma_start(out=mv, in_=MVs[:]), s_o, reads=[r_MVs])
        P.emit(final_waits={"sp": [(s_o.h, s_o.count)]})
    return nc


from concourse.bass_utils import run_bass_kernel_spmd

NCORES = 8
_PROGS = {}


def _prog(name, fn):
    if name not in _PROGS:
        _PROGS[name] = fn()
    return _PROGS[name]


def to_fm(a):
    return np.ascontiguousarray(np.asarray(a, np.float32).T.reshape(16, 128, -1).transpose(1, 0, 2))


def from_fm(a):
    return np.asarray(a).transpose(1, 0, 2).reshape(D, -1).T


def vrow(v):
    return np.asarray(v, np.float32).reshape(16, 128).T


def _run(nc, in_maps):
    res = run_bass_kernel_spmd(nc, in_maps, core_ids=list(range(NCORES)))
    return res.results


LAT4 = [(i * 512, 512, 0) for i in range(4)]
TILES_A = [(i * 512, 512, 0) for i in range(5)] + [(2560, 448, 0), (3008, 256, 1)]
TILES_B = LAT4 + [(2048, 256, 1)]


def ffn_vec(MV, ln_g, ln_b, i, s, pre=None):
    vec = np.zeros((128, NVROW, 16), np.float32)
    mrow = lambda j, r: MV[:, i, (s * 3 + j) * 16:(s * 3 + j + 1) * 16, r]
    vec[:, 0] = mrow(1, 0); vec[:, 1] = mrow(0, 0); vec[:, 2] = mrow(2, 0)
    vec[:, 3] = mrow(1, 1); vec[:, 4] = mrow(0, 1); vec[:, 5] = mrow(2, 1)
    vec[:, 6] = vrow(ln_g[i, s]); vec[:, 7] = vrow(ln_b[i, s])
    if pre is not None:
        vec[:, 8] = vrow(ln_g[pre]); vec[:, 9] = vrow(ln_b[pre])
    return vec


def kernel(x, c, ctx, c_ctx, mod_w, mod_b, ln_g, ln_b, ffn_w_in, ffn_w_out, pool_w, pool_scale,
           hgrn_w_in, hgrn_lb, hgrn_norm_g, hgrn_w_out):
    f32 = lambda a: np.ascontiguousarray(np.asarray(a, np.float32))
    x = f32(x)[0]; ctx = f32(ctx)[0]; c = f32(c); c_ctx = f32(c_ctx)
    mod_w = np.asarray(mod_w, np.float32); mod_b = f32(mod_b); ln_g = f32(ln_g); ln_b = f32(ln_b)
    ffn_w_in = np.asarray(ffn_w_in, np.float32); ffn_w_out = np.asarray(ffn_w_out, np.float32)
    pool_w = f32(pool_w)[0]; pool_scale = f32(pool_scale)[0]
    hgrn_w_in = f32(hgrn_w_in)[0]; hgrn_lb = f32(hgrn_lb); hgrn_norm_g = f32(hgrn_norm_g)[0]; hgrn_w_out = f32(hgrn_w_out)[0]
    cores = list(range(NCORES))
    cc = np.stack([c.reshape(16, 128), c_ctx.reshape(16, 128)])
    ins = [{"cc": cc, "w": np.ascontiguousarray(mod_w[:, :, k * 2304:(k + 1) * 2304]),
            "b": np.ascontiguousarray(mod_b[:, k * 2304:(k + 1) * 2304].reshape(36, 128))} for k in cores]
    r = _run(_prog("mod", build_mod_program), ins)
    MV = np.concatenate([rk["mv"] for rk in r], axis=2)
    ins = []
    for k in cores:
        slab = np.zeros((3008 + 256, D), np.float32)
        lo, hi = 2048 * k - 512, 2048 * k + 2048 + 448
        a, b = max(lo, 0), min(hi, 16384)
        slab[a - lo:b - lo] = x[a:b]
        slab[3008:] = ctx
        ins.append({"xin": to_fm(slab), "vec": ffn_vec(MV, ln_g, ln_b, 0, 0), "w_in": f32(ffn_w_in[0, 0]), "w_out": f32(ffn_w_out[0, 0])})
    r = _run(_prog("ffnA", lambda: build_ffn_program(TILES_A, 3264, pre_ln=False)), ins)
    x1 = [rk["xout"] for rk in r]
    vecp = np.zeros((128, 7, 16), np.float32)
    mrow = lambda i, s, j, rr: MV[:, i, (s * 3 + j) * 16:(s * 3 + j + 1) * 16, rr]
    vecp[:, 0] = mrow(0, 1, 1, 0); vecp[:, 1] = mrow(0, 1, 0, 0); vecp[:, 2] = mrow(0, 1, 2, 0)
    vecp[:, 3] = mrow(0, 1, 1, 1); vecp[:, 4] = mrow(0, 1, 0, 1); vecp[:, 5] = mrow(0, 1, 2, 1)
    vecp[:, 6] = vrow(pool_scale)
    ins = []
    for k in cores:
        valid, inv, invc = pool_consts(k)
        ins.append({"xin": np.ascontiguousarray(x1[k][:, :, :3008]), "xctx": np.ascontiguousarray(x1[k][:, :, 3008:]), "vec": vecp,
                    "valid": valid, "inv": inv, "invc": invc, "pool_w": pool_w})
    r = _run(_prog("pool", build_pool_program), ins)
    zz = [np.ascontiguousarray(np.concatenate([rk["z"], rk["zc"]], axis=2)) for rk in r]
    vec = ffn_vec(MV, ln_g, ln_b, 0, 2, pre=(0, 1))
    ins = [{"xin": zz[k], "vec": vec, "w_in": f32(ffn_w_in[0, 1]), "w_out": f32(ffn_w_out[0, 1])} for k in cores]
    r = _run(_prog("ffnB", lambda: build_ffn_program(TILES_B, 2304, pre_ln=True)), ins)
    x2 = [rk["xout"] for rk in r]
    vec = ffn_vec(MV, ln_g, ln_b, 1, 0)
    ins = [{"xin": x2[k], "vec": vec, "w_in": f32(ffn_w_in[1, 0]), "w_out": f32(ffn_w_out[1, 0])} for k in cores]
    r = _run(_prog("ffnC", lambda: build_ffn_program(TILES_B, 2304, pre_ln=False)), ins)
    x3 = [rk["xout"] for rk in r]
    vech = np.zeros((128, 8, 16), np.float32)
    vech[:, 0] = mrow(1, 1, 1, 0); vech[:, 1] = mrow(1, 1, 0, 0); vech[:, 2] = mrow(1, 1, 1, 1); vech[:, 3] = mrow(1, 1, 0, 1)
    vech[:, 4] = vrow(hgrn_lb[0, 0]); vech[:, 5] = vrow(hgrn_lb[0, 1]); vech[:, 6] = vrow(hgrn_lb[1, 0]); vech[:, 7] = vrow(hgrn_lb[1, 1])
    mf, mb = hgrn_masks()
    masks = np.stack([mf, mb])
    segs = [(i * 512, 512, 0, True, i) for i in range(4)] + [(2048, 256, 1, False, 4)]
    ins = [{"xin": x3[k], "vec": vech, "w_in": hgrn_w_in, "masks": masks} for k in cores]
    r1 = _run(_prog("hgrn1", lambda: build_hgrn1_program(segs, 2304, 5)), ins)
    vec2 = np.zeros((128, 5, 16), np.float32)
    vec2[:, 0] = mrow(1, 1, 2, 0); vec2[:, 1] = vrow(ln_g[1, 1]); vec2[:, 2] = vrow(ln_b[1, 1]); vec2[:, 3, 0] = hgrn_norm_g
    ins = []
    for k in cores:
        chS = np.zeros((2, 7, 128, 16, 128), np.float32); chD = np.ones((2, 7, 128, 16), np.float32)
        for i, cc_ in enumerate(range(0, k)):
            chS[0, i] = r1[cc_]["coreS"][0]; chD[0, i] = r1[cc_]["coreD"][:, 0, :]
        for i, cc_ in enumerate(range(7, k, -1)):
            chS[1, i] = r1[cc_]["coreS"][1]; chD[1, i] = r1[cc_]["coreD"][:, 1, :]
        ins.append({"xin": np.ascontiguousarray(x3[k][:, :, :2048]), "vec": vec2, "ol": r1[k]["ol"], "sg": r1[k]["sg"], "qD": r1[k]["qD"],
                    "segS": r1[k]["segS"], "segD": r1[k]["segD"], "chainS": chS, "chainD": chD, "w_out": hgrn_w_out})
    r = _run(_prog("hgrn2", build_hgrn2_program), ins)
    x4 = [rk["xout"] for rk in r]
    vec = ffn_vec(MV, ln_g, ln_b, 1, 2)
    ins = [{"xin": x4[k], "vec": vec, "w_in": f32(ffn_w_in[1, 1]), "w_out": f32(ffn_w_out[1, 1])} for k in cores]
    r = _run(_prog("ffnD", lambda: build_ffn_program(LAT4, 2048, pre_ln=False)), ins)
    out = np.concatenate([from_fm(rk["xout"]) for rk in r], axis=0)
    return np.ascontiguousarray(out[None].astype(np.float32))
```
